# Optimizing a Trainium2 kernel written in Bass

```python
import math
import jax, jax.numpy as jnp
from jax import lax
import numpy as np

D_MODEL = 2048
BATCH = 8
SEQ = 4096
DEPTH = 2

GDN_HEADS = 16
GDN_HEAD_DIM = 128
GDN_WIDTH = GDN_HEADS * GDN_HEAD_DIM
GDN_CHUNK = 64
CONV_K = 4
CONV_CH = 3 * GDN_WIDTH
SWA_Q_HEADS = 32
SWA_KV_HEADS = 4
SWA_HEAD_DIM = 64
SWA_GROUP = SWA_Q_HEADS // SWA_KV_HEADS
SWA_WIDTH = SWA_Q_HEADS * SWA_HEAD_DIM
SWA_KV_WIDTH = SWA_KV_HEADS * SWA_HEAD_DIM
WINDOW = 128
SWA_BLOCK = 128
ROPE_THETA = 10000.0
NORM_EPS = 1e-6
L2_EPS = 1e-6
SPLIT_SIZES = (CONV_CH, GDN_HEADS, GDN_HEADS, GDN_WIDTH,
               SWA_WIDTH, SWA_KV_WIDTH, SWA_KV_WIDTH, SWA_WIDTH,
               D_MODEL, D_MODEL)
IN_COLS = CONV_CH + 2 * GDN_HEADS + GDN_WIDTH + 2 * SWA_WIDTH + 2 * SWA_KV_WIDTH + 2 * D_MODEL

kernel_name = "hybrid_gdn_swa_sink_gated_merge_adaln"


def rmsnorm(x, w):
    xf = x.astype(jnp.float32)
    y = xf * lax.rsqrt(jnp.mean(xf * xf, axis=-1, keepdims=True) + NORM_EPS)
    return (y * w.astype(jnp.float32)).astype(x.dtype)


def l2norm(x):
    return x * lax.rsqrt(jnp.sum(x * x, axis=-1, keepdims=True) + L2_EPS)


def split_cols(u, sizes):
    outs, start = [], 0
    for s in sizes:
        outs.append(u[..., start:start + s])
        start += s
    return outs


def causal_conv(x, w):
    ch = x.shape[-1]
    return lax.conv_general_dilated(
        x, w.astype(x.dtype)[:, None, :], window_strides=(1,), padding=[(CONV_K - 1, 0)],
        dimension_numbers=("NWC", "WIO", "NWC"), feature_group_count=ch)


def rope(x, positions):
    half = x.shape[-1] // 2
    inv_freq = ROPE_THETA ** (-jnp.arange(half, dtype=jnp.float32) / half)
    ang = positions.astype(jnp.float32)[..., None] * inv_freq
    cos = jnp.cos(ang)[:, :, None, :]
    sin = jnp.sin(ang)[:, :, None, :]
    xf = x.astype(jnp.float32)
    x1, x2 = xf[..., :half], xf[..., half:]
    return jnp.concatenate([x1 * cos - x2 * sin, x2 * cos + x1 * sin], axis=-1).astype(x.dtype)


def gated_delta_rule(q, k, v, g, beta):
    b, s, h, dk = q.shape
    dv = v.shape[-1]
    n = s // GDN_CHUNK

    def to_chunks(t):
        t = t.reshape((b, n, GDN_CHUNK) + t.shape[2:])
        return jnp.moveaxis(t, 3, 1)

    q, k, v, beta = to_chunks(q), to_chunks(k), to_chunks(v), to_chunks(beta)
    g_cum = jnp.cumsum(to_chunks(g), axis=-1)
    idx = jnp.arange(GDN_CHUNK)
    causal = idx[:, None] >= idx[None, :]
    strict = idx[:, None] > idx[None, :]
    decay = jnp.exp(jnp.where(causal, g_cum[..., :, None] - g_cum[..., None, :], -jnp.inf))
    k_beta = k * beta[..., None]
    l_mat = jnp.where(strict, jnp.einsum("bhncd,bhnjd->bhncj", k_beta, k) * decay, 0.0)
    a_mat = l_mat + jnp.eye(GDN_CHUNK, dtype=l_mat.dtype)
    rhs = jnp.concatenate([v * beta[..., None], k_beta * jnp.exp(g_cum)[..., None]], axis=-1)
    sol = lax.linalg.triangular_solve(a_mat, rhs, left_side=True, lower=True, unit_diagonal=True)
    u, w = sol[..., :dv], sol[..., dv:]
    qk = jnp.einsum("bhncd,bhnjd->bhncj", q, k) * decay
    q_dec = q * jnp.exp(g_cum)[..., None]
    k_dec = k * jnp.exp(g_cum[..., -1:] - g_cum)[..., None]
    chunk_decay = jnp.exp(g_cum[..., -1])
    xs = tuple(jnp.moveaxis(t, 2, 0) for t in (q_dec, k_dec, u, w, qk, chunk_decay))

    def step(state, inp):
        q_c, k_c, u_c, w_c, qk_c, dec_c = inp
        v_new = u_c - jnp.einsum("bhcd,bhde->bhce", w_c, state)
        o_c = jnp.einsum("bhcd,bhde->bhce", q_c, state) + jnp.einsum("bhcj,bhje->bhce", qk_c, v_new)
        state = state * dec_c[..., None, None] + jnp.einsum("bhcd,bhce->bhde", k_c, v_new)
        return state, o_c

    state0 = jnp.zeros((b, h, dk, dv), jnp.float32)
    _, o = lax.scan(step, state0, xs)
    return jnp.transpose(o, (1, 0, 3, 2, 4)).reshape(b, s, h, dv)


def swa_with_sinks(q, k, v, sinks):
    b, s, hq, d = q.shape
    nb, t = s // SWA_BLOCK, SWA_BLOCK
    qb = q.reshape(b, nb, t, SWA_KV_HEADS, SWA_GROUP, d)
    kb = k.reshape(b, nb, t, SWA_KV_HEADS, d)
    vb = v.reshape(b, nb, t, SWA_KV_HEADS, d)
    pad = ((0, 0), (1, 0), (0, 0), (0, 0), (0, 0))
    keys = jnp.concatenate([jnp.pad(kb, pad)[:, :-1], kb], axis=2)
    vals = jnp.concatenate([jnp.pad(vb, pad)[:, :-1], vb], axis=2)
    scores = jnp.einsum("bnqhgd,bnkhd->bnhgqk", qb, keys,
                        preferred_element_type=jnp.float32) * (d ** -0.5)
    blk = jnp.arange(nb)[:, None]
    qpos = blk * t + jnp.arange(t)[None, :]
    kpos = blk * t - t + jnp.arange(2 * t)[None, :]
    diff = qpos[:, :, None] - kpos[:, None, :]
    valid = (diff >= 0) & (diff < WINDOW) & (kpos[:, None, :] >= 0)
    scores = jnp.where(valid[None, :, None, None], scores, -jnp.inf)
    sink = sinks.astype(jnp.float32).reshape(SWA_KV_HEADS, SWA_GROUP)[None, None, :, :, None, None]
    m = jnp.maximum(jnp.max(scores, axis=-1, keepdims=True), sink)
    p = jnp.exp(scores - m)
    probs = p / (jnp.sum(p, axis=-1, keepdims=True) + jnp.exp(sink - m))
    o = jnp.einsum("bnhgqk,bnkhd->bnqhgd", probs.astype(v.dtype), vals,
                   preferred_element_type=jnp.float32)
    return o.reshape(b, s, hq, d).astype(v.dtype)


def hybrid_layer(x, c_act, positions, ada_w, ada_b, norm_w, w_in, conv_w, a_log, dt_bias,
                 gdn_norm_w, sinks, proj_a, proj_b, w_out):
    b, s, _ = x.shape
    mod = c_act @ ada_w + ada_b
    shift, scale, gate = mod[:, :D_MODEL], mod[:, D_MODEL:2 * D_MODEL], mod[:, 2 * D_MODEL:]
    h = rmsnorm(x, norm_w) * (1.0 + scale[:, None, :]) + shift[:, None, :]
    u = h @ w_in
    qkv_a, beta_a, a_a, z_a, q_b, k_b, v_b, z_b, g_a, g_b = split_cols(u, SPLIT_SIZES)

    qkv_a = jax.nn.silu(causal_conv(qkv_a, conv_w)).astype(jnp.float32)
    q_a = qkv_a[..., :GDN_WIDTH].reshape(b, s, GDN_HEADS, GDN_HEAD_DIM)
    k_a = qkv_a[..., GDN_WIDTH:2 * GDN_WIDTH].reshape(b, s, GDN_HEADS, GDN_HEAD_DIM)
    v_a = qkv_a[..., 2 * GDN_WIDTH:].reshape(b, s, GDN_HEADS, GDN_HEAD_DIM)
    q_a = l2norm(q_a) * (GDN_HEAD_DIM ** -0.5)
    k_a = l2norm(k_a)
    beta = jax.nn.sigmoid(beta_a.astype(jnp.float32))
    g = -jnp.exp(a_log.astype(jnp.float32)) * jax.nn.softplus(
        a_a.astype(jnp.float32) + dt_bias.astype(jnp.float32))
    o_a = gated_delta_rule(q_a, k_a, v_a, g, beta)
    o_a = rmsnorm(o_a, gdn_norm_w).reshape(b, s, GDN_WIDTH).astype(x.dtype) * jax.nn.silu(z_a)

    q_b = rope(q_b.reshape(b, s, SWA_Q_HEADS, SWA_HEAD_DIM), positions)
    k_b = rope(k_b.reshape(b, s, SWA_KV_HEADS, SWA_HEAD_DIM), positions)
    v_b = v_b.reshape(b, s, SWA_KV_HEADS, SWA_HEAD_DIM)
    o_b = swa_with_sinks(q_b, k_b, v_b, sinks).reshape(b, s, SWA_WIDTH) * jax.nn.silu(z_b)

    y = jax.nn.sigmoid(g_a) * (o_a @ proj_a) + jax.nn.sigmoid(g_b) * (o_b @ proj_b)
    return x + gate[:, None, :] * (y @ w_out)


def setup_inputs(seed: int = 0) -> dict:
    key = jax.random.key(seed)
    ks = jax.random.split(key, 16)
    f32 = jnp.float32
    x = jax.random.normal(ks[0], (BATCH, SEQ, D_MODEL), f32)
    c = jax.random.normal(ks[1], (BATCH, D_MODEL), f32)
    offsets = jax.random.randint(ks[2], (BATCH, 1), 0, 1024, dtype=jnp.int32)
    positions = (offsets + jnp.arange(SEQ, dtype=jnp.int32)[None, :]).astype(jnp.int32)
    ada_w = jax.random.normal(ks[3], (DEPTH, D_MODEL, 3 * D_MODEL), f32) * D_MODEL ** -0.5
    ada_b = 0.02 * jax.random.normal(ks[4], (DEPTH, 3 * D_MODEL), f32)
    norm_w = 1.0 + 0.02 * jax.random.normal(ks[5], (DEPTH, D_MODEL), f32)
    w_in = jax.random.normal(ks[6], (DEPTH, D_MODEL, IN_COLS), f32) * D_MODEL ** -0.5
    conv_w = jax.random.normal(ks[7], (DEPTH, CONV_K, CONV_CH), f32) * CONV_K ** -0.5
    gdn_a_log = jnp.log(jax.random.uniform(ks[8], (DEPTH, GDN_HEADS), f32, 1.0, 16.0))
    dt = jnp.exp(jax.random.uniform(ks[9], (DEPTH, GDN_HEADS), f32, math.log(1e-3), math.log(1e-1)))
    gdn_dt_bias = dt + jnp.log(-jnp.expm1(-dt))
    gdn_norm_w = 1.0 + 0.02 * jax.random.normal(ks[10], (DEPTH, GDN_HEAD_DIM), f32)
    swa_sinks = 0.5 * jax.random.normal(ks[11], (DEPTH, SWA_Q_HEADS), f32)
    proj_a = jax.random.normal(ks[12], (DEPTH, GDN_WIDTH, D_MODEL), f32) * GDN_WIDTH ** -0.5
    proj_b = jax.random.normal(ks[13], (DEPTH, SWA_WIDTH, D_MODEL), f32) * SWA_WIDTH ** -0.5
    w_out = jax.random.normal(ks[14], (DEPTH, D_MODEL, D_MODEL), f32) * D_MODEL ** -0.5
    final_norm_w = 1.0 + 0.02 * jax.random.normal(ks[15], (D_MODEL,), f32)
    return {"x": x, "c": c, "positions": positions, "ada_w": ada_w, "ada_b": ada_b,
            "norm_w": norm_w, "w_in": w_in, "conv_w": conv_w, "gdn_a_log": gdn_a_log,
            "gdn_dt_bias": gdn_dt_bias, "gdn_norm_w": gdn_norm_w, "swa_sinks": swa_sinks,
            "proj_a": proj_a, "proj_b": proj_b, "w_out": w_out, "final_norm_w": final_norm_w}


def reference(x, c, positions, ada_w, ada_b, norm_w, w_in, conv_w, gdn_a_log, gdn_dt_bias,
              gdn_norm_w, swa_sinks, proj_a, proj_b, w_out, final_norm_w):
    c_act = jax.nn.silu(c)
    for l in range(DEPTH):
        x = hybrid_layer(x, c_act, positions, ada_w[l], ada_b[l], norm_w[l], w_in[l], conv_w[l],
                         gdn_a_log[l], gdn_dt_bias[l], gdn_norm_w[l], swa_sinks[l],
                         proj_a[l], proj_b[l], w_out[l])
    return rmsnorm(x, final_norm_w)
```

```python
import math
from contextlib import ExitStack

import numpy as np
import ml_dtypes

import concourse.bass as bass
import concourse.mybir as mybir
from concourse.bass_utils import run_bass_kernel_spmd

F32 = mybir.dt.float32
BF16 = mybir.dt.bfloat16
I32 = mybir.dt.int32
AF = mybir.ActivationFunctionType
ALU = mybir.AluOpType
AX = mybir.AxisListType

D = 2048
KC = 16
T = 512
NEG = -30000.0
IN_COLS = 16928
NSLAB = 92
SW = 256
RING = 3
NDS = 16
TWO_PI = 2.0 * math.pi


def _merge(dst, src):
    for k, (s, v) in src.items():
        if k not in dst or dst[k][1] < v:
            dst[k] = (s, v)


class Tn:
    def __init__(self, t, inherit=None, excl=False):
        self.t = t
        self.excl = excl
        self.w = dict(inherit) if inherit else {}
        self.r = {}

    def __getitem__(self, k):
        return self.t[k]


class Scope:
    def __init__(self, sched):
        self.s = sched
        self.es = ExitStack()
        self.tns = []

    def __enter__(self):
        self.es.__enter__()
        return self

    def sb(self, name, shape, dt):
        t = self.es.enter_context(self.s.nc.sbuf_tensor(self.s.uname(name), list(shape), dt))
        tn = Tn(t, self.s.residue)
        self.tns.append(tn)
        return tn

    def __exit__(self, *a):
        for tn in self.tns:
            _merge(self.s.residue, tn.w)
            _merge(self.s.residue, tn.r)
        return self.es.__exit__(*a)


class Sched:
    def __init__(self, nc, es):
        self.nc = nc
        self.eng = {"pe": nc.tensor, "act": nc.scalar, "dve": nc.vector, "pool": nc.gpsimd, "sp": nc.sync}
        self.sem = {k: es.enter_context(nc.semaphore("sem_" + k)) for k in ("pe", "act", "dve", "pool")}
        self.cnt = {k: 0 for k in self.sem}
        self.known = {k: {} for k in self.eng}
        self.dsem = [es.enter_context(nc.semaphore("dsem%d" % i)) for i in range(NDS)]
        self.dcnt = [0] * NDS
        self.dnext = 0
        self.residue = {}
        self._n = 0
        self.ninst = 0
        self.stopped = False
        CUR[0] = self

    def uname(self, n):
        self._n += 1
        if not self.stopped:
            NAMES[n] = "%s_%d" % (n, self._n)
        return "%s_%d" % (n, self._n)

    def scope(self):
        return Scope(self)

    def _need(self, reads, writes):
        need = {}
        for b in reads:
            _merge(need, b.w)
            if b.excl:
                _merge(need, b.r)
        for b in writes:
            _merge(need, b.w)
            _merge(need, b.r)
        return need

    def _wait(self, e, need):
        if self.stopped:
            return
        kn = self.known[e]
        for k, (s, v) in need.items():
            if kn.get(k, 0) < v:
                self.eng[e].wait_ge(s, v)
                kn[k] = v
                self.ninst += 1

    def _commit(self, tok, reads, writes):
        for b in writes:
            b.w = dict(tok)
            b.r = {}
        for b in reads:
            _merge(b.r, tok)

    def op(self, e, fn, reads=(), writes=()):
        if self.stopped:
            return
        need = self._need(reads, writes)
        if e == "pe":
            need.pop("pe", None)
        self._wait(e, need)
        inst = fn(self.eng[e])
        self.cnt[e] += 1
        inst.then_inc(self.sem[e], 1)
        self.ninst += 1
        self._commit({e: (self.sem[e], self.cnt[e])}, reads, writes)

    def par(self, items, reads=(), writes=()):
        if self.stopped:
            return
        need = self._need(reads, writes)
        tok = {}
        for (e, fn) in items:
            nd = dict(need)
            if e == "pe":
                nd.pop("pe", None)
            self._wait(e, nd)
            inst = fn(self.eng[e])
            self.cnt[e] += 1
            inst.then_inc(self.sem[e], 1)
            self.ninst += 1
            tok[e] = (self.sem[e], self.cnt[e])
        self._commit(tok, reads, writes)

    def dma(self, q, pairs, reads=(), writes=()):
        if self.stopped:
            return
        i = self.dnext
        self.dnext = (i + 1) % NDS
        need = self._need(reads, writes)
        if self.dcnt[i] > 0:
            need["d%d" % i] = (self.dsem[i], self.dcnt[i])
        self._wait(q, need)
        for (o, a) in pairs:
            self.eng[q].dma_start(out=o, in_=a).then_inc(self.dsem[i], 16)
            self.dcnt[i] += 16
            self.ninst += 1
        self._commit({"d%d" % i: (self.dsem[i], self.dcnt[i])}, reads, writes)

    def finish(self, tns):
        need = {}
        for b in tns:
            _merge(need, b.w)
            _merge(need, b.r)
        _merge(need, self.residue)
        self.stopped = False
        for e in ("sp", "act", "pool", "dve", "pe"):
            self._wait(e, dict(need))


def _consts():
    idx = np.arange(128)
    ch = idx // 64
    same = ch[:, None] == ch[None, :]
    c = {}
    c["ident_f"] = np.eye(128, dtype=np.float32)
    c["ones_f"] = np.ones((128, 128), np.float32)
    mS = np.where(same & (idx[:, None] > idx[None, :]), 0.0, NEG)
    mST = np.where(same & (idx[None, :] > idx[:, None]), 0.0, NEG)
    mIT = np.where(same & (idx[None, :] >= idx[:, None]), 0.0, NEG)
    c["tri"] = (same & (idx[:, None] <= idx[None, :])).astype(np.float32)
    lsel = np.zeros((128, 128), np.float32)
    for m in range(128):
        lsel[64 * (m // 64) + 63, m] = 1.0
    c["lsel"] = lsel
    l0 = np.zeros((128, 128), np.float32); l0[63, :] = 1.0
    l1 = np.zeros((128, 128), np.float32); l1[127, :] = 1.0
    c["l0"] = l0
    c["l1"] = l1
    c["invf"] = (10000.0 ** (-(np.arange(128) % 32).astype(np.float64) / 32.0)).astype(np.float32)[:, None]
    bf = {}
    bf["ident_b"] = np.eye(128)
    bf["ident4_b"] = np.tile(np.eye(128), (1, 4))
    bf["ones_b"] = np.ones((128, 128))
    bf["mS4"] = np.tile(mS, (1, 4))
    bf["mST4"] = np.tile(mST, (1, 4))
    bf["mIT4"] = np.tile(mIT, (1, 4))
    left = np.where(idx[None, :] > idx[:, None], 0.0, NEG)
    right = np.where(idx[None, :] <= idx[:, None], 0.0, NEG)
    bf["mB"] = np.concatenate([left, right], 1)
    bf["mB0"] = np.concatenate([np.full((128, 128), NEG), right], 1)
    for lv in range(4):
        a = 8 * 2 ** lv
        if lv == 0:
            bf["hm0"] = (idx[:, None] // 8 == idx[None, :] // 8).astype(np.float64)
        else:
            bf["hm%d" % lv] = ((idx[:, None] // a == idx[None, :] // a) & (idx[:, None] // (a // 2) != idx[None, :] // (a // 2))).astype(np.float64)
    rot = np.zeros((128, 128))
    for m in range(128):
        if m % 64 < 32:
            rot[m + 32, m] = -1.0
        else:
            rot[m - 32, m] = 1.0
    bf["rotT"] = rot
    for k, v in bf.items():
        c[k] = v.astype(ml_dtypes.bfloat16)
    return c


CONST_F32 = ["ident_f", "ones_f", "tri", "lsel", "l0", "l1", "invf"]
CONST_BF = ["ident_b", "ident4_b", "ones_b", "mS4", "mST4", "mIT4", "mB", "mB0", "rotT", "hm0", "hm1", "hm2", "hm3"]


def _slab_pieces():
    sl = []
    for i in range(24):
        sl.append([("w_in", i * SW, SW, 0)])
    sl.append([("w_in", 6144, 32, 0)])
    for i in range(8):
        sl.append([("w_in", 6176 + i * SW, SW, 0)])
    for i in range(8):
        sl.append([("w_in", 8224 + i * SW, SW, 0)])
    kb = 10272
    sl.append([("w_in", kb, 64, 0), ("w_in", kb, 64, 64), ("w_in", kb + 64, 64, 128), ("w_in", kb + 64, 64, 192)])
    sl.append([("w_in", kb + 128, 64, 0), ("w_in", kb + 128, 64, 64), ("w_in", kb + 192, 64, 128), ("w_in", kb + 192, 64, 192)])
    sl.append([("w_in", 10528, SW, 0)])
    for i in range(8):
        sl.append([("w_in", 10784 + i * SW, SW, 0)])
    for i in range(8):
        sl.append([("w_in", 12832 + i * SW, SW, 0)])
    for i in range(8):
        sl.append([("w_in", 14880 + i * SW, SW, 0)])
    for nm in ("proj_a", "proj_b", "w_out"):
        for i in range(8):
            sl.append([(nm, i * SW, SW, 0)])
    assert len(sl) == NSLAB
    return sl


def _layer_order():
    o = [24]
    for hf in range(2):
        o += [hf * 4 + i for i in range(4)] + [8 + hf * 4 + i for i in range(4)] + [16 + hf * 4 + i for i in range(4)]
    o += list(range(25, 33))
    for g in range(4):
        o += [33 + 2 * g, 34 + 2 * g]
        if g % 2 == 0:
            o += [41 + g // 2]
        if g == 0:
            o += [43]
        o += [44 + 2 * g, 45 + 2 * g]
    for j in range(8):
        o += [52 + j, 68 + j, 60 + j, 76 + j]
    o += list(range(84, 92))
    return o


STOP = None


class StopBuild(Exception):
    pass


CUR = [None]
NAMES = {}


def ckpt(name):
    if STOP == name:
        CUR[0].stopped = True


def build_nc(S, DEPTH=2):
    NT = S // T
    nc = bass.Bass("TRN2", target_bir_lowering=False)
    dram = {}

    def din(name, shape, dt=F32):
        dram[name] = nc.dram_tensor(name, list(shape), dt, kind="ExternalInput").ap()
        return dram[name]

    x_d = din("x", [S, D])
    c_d = din("c", [16, 128])
    pos_d = din("positions", [1, S], I32)
    adaw_d = din("ada_w", [DEPTH, D, 3 * D])
    adab_d = din("ada_b", [DEPTH, 1, 3 * D])
    normw_d = din("norm_w", [DEPTH, 16, 128])
    win_d = din("w_in", [DEPTH, D, IN_COLS])
    convw_d = din("conv_w", [DEPTH, 192, 128])
    alog_d = din("gdn_a_log", [DEPTH, 1, 16])
    dtb_d = din("gdn_dt_bias", [DEPTH, 1, 16])
    gnw_d = din("gdn_norm_w", [DEPTH, 128, 1])
    sink_d = din("swa_sinks", [DEPTH, 1, 32])
    pa_d = din("proj_a", [DEPTH, D, D])
    pb_d = din("proj_b", [DEPTH, D, D])
    wo_d = din("w_out", [DEPTH, D, D])
    fnw_d = din("final_norm_w", [1, D])
    for n in CONST_F32:
        din(n, [128, 1] if n == "invf" else [128, 128])
    for n in CONST_BF:
        w = {"ident4_b": 512, "mS4": 512, "mST4": 512, "mIT4": 512, "mB": 256, "mB0": 256}.get(n, 128)
        din(n, [128, w], BF16)
    out_d = nc.dram_tensor("out", [S, D], F32, kind="ExternalOutput").ap()
    wb_d = nc.dram_tensor("wb", [DEPTH, NSLAB, 128, KC * SW], BF16, kind="Internal").ap()
    gate_d = nc.dram_tensor("gate_s", [DEPTH, 1, D], F32, kind="Internal").ap()
    wsrc = {"w_in": win_d, "proj_a": pa_d, "proj_b": pb_d, "w_out": wo_d}
    pieces = _slab_pieces()
    order = []
    for t in range(NT):
        for l in range(DEPTH):
            order += [(l, s) for s in _layer_order()]

    with ExitStack() as es:
        sc = Sched(nc, es)
        op = sc.op

        def P(name, shape, dt):
            return Tn(es.enter_context(nc.sbuf_tensor(name, list(shape), dt)))

        ps = [Tn(es.enter_context(nc.psum_tensor("psb%d" % i, [128, 512], F32)), excl=True) for i in range(8)]
        C = {}
        for n in CONST_F32 + CONST_BF:
            shp = list(dram[n].shape)
            C[n] = P("c_" + n, shp, BF16 if n in CONST_BF else F32)
            sc.dma("sp", [(C[n][:], dram[n][:, :])], writes=[C[n]])
        ident_f, ones_f, ident_b, ones_b = C["ident_f"], C["ones_f"], C["ident_b"], C["ones_b"]
        one11 = ones_f
        xs_ = [P("x%d" % s, [128, D], F32) for s in range(4)]
        hT = P("hT", [128, KC, T], BF16)
        oaT = P("oaT", [128, KC, T], BF16)
        ring = [P("ring%d" % i, [128, KC, SW], BF16) for i in range(RING)]
        Sst = [[P("S%d_%d" % (l, j), [128, 512], F32) for j in range(4)] for l in range(DEPTH)]
        bc8k = P("bc8k", [128, D], F32)
        cs = [P("cs%d" % l, [128, 48, 3], F32) for l in range(DEPTH)]
        kh = [P("kh%d" % l, [128, 4, 128], BF16) for l in range(DEPTH)]
        vh = [P("vh%d" % l, [128, 4, 64], BF16) for l in range(DEPTH)]
        AT = [P("AT%d" % l, [128, 16], F32) for l in range(DEPTH)]
        BT = [P("BT%d" % l, [128, 16], F32) for l in range(DEPTH)]
        cw = [P("cw%d" % l, [128, 4, 48], F32) for l in range(DEPTH)]
        negA = [P("negA%d" % l, [128, 16], F32) for l in range(DEPTH)]
        dtb = [P("dtb%d" % l, [128, 16], F32) for l in range(DEPTH)]
        gnw = [P("gnw%d" % l, [128, 1], F32) for l in range(DEPTH)]
        sinkb = [P("sink%d" % l, [128, 32], F32) for l in range(DEPTH)]
        cosT = P("cosT", [128, T], F32)
        sinT = P("sinT", [128, T], F32)
        for l in range(DEPTH):
            for j in range(4):
                op("pool", lambda e, a=Sst[l][j]: e.memset(a[:], 0.0), writes=[Sst[l][j]])
            op("pool", lambda e, a=cs[l]: e.memset(a[:], 0.0), writes=[cs[l]])
            op("pool", lambda e, a=kh[l]: e.memset(a[:], 0.0), writes=[kh[l]])
            op("pool", lambda e, a=vh[l]: e.memset(a[:], 0.0), writes=[vh[l]])
            sc.dma("sp", [(dtb[l][:], dtb_d[l].partition_broadcast(128))], writes=[dtb[l]])
            sc.dma("sp", [(negA[l][:], alog_d[l].partition_broadcast(128))], writes=[negA[l]])
            sc.dma("sp", [(sinkb[l][:], sink_d[l].partition_broadcast(128))], writes=[sinkb[l]])
            sc.dma("sp", [(gnw[l][:], gnw_d[l])], writes=[gnw[l]])
            op("act", lambda e, a=negA[l]: e.activation(out=a[:], in_=a[:], func=AF.Exp), reads=[negA[l]], writes=[negA[l]])
            op("dve", lambda e, a=negA[l]: e.tensor_scalar(out=a[:], in0=a[:], scalar1=-1.0, scalar2=None, op0=ALU.mult),
               reads=[negA[l]], writes=[negA[l]])

        def mm(out, lhsT, rhs, start=True, stop=True):
            return lambda e: e.matmul(out, lhsT=lhsT, rhs=rhs, start=start, stop=stop, skip_group_check=True)

        def mm_group(bank, items, reads):
            def fn(e):
                inst = None
                for (o, l_, r_, st, sp) in items:
                    inst = e.matmul(o, lhsT=l_, rhs=r_, start=st, stop=sp, skip_group_check=True)
                    sc.ninst += 1
                return inst
            op("pe", fn, reads=reads, writes=[bank])

        def run_window(factories, nslots):
            pending = list(factories)
            slots = [None] * nslots
            while pending or any(x is not None for x in slots):
                for k in range(nslots):
                    if slots[k] is None and pending:
                        slots[k] = pending.pop(0)(k)
                    if slots[k] is not None:
                        try:
                            next(slots[k])
                        except StopIteration:
                            slots[k] = None

        def bc_last(ap2, n):
            return ap2.unsqueeze(2).broadcast_to([128, ap2.shape[1], n])

        def bc_mid(ap2, k):
            return ap2.unsqueeze(1).broadcast_to([128, k, ap2.shape[1]])

        def v3(ap, k):
            return ap.rearrange("p (k n) -> p k n", k=k)

        def body():
            with sc.scope() as pre:
                crow = pre.sb("crow", [16, 128], F32)
                cact = pre.sb("cact", [128, 16], F32)
                tmp16 = pre.sb("tmp16", [128, 16], F32)
                sc.dma("sp", [(crow[:], c_d[:, :])], writes=[crow])
                op("pe", mm(ps[0][:, 0:16], crow[:], ident_f[0:16, 0:16]), reads=[crow, ident_f], writes=[ps[0]])
                op("act", lambda e: e.activation(out=tmp16[:], in_=ps[0][:, 0:16], func=AF.Exp, scale=-1.0), reads=[ps[0]], writes=[tmp16])
                op("dve", lambda e: e.tensor_scalar(out=tmp16[:], in0=tmp16[:], scalar1=1.0, scalar2=None, op0=ALU.add), reads=[tmp16], writes=[tmp16])
                op("dve", lambda e: e.reciprocal(out=tmp16[:], in_=tmp16[:]), reads=[tmp16], writes=[tmp16])
                op("dve", lambda e: e.tensor_tensor(out=cact[:], in0=ps[0][:, 0:16], in1=tmp16[:], op=ALU.mult), reads=[ps[0], tmp16], writes=[cact])
                ckpt("c1")
                row = pre.sb("row", [1, D], F32)
                brow = pre.sb("brow", [1, D], F32)
                wst = [pre.sb("wst%d" % i, [128, D], F32) for i in range(2)]
                nwrow = pre.sb("nwrow", [16, 128], F32)
                cwrow = [pre.sb("cwrow%d" % i, [96, 128], F32) for i in range(2)]
                modT = pre.sb("modT", [128, 32], F32)
                nwT = pre.sb("nwT", [128, 16], F32)
                wi = 0
                for l in range(DEPTH):
                    for third in range(3):
                        sc.dma("sp", [(brow[:], adab_d[l, :, third * D:(third + 1) * D])], writes=[brow])
                        for kc in range(KC):
                            w = wst[wi % 2]
                            wi += 1
                            sc.dma("sp", [(w[:, 0:1024], adaw_d[l, kc * 128:(kc + 1) * 128, third * D:third * D + 1024]),
                                          (w[:, 1024:2048], adaw_d[l, kc * 128:(kc + 1) * 128, third * D + 1024:third * D + 2048])], writes=[w])
                            for j in range(4):
                                op("pe", mm(ps[j][0:1, :], cact[:, kc:kc + 1], w[:, j * 512:(j + 1) * 512], start=(kc == 0), stop=(kc == KC - 1)),
                                   reads=[cact, w], writes=[ps[j]])
                        for j in range(4):
                            c0 = j * 512
                            op("dve", lambda e, j=j, c0=c0: e.tensor_tensor(out=row[0:1, c0:c0 + 512], in0=ps[j][0:1, :], in1=brow[0:1, c0:c0 + 512], op=ALU.add),
                               reads=[ps[j], brow], writes=[row])
                        if third == 2:
                            sc.dma("sp", [(gate_d[l], row[0:1, :])], reads=[row])
                        else:
                            for cidx in range(16):
                                op("pe", mm(ps[6][:, third * 16 + cidx:third * 16 + cidx + 1], row[0:1, cidx * 128:(cidx + 1) * 128], one11[0:1, 0:1]),
                                   reads=[row, ones_f], writes=[ps[6]])
                    ckpt("c2")
                    op("act", lambda e: e.activation(out=modT[:], in_=ps[6][:, 0:32], func=AF.Copy), reads=[ps[6]], writes=[modT])
                    sc.dma("sp", [(nwrow[:], normw_d[l])], writes=[nwrow])
                    op("pe", mm(ps[7][:, 0:16], nwrow[:], ident_f[0:16, 0:16]), reads=[nwrow, ident_f], writes=[ps[7]])
                    op("act", lambda e: e.activation(out=nwT[:], in_=ps[7][:, 0:16], func=AF.Copy), reads=[ps[7]], writes=[nwT])
                    op("dve", lambda e, l=l: e.scalar_tensor_tensor(out=AT[l][:], in0=modT[:, 16:32], scalar=1.0, in1=nwT[:], op0=ALU.add, op1=ALU.mult),
                       reads=[modT, nwT], writes=[AT[l]])
                    op("dve", lambda e, l=l: e.tensor_copy(out=BT[l][:], in_=modT[:, 0:16]), reads=[modT], writes=[BT[l]])
                    ckpt("c3")
                    for i in range(2):
                        sc.dma("sp", [(cwrow[i][:], convw_d[l, i * 96:(i + 1) * 96, :])], writes=[cwrow[i]])
                        op("pe", mm(ps[7][:, 32 + i * 96:32 + (i + 1) * 96], cwrow[i][:], ident_f[0:96, 0:96]), reads=[cwrow[i], ident_f], writes=[ps[7]])
                    op("act", lambda e, l=l: e.activation(out=cw[l][:].rearrange("p a b -> p (a b)"), in_=ps[7][:, 32:32 + 192], func=AF.Copy),
                       reads=[ps[7]], writes=[cw[l]])
            ckpt("pre1")
            with sc.scope() as pre2:
                stf = [pre2.sb("stf%d" % i, [128, KC, SW], F32) for i in range(3)]
                stb = [pre2.sb("stb%d" % i, [128, KC * SW], BF16) for i in range(2)]
                jobs = [(l, s) for l in range(DEPTH) for s in range(NSLAB)]

                def emit_load(i):
                    l, s = jobs[i]
                    f = stf[i % 3]
                    pairs = []
                    for (nm, c0, ncol, dc) in pieces[s]:
                        src = wsrc[nm][l, :, c0:c0 + ncol].rearrange("(k p) c -> p k c", p=128)
                        for q in range(4):
                            pairs.append((f[:, q * 4:(q + 1) * 4, dc:dc + ncol], src[:, q * 4:(q + 1) * 4, :]))
                    wcols = max(dc + ncol for (_, _, ncol, dc) in pieces[s])
                    if wcols < SW:
                        op("pool", lambda e, f=f: e.memset(f[:], 0.0), writes=[f])
                    sc.dma("sp", pairs, writes=[f])

                emit_load(0)
                emit_load(1)
                for i, (l, s) in enumerate(jobs):
                    if i + 2 < len(jobs):
                        emit_load(i + 2)
                    f = stf[i % 3]
                    b = stb[i % 2]
                    ff = f[:].rearrange("p k c -> p (k c)")
                    n = KC * SW
                    a1, a2 = 2000, 3100
                    sc.par([("act", lambda e, b=b, ff=ff: e.activation(out=b[:, 0:a1], in_=ff[:, 0:a1], func=AF.Copy)),
                            ("dve", lambda e, b=b, ff=ff: e.tensor_copy(out=b[:, a1:a2], in_=ff[:, a1:a2])),
                            ("pool", lambda e, b=b, ff=ff: e.tensor_copy(out=b[:, a2:n], in_=ff[:, a2:n]))], reads=[f], writes=[b])
                    sc.dma("sp", [(wb_d[l, s], b[:])], reads=[b])
                    _merge(sc.residue, b.r)
            ckpt("pre2")
            sc._wait("sp", dict(sc.residue))

            wstate = {"i": 0, "issued": 0}

            def issue_slab():
                k = wstate["issued"]
                if k >= len(order):
                    return
                l_, s_ = order[k]
                slot = ring[k % RING]
                half = KC * SW // 2
                dst = slot[:].rearrange("p k c -> p (k c)")
                sc.dma("sp", [(dst[:, 0:half], wb_d[l_, s_, :, 0:half]), (dst[:, half:2 * half], wb_d[l_, s_, :, half:2 * half])], writes=[slot])
                wstate["issued"] += 1

            for _ in range(RING - 1):
                issue_slab()

            def next_slab(l_, s_):
                i = wstate["i"]
                assert order[i] == (l_, s_), (order[i], l_, s_)
                issue_slab()
                wstate["i"] += 1
                return ring[i % RING]

            def rsqrt_small(sco, ss, scale, eps, name):
                lnv = sco.sb(name + "ln", [128, 1], F32)
                r = sco.sb(name + "r", [128, 1], F32)
                op("dve", lambda e: e.tensor_scalar(out=lnv[:], in0=ss[:], scalar1=scale, scalar2=eps, op0=ALU.mult, op1=ALU.add), reads=[ss], writes=[lnv])
                op("act", lambda e: e.activation(out=lnv[:], in_=lnv[:], func=AF.Ln), reads=[lnv], writes=[lnv])
                op("act", lambda e: e.activation(out=r[:], in_=lnv[:], func=AF.Exp, scale=-0.5), reads=[lnv], writes=[r])
                return r

            for t in range(NT):
                t0 = t * T
                for s in range(4):
                    if t > 0:
                        break
                    sc.dma("act", [(xs_[s][:, 0:1024], x_d[t0 + s * 128:t0 + (s + 1) * 128, 0:1024]),
                                   (xs_[s][:, 1024:2048], x_d[t0 + s * 128:t0 + (s + 1) * 128, 1024:2048])], writes=[xs_[s]])
                def emit_rope():
                    with sc.scope() as rs_:
                        pib = rs_.sb("pib", [128, T], I32)
                        ang = rs_.sb("ang", [128, T], F32)
                        u = rs_.sb("u", [128, T], F32)
                        ni = rs_.sb("ni", [128, T], I32)
                        m_ = rs_.sb("m_", [128, T], F32)
                        sc.dma("sp", [(pib[:], pos_d[0:1, t0:t0 + T].partition_broadcast(128))], writes=[pib])
                        op("dve", lambda e: e.tensor_copy(out=ang[:], in_=pib[:]), reads=[pib], writes=[ang])
                        op("dve", lambda e: e.tensor_scalar(out=ang[:], in0=ang[:], scalar1=C["invf"][:, 0:1], scalar2=None, op0=ALU.mult),
                           reads=[ang, C["invf"]], writes=[ang])
                        for (dst, off) in ((sinT, 0.0), (cosT, math.pi / 2)):
                            op("dve", lambda e, off=off: e.tensor_scalar(out=u[:], in0=ang[:], scalar1=off, scalar2=1.0 / TWO_PI, op0=ALU.add, op1=ALU.mult),
                               reads=[ang], writes=[u])
                            op("dve", lambda e: e.tensor_copy(out=ni[:], in_=u[:]), reads=[u], writes=[ni])
                            op("dve", lambda e: e.tensor_copy(out=u[:], in_=ni[:]), reads=[ni], writes=[u])
                            op("dve", lambda e: e.scalar_tensor_tensor(out=u[:], in0=u[:], scalar=-TWO_PI, in1=ang[:], op0=ALU.mult, op1=ALU.add),
                               reads=[u, ang], writes=[u])
                            if off != 0.0:
                                op("dve", lambda e, off=off: e.tensor_scalar(out=u[:], in0=u[:], scalar1=off, scalar2=None, op0=ALU.add), reads=[u], writes=[u])
                            op("dve", lambda e: e.tensor_scalar(out=m_[:], in0=u[:], scalar1=math.pi, scalar2=-TWO_PI, op0=ALU.is_gt, op1=ALU.mult), reads=[u], writes=[m_])
                            op("dve", lambda e: e.tensor_tensor(out=u[:], in0=u[:], in1=m_[:], op=ALU.add), reads=[u, m_], writes=[u])
                            op("dve", lambda e: e.tensor_scalar(out=m_[:], in0=u[:], scalar1=-math.pi, scalar2=TWO_PI, op0=ALU.is_lt, op1=ALU.mult), reads=[u], writes=[m_])
                            op("dve", lambda e: e.tensor_tensor(out=u[:], in0=u[:], in1=m_[:], op=ALU.add), reads=[u, m_], writes=[u])
                            op("act", lambda e, dst=dst: e.activation(out=dst[:], in_=u[:], func=AF.Sin), reads=[u], writes=[dst])

                for l in range(DEPTH):
                    sc.dma("sp", [(bc8k[:, 0:1024], gate_d[l, :, 0:1024].partition_broadcast(128)),
                                  (bc8k[:, 1024:2048], gate_d[l, :, 1024:2048].partition_broadcast(128))], writes=[bc8k])
                    with sc.scope() as p0:
                        junk = p0.sb("junk", [128, D], BF16)
                        ss = p0.sb("ss", [128, 1], F32)
                        dg = p0.sb("dg", [128, 128], F32)
                        tmpm = [p0.sb("tmpm%d" % i, [128, 512], F32) for i in range(2)]
                        for s in range(4):
                            op("act", lambda e, s=s: e.activation(out=junk[:], in_=xs_[s][:], func=AF.Square, accum_out=ss[:, 0:1]), reads=[xs_[s]], writes=[junk, ss])
                            rstd = rsqrt_small(p0, ss, 1.0 / D, 1e-6, "p0")
                            op("dve", lambda e: e.tensor_scalar(out=dg[:], in0=ident_f[:], scalar1=rstd[:, 0:1], scalar2=None, op0=ALU.mult),
                               reads=[ident_f, rstd], writes=[dg])
                            for q4 in range(4):
                                mm_group(ps[q4], [(ps[q4][:, i * 128:(i + 1) * 128], xs_[s][:, (q4 * 4 + i) * 128:(q4 * 4 + i + 1) * 128], dg[:], True, True)
                                                  for i in range(4)], reads=[xs_[s], dg])
                                tm = tmpm[q4 % 2]
                                op("dve", lambda e, q4=q4, tm=tm: e.tensor_tensor(out=v3(tm[:], 4), in0=v3(ps[q4][:], 4), in1=bc_last(AT[l][:, q4 * 4:q4 * 4 + 4], 128), op=ALU.mult),
                                   reads=[ps[q4], AT[l]], writes=[tm])
                                op("pool", lambda e, q4=q4, tm=tm, s=s: e.tensor_tensor(out=hT[:, q4 * 4:q4 * 4 + 4, s * 128:(s + 1) * 128], in0=v3(tm[:], 4),
                                                                                      in1=bc_last(BT[l][:, q4 * 4:q4 * 4 + 4], 128), op=ALU.add),
                                   reads=[tm, BT[l]], writes=[hT])

                    ckpt("p0")
                    if l == 0:
                        emit_rope()
                        ckpt("rope")
                    with sc.scope() as gs:
                        sm = {n: gs.sb("sm_" + n, [128, 4, 16], F32) for n in ("bet", "q1", "gc", "ngc", "kdf", "ngam", "dec0", "dec1", "g", "tmp")}
                        slab = next_slab(l, 24)
                        mm_group(ps[0], [(ps[0][:, b * 32:(b + 1) * 32], hT[:, kc, b * 128:(b + 1) * 128], slab[:, kc, 0:32], kc == 0, kc == KC - 1)
                                         for b in range(4) for kc in range(KC)], reads=[hT, slab])
                        pv = v3(ps[0][:, 0:128], 4)
                        op("act", lambda e: e.activation(out=sm["tmp"][:], in_=pv[:, :, 0:16], func=AF.Exp, scale=-1.0), reads=[ps[0]], writes=[sm["tmp"]])
                        op("dve", lambda e: e.tensor_scalar(out=sm["tmp"][:], in0=sm["tmp"][:], scalar1=1.0, scalar2=None, op0=ALU.add), reads=[sm["tmp"]], writes=[sm["tmp"]])
                        op("dve", lambda e: e.reciprocal(out=sm["bet"][:], in_=sm["tmp"][:]), reads=[sm["tmp"]], writes=[sm["bet"]])
                        op("act", lambda e: e.activation(out=sm["q1"][:], in_=sm["bet"][:], func=AF.Ln), reads=[sm["bet"]], writes=[sm["q1"]])
                        op("dve", lambda e: e.tensor_tensor(out=sm["g"][:], in0=pv[:, :, 16:32], in1=bc_mid(dtb[l][:, :], 4), op=ALU.add), reads=[ps[0], dtb[l]], writes=[sm["g"]])
                        op("act", lambda e: e.activation(out=sm["g"][:], in_=sm["g"][:], func=AF.Exp), reads=[sm["g"]], writes=[sm["g"]])
                        op("act", lambda e: e.activation(out=sm["g"][:], in_=sm["g"][:], func=AF.Ln, bias=1.0), reads=[sm["g"]], writes=[sm["g"]])
                        op("dve", lambda e: e.tensor_tensor(out=sm["g"][:], in0=sm["g"][:], in1=bc_mid(negA[l][:, :], 4), op=ALU.mult), reads=[sm["g"], negA[l]], writes=[sm["g"]])
                        flat = lambda n: sm[n][:].rearrange("p a b -> p (a b)")
                        op("pe", mm(ps[1][:, 0:64], C["tri"][:], flat("g")), reads=[C["tri"], sm["g"]], writes=[ps[1]])
                        op("dve", lambda e: e.tensor_copy(out=flat("gc"), in_=ps[1][:, 0:64]), reads=[ps[1]], writes=[sm["gc"]])
                        op("dve", lambda e: e.tensor_scalar(out=flat("ngc"), in0=ps[1][:, 0:64], scalar1=-1.0, scalar2=None, op0=ALU.mult), reads=[ps[1]], writes=[sm["ngc"]])
                        op("dve", lambda e: e.tensor_tensor(out=flat("q1"), in0=flat("q1"), in1=flat("ngc"), op=ALU.add), reads=[sm["q1"], sm["ngc"]], writes=[sm["q1"]])
                        op("act", lambda e: e.activation(out=flat("ngam"), in_=flat("gc"), func=AF.Exp), reads=[sm["gc"]], writes=[sm["ngam"]])
                        op("dve", lambda e: e.tensor_scalar(out=flat("ngam"), in0=flat("ngam"), scalar1=-1.0, scalar2=None, op0=ALU.mult), reads=[sm["ngam"]], writes=[sm["ngam"]])
                        op("pe", mm(ps[2][:, 0:64], C["lsel"][:], flat("gc")), reads=[C["lsel"], sm["gc"]], writes=[ps[2]])
                        op("pe", mm(ps[2][:, 64:128], C["l0"][:], flat("gc")), reads=[C["l0"], sm["gc"]], writes=[ps[2]])
                        op("pe", mm(ps[2][:, 128:192], C["l1"][:], flat("gc")), reads=[C["l1"], sm["gc"]], writes=[ps[2]])
                        op("dve", lambda e: e.tensor_tensor(out=flat("kdf"), in0=ps[2][:, 0:64], in1=flat("ngc"), op=ALU.add), reads=[ps[2], sm["ngc"]], writes=[sm["kdf"]])
                        op("act", lambda e: e.activation(out=flat("kdf"), in_=flat("kdf"), func=AF.Exp), reads=[sm["kdf"]], writes=[sm["kdf"]])
                        op("act", lambda e: e.activation(out=flat("dec0"), in_=ps[2][:, 64:128], func=AF.Exp), reads=[ps[2]], writes=[sm["dec0"]])
                        op("act", lambda e: e.activation(out=flat("dec1"), in_=ps[2][:, 128:192], func=AF.Exp), reads=[ps[2]], writes=[sm["dec1"]])

                        ckpt("ba")
                        qT = gs.sb("qT", [128, 8, T], BF16)
                        kT = gs.sb("kT", [128, 8, T], BF16)
                        vT = gs.sb("vT", [128, 8, T], BF16)
                        for hf in range(2):
                            with sc.scope() as p1:
                                psets = []
                                for k_ in range(2):
                                    psets.append(dict(xsb=[p1.sb("xsb", [128, T + 3], F32) for i in range(2)], acca=[p1.sb("acca", [128, T], F32) for i in range(2)],
                                                      svb=[p1.sb("svb", [128, T], F32) for i in range(2)], sqb=[p1.sb("sqb", [128, T], BF16) for i in range(2)],
                                                      rnb=[p1.sb("rnb", [128, T], F32) for i in range(2)], pb=[ps[2 * k_], ps[2 * k_ + 1]], sb_=[ps[4 + 2 * k_], ps[5 + 2 * k_]]))
                                chunks = [(kind, hl) for kind in range(3) for hl in range(8)]

                                def p1_pair(pi_, B_):
                                    kind = chunks[2 * pi_][0]
                                    h_first = hf * 8 + chunks[2 * pi_][1]
                                    slab = next_slab(l, kind * 8 + h_first // 2)
                                    info = []
                                    for sub_i in range(2):
                                        hl = chunks[2 * pi_ + sub_i][1]
                                        h = hf * 8 + hl
                                        sub = h % 2
                                        ci = kind * 16 + h
                                        bank = B_["pb"][sub_i]
                                        mm_group(bank, [(bank[:], slab[:, kc, sub * 128:(sub + 1) * 128], hT[:, kc, :], kc == 0, kc == KC - 1) for kc in range(KC)],
                                                 reads=[slab, hT])
                                        info.append((hl, ci, bank))
                                    yield
                                    for sub_i, (hl, ci, bank) in enumerate(info):
                                        xb_ = B_["xsb"][sub_i]
                                        sc.par([("pool", lambda e, xb_=xb_, ci=ci: e.tensor_copy(out=xb_[:, 0:3], in_=cs[l][:, ci, :])),
                                                ("act", lambda e, xb_=xb_, bank=bank: e.activation(out=xb_[:, 3:T + 3], in_=bank[:], func=AF.Copy))],
                                               reads=[cs[l], bank], writes=[xb_])
                                        op("pool", lambda e, xb_=xb_, ci=ci: e.tensor_copy(out=cs[l][:, ci, :], in_=xb_[:, T:T + 3]), reads=[xb_], writes=[cs[l]])
                                    yield
                                    for sub_i, (hl, ci, bank) in enumerate(info):
                                        xb_, aa = B_["xsb"][sub_i], B_["acca"][sub_i]
                                        op("dve", lambda e, xb_=xb_, aa=aa, ci=ci: e.tensor_scalar(out=aa[:], in0=xb_[:, 0:T], scalar1=cw[l][:, 0, ci:ci + 1], scalar2=None, op0=ALU.mult),
                                           reads=[xb_, cw[l]], writes=[aa])
                                        for tap in (1, 2, 3):
                                            op("dve", lambda e, xb_=xb_, aa=aa, ci=ci, tap=tap: e.scalar_tensor_tensor(out=aa[:], in0=xb_[:, tap:T + tap], scalar=cw[l][:, tap, ci:ci + 1], in1=aa[:],
                                                                                                                 op0=ALU.mult, op1=ALU.add), reads=[xb_, cw[l], aa], writes=[aa])
                                    yield
                                    if kind == 2:
                                        for sub_i, (hl, ci, bank) in enumerate(info):
                                            aa = B_["acca"][sub_i]
                                            op("act", lambda e, aa=aa, hl=hl: e.activation(out=vT[:, hl, :], in_=aa[:], func=AF.Silu), reads=[aa], writes=[vT])
                                        return
                                    for sub_i, (hl, ci, bank) in enumerate(info):
                                        aa, sv, sq = B_["acca"][sub_i], B_["svb"][sub_i], B_["sqb"][sub_i]
                                        op("act", lambda e, aa=aa, sv=sv: e.activation(out=sv[:], in_=aa[:], func=AF.Silu), reads=[aa], writes=[sv])
                                        op("act", lambda e, sv=sv, sq=sq: e.activation(out=sq[:], in_=sv[:], func=AF.Square), reads=[sv], writes=[sq])
                                        bank2 = B_["sb_"][sub_i]
                                        op("pe", mm(bank2[:], ones_b[:], sq[:]), reads=[ones_b, sq], writes=[bank2])
                                    yield
                                    dst = qT if kind == 0 else kT
                                    scl = 128.0 ** -0.5 if kind == 0 else 1.0
                                    for sub_i, (hl, ci, bank) in enumerate(info):
                                        sv, rn, bank2 = B_["svb"][sub_i], B_["rnb"][sub_i], B_["sb_"][sub_i]
                                        op("act", lambda e, bank2=bank2, rn=rn: e.activation(out=rn[:], in_=bank2[:], func=AF.Ln, bias=1e-6), reads=[bank2], writes=[rn])
                                        op("act", lambda e, rn=rn: e.activation(out=rn[:], in_=rn[:], func=AF.Exp, scale=-0.5, bias=math.log(scl)), reads=[rn], writes=[rn])
                                        op("pool", lambda e, sv=sv, rn=rn, hl=hl: e.tensor_tensor(out=dst[:, hl, :], in0=sv[:], in1=rn[:], op=ALU.mult), reads=[sv, rn], writes=[dst])

                                for pp in range(6):
                                    alive = [p1_pair(2 * pp, psets[0]), p1_pair(2 * pp + 1, psets[1])]
                                    while alive:
                                        for gen in list(alive):
                                            try:
                                                next(gen)
                                            except StopIteration:
                                                alive.remove(gen)
                            ckpt("p1")
                            tg = sc.scope()
                            tg.__enter__()
                            TT = tg.sb("TT", [128, 4, 8, 128], BF16)
                            AqT = tg.sb("AqT", [128, 4, 8, 128], BF16)
                            with sc.scope() as g1:
                                def msk(dst, src, m):
                                    op("pool", lambda e: e.tensor_tensor(out=v3(dst[:], 4), in0=v3(src[:], 4), in1=bc_mid(C[m][:, :], 4), op=ALU.mult), reads=[src, C[m]], writes=[dst])

                                def ev_copy(dst, bank):
                                    op("act", lambda e: e.activation(out=dst[:], in_=bank[:], func=AF.Copy), reads=[bank], writes=[dst])

                                def ev_add(dst, bank, addend):
                                    op("dve", lambda e: e.tensor_tensor(out=dst[:], in0=bank[:], in1=addend[:], op=ALU.add), reads=[bank, addend], writes=[dst])

                                gsets = []
                                for gi in range(2):
                                    gsets.append(dict(
                                        RD1=g1.sb("RD1", [128, 512], F32), RD2=g1.sb("RD2", [128, 512], F32),
                                        gbc=g1.sb("gbc", [128, 512], BF16), E2T=g1.sb("E2T", [128, 512], BF16),
                                        E1T=g1.sb("E1T", [128, 512], BF16), E1=g1.sb("E1", [128, 512], BF16),
                                        Bf=[g1.sb("Bf%d" % i, [128, 512], BF16) for i in range(8)],
                                        banks=ps[4 * gi:4 * gi + 4], ctr=[0]))

                                def g1_group(b, j, G):
                                    RD1, RD2, gbc, E2T, E1T, E1, Bf = G["RD1"], G["RD2"], G["gbc"], G["E2T"], G["E1T"], G["E1"], G["Bf"]

                                    def nb():
                                        bank = G["banks"][G["ctr"][0] % 4]
                                        G["ctr"][0] += 1
                                        return bank

                                    def mm4(lhs, rhs):
                                        bank = nb()
                                        mm_group(bank, [(bank[:, hh * 128:(hh + 1) * 128], lhs[:, hh * 128:(hh + 1) * 128], rhs[:, hh * 128:(hh + 1) * 128], True, True) for hh in range(4)],
                                                 reads=[lhs, rhs])
                                        return bank
                                    bs = slice(b * 128, (b + 1) * 128)
                                    h0 = hf * 8 + j * 4
                                    hl0 = j * 4
                                    op("dve", lambda e: e.tensor_tensor(out=v3(RD1[:], 4), in0=bc_mid(ident_f[:, :], 4), in1=bc_last(sm["gc"][:, b, h0:h0 + 4], 128), op=ALU.mult),
                                       reads=[ident_f, sm["gc"]], writes=[RD1])
                                    op("pool", lambda e: e.tensor_tensor(out=v3(RD2[:], 4), in0=bc_mid(ident_f[:, :], 4), in1=bc_last(sm["q1"][:, b, h0:h0 + 4], 128), op=ALU.mult),
                                       reads=[ident_f, sm["q1"]], writes=[RD2])
                                    pG, pH, pI = nb(), nb(), nb()
                                    op("pe", mm(pG[:], ones_f[:], RD1[:], True, False), reads=[ones_f, RD1], writes=[pG])
                                    mm_group(pH, [(pH[:], ones_f[:], RD1[:], True, False), (pH[:], ident_b[:], C["mST4"][:], False, True)], reads=[ones_f, RD1, ident_b, C["mST4"]])
                                    mm_group(pI, [(pI[:], ones_f[:], RD2[:], True, False), (pI[:], ident_b[:], C["mS4"][:], False, True)], reads=[ones_f, RD2, ident_b, C["mS4"]])
                                    yield
                                    op("act", lambda e: e.activation(out=gbc[:], in_=pG[:], func=AF.Exp), reads=[pG], writes=[gbc])
                                    op("pe", mm(pG[:], ident_b[:], C["mIT4"][:], False, True), reads=[ident_b, C["mIT4"]], writes=[pG])
                                    for (bank, Et, bias) in ((pH, E1T, "q1"), (pI, E1, "gc")):
                                        for hh in range(4):
                                            op("act", lambda e, bank=bank, Et=Et, bias=bias, hh=hh: e.activation(
                                                out=Et[:, hh * 128:(hh + 1) * 128], in_=bank[:, hh * 128:(hh + 1) * 128], func=AF.Exp, bias=sm[bias][:, b, h0 + hh:h0 + hh + 1]),
                                               reads=[bank, sm[bias]], writes=[Et])
                                    pK = nb()
                                    mm_group(pK, [(pK[:, hh * 128:(hh + 1) * 128], kT[:, hl0 + hh, bs], kT[:, hl0 + hh, bs], True, True) for hh in range(4)], reads=[kT])
                                    yield
                                    for hh in range(4):
                                        op("act", lambda e, hh=hh: e.activation(out=E2T[:, hh * 128:(hh + 1) * 128], in_=pG[:, hh * 128:(hh + 1) * 128], func=AF.Exp,
                                                                              bias=sm["ngc"][:, b, h0 + hh:h0 + hh + 1]), reads=[pG, sm["ngc"]], writes=[E2T])
                                    P0, P0T = Bf[0], Bf[1]
                                    op("dve", lambda e: e.scalar_tensor_tensor(out=P0T[:], in0=pK[:], scalar=-1.0, in1=E1T[:], op0=ALU.mult, op1=ALU.mult), reads=[pK, E1T], writes=[P0T])
                                    op("dve", lambda e: e.scalar_tensor_tensor(out=P0[:], in0=pK[:], scalar=-1.0, in1=E1[:], op0=ALU.mult, op1=ALU.mult), reads=[pK, E1], writes=[P0])
                                    M, MT, M2, M2T, Y1, Y1T = Bf[2], Bf[3], Bf[4], Bf[5], Bf[6], Bf[7]
                                    msk(M, P0, "hm0")
                                    msk(MT, P0T, "hm0")
                                    yield
                                    pQ = nb()
                                    mm_group(pQ, [(pQ[:, hh * 128:(hh + 1) * 128], kT[:, hl0 + hh, bs], qT[:, hl0 + hh, bs], True, True) for hh in range(4)], reads=[kT, qT])
                                    bA = mm4(MT, M)
                                    bB = mm4(M, MT)
                                    yield
                                    op("dve", lambda e: e.tensor_tensor(out=AqT[:, b, hl0:hl0 + 4, :], in0=v3(pQ[:], 4), in1=v3(E2T[:], 4), op=ALU.mult), reads=[pQ, E2T], writes=[AqT])
                                    op("pool", lambda e: e.tensor_tensor(out=qT[:, hl0:hl0 + 4, bs], in0=qT[:, hl0:hl0 + 4, bs], in1=v3(gbc[:], 4), op=ALU.mult), reads=[qT, gbc], writes=[qT])
                                    ev_copy(M2, bA)
                                    op("dve", lambda e: e.tensor_copy(out=M2T[:], in_=bB[:]), reads=[bB], writes=[M2T])
                                    Y0, Y0T = M, MT
                                    op("pool", lambda e: e.tensor_tensor(out=Y0[:], in0=M[:], in1=C["ident4_b"][:], op=ALU.add), reads=[M, C["ident4_b"]], writes=[Y0])
                                    op("pool", lambda e: e.tensor_tensor(out=Y0T[:], in0=MT[:], in1=C["ident4_b"][:], op=ALU.add), reads=[MT, C["ident4_b"]], writes=[Y0T])
                                    yield
                                    bA = mm4(M2T, Y0)
                                    bB = mm4(M2, Y0T)
                                    bC = mm4(M2T, M2)
                                    bD = mm4(M2, M2T)
                                    yield
                                    ev_add(Y1, bA, Y0)
                                    ev_add(Y1T, bB, Y0T)
                                    M4, M4T = Bf[2], Bf[3]
                                    ev_copy(M4, bC)
                                    ev_copy(M4T, bD)
                                    yield
                                    bA = mm4(M4T, Y1)
                                    bB = mm4(M4, Y1T)
                                    yield
                                    Tm, Um = Bf[4], Bf[5]
                                    ev_add(Tm, bA, Y1)
                                    ev_add(Um, bB, Y1T)
                                    for lv in (1, 2):
                                        Pl, PlT, Wp, W = Bf[6], Bf[7], Bf[2], Bf[3]
                                        msk(Pl, P0, "hm%d" % lv)
                                        msk(PlT, P0T, "hm%d" % lv)
                                        yield
                                        bA = mm4(PlT, Tm)
                                        bB = mm4(Pl, Um)
                                        yield
                                        ev_copy(Wp, bA)
                                        op("dve", lambda e, W=W, bB=bB: e.tensor_copy(out=W[:], in_=bB[:]), reads=[bB], writes=[W])
                                        yield
                                        bA = mm4(Um, Wp)
                                        bB = mm4(Tm, W)
                                        yield
                                        ev_add(Tm, bA, Tm)
                                        ev_add(Um, bB, Um)
                                    Pl, W = Bf[6], Bf[2]
                                    msk(Pl, P0, "hm3")
                                    yield
                                    bA = mm4(Pl, Um)
                                    yield
                                    ev_copy(W, bA)
                                    yield
                                    bank = mm4(Tm, W)
                                    yield
                                    op("dve", lambda e: e.tensor_tensor(out=TT[:, b, hl0:hl0 + 4, :], in0=v3(bank[:], 4), in1=v3(Um[:], 4), op=ALU.add), reads=[bank, Um], writes=[TT])

                                run_window([(lambda k, b=b, j=j: g1_group(b, j, gsets[k])) for b in range(4) for j in range(2)], 2)
                            ckpt("g1")
                            with sc.scope() as g2:
                                Sbf = [g2.sb("Sbf%d" % j, [128, 512], BF16) for j in range(2)]
                                kdec = [g2.sb("kdec%d" % j, [128, 512], BF16) for j in range(2)]
                                vtok = [g2.sb("vtok%d" % j, [128, 512], BF16) for j in range(2)]
                                tmpr = [g2.sb("tmpr%d" % j, [128, 512], F32) for j in range(2)]
                                Rp = [g2.sb("Rp%d" % j, [128, 512], BF16) for j in range(2)]
                                Vn = [g2.sb("Vn%d" % j, [128, 512], BF16) for j in range(2)]
                                tmpS = [g2.sb("tmpS%d" % j, [128, 512], F32) for j in range(2)]
                                sqo = [g2.sb("sqo%d" % j, [128, 512], BF16) for j in range(2)]
                                rno = [g2.sb("rno%d" % j, [128, 512], F32) for j in range(2)]
                                for j in range(2):
                                    op("pool", lambda e, j=j: e.memset(Rp[j][:], 0.0), writes=[Rp[j]])
                                    op("pool", lambda e, j=j: e.memset(Vn[j][:], 0.0), writes=[Vn[j]])
                                St = [Sst[l][hf * 2 + j] for j in range(2)]
                                for j in range(2):
                                    op("act", lambda e, j=j: e.activation(out=Sbf[j][:], in_=St[j][:], func=AF.Copy), reads=[St[j]], writes=[Sbf[j]])
                                psA, psB, psC, OT = ps[0:2], ps[2:4], ps[4:6], ps[6:8]
                                for b in range(4):
                                    bs = slice(b * 128, (b + 1) * 128)
                                    for j in range(2):
                                        h0 = hf * 8 + j * 4
                                        hl0 = j * 4
                                        mm_group(psA[j], [(psA[j][:, hh * 128:(hh + 1) * 128], kT[:, hl0 + hh, bs], ident_b[:], True, True) for hh in range(4)], reads=[kT, ident_b])
                                        op("dve", lambda e, j=j, b=b, h0=h0: e.tensor_tensor(out=v3(kdec[j][:], 4), in0=v3(psA[j][:], 4), in1=bc_last(sm["kdf"][:, b, h0:h0 + 4], 128), op=ALU.mult),
                                           reads=[psA[j], sm["kdf"]], writes=[kdec[j]])
                                        mm_group(psB[j], [(psB[j][:, hh * 128:(hh + 1) * 128], vT[:, hl0 + hh, bs], ident_b[:], True, True) for hh in range(4)], reads=[vT, ident_b])
                                        op("act", lambda e, j=j: e.activation(out=vtok[j][:], in_=psB[j][:], func=AF.Copy), reads=[psB[j]], writes=[vtok[j]])
                                    for c in range(2):
                                        r0, r1 = 64 * c, 64 * c + 64
                                        decn = "dec%d" % c
                                        for j in range(2):
                                            hl0 = j * 4
                                            mm_group(psA[j], [(psA[j][:, hh * 128:(hh + 1) * 128], kT[:, hl0 + hh, bs], Sbf[j][:, hh * 128:(hh + 1) * 128], True, True) for hh in range(4)],
                                                     reads=[kT, Sbf[j]])
                                            mm_group(OT[j], [(OT[j][:, hh * 128 + r0:hh * 128 + r1], Sbf[j][:, hh * 128:(hh + 1) * 128], qT[:, hl0 + hh, b * 128 + r0:b * 128 + r1], (c == 0 and hh == 0), False)
                                                             for hh in range(4)], reads=[Sbf[j], qT])
                                        for j in range(2):
                                            h0 = hf * 8 + j * 4
                                            def rp_fn(e, j=j, b=b, h0=h0):
                                                inst = None
                                                for hh in range(4):
                                                    inst = e.scalar_tensor_tensor(out=Rp[j][r0:r1, hh * 128:(hh + 1) * 128], in0=psA[j][r0:r1, hh * 128:(hh + 1) * 128],
                                                                                  scalar=sm["ngam"][r0:r1, b, h0 + hh:h0 + hh + 1], in1=vtok[j][r0:r1, hh * 128:(hh + 1) * 128],
                                                                                  op0=ALU.mult, op1=ALU.add)
                                                    sc.ninst += 1
                                                return inst
                                            op("dve", rp_fn, reads=[psA[j], sm["ngam"], vtok[j]], writes=[Rp[j]])
                                        for j in range(2):
                                            hl0 = j * 4
                                            mm_group(psB[j], [(psB[j][:, hh * 128:(hh + 1) * 128], TT[r0:r1, b, hl0 + hh, :], Rp[j][r0:r1, hh * 128:(hh + 1) * 128], True, True) for hh in range(4)],
                                                     reads=[TT, Rp[j]])
                                        for j in range(2):
                                            h0 = hf * 8 + j * 4
                                            op("dve", lambda e, j=j, b=b, h0=h0: e.tensor_tensor(out=v3(Vn[j][r0:r1, :], 4), in0=v3(psB[j][r0:r1, :], 4),
                                                                                              in1=sm["bet"][r0:r1, b, h0:h0 + 4].unsqueeze(2).broadcast_to([64, 4, 128]), op=ALU.mult),
                                               reads=[psB[j], sm["bet"]], writes=[Vn[j]])
                                        for j in range(2):
                                            mm_group(psC[j], [(psC[j][:, hh * 128:(hh + 1) * 128], kdec[j][r0:r1, hh * 128:(hh + 1) * 128], Vn[j][r0:r1, hh * 128:(hh + 1) * 128], True, True)
                                                              for hh in range(4)], reads=[kdec[j], Vn[j]])
                                        for j in range(2):
                                            h0 = hf * 8 + j * 4
                                            def s_fn(e, j=j, b=b, h0=h0, decn=decn):
                                                inst = None
                                                for hh in range(4):
                                                    inst = e.scalar_tensor_tensor(out=St[j][:, hh * 128:(hh + 1) * 128], in0=St[j][:, hh * 128:(hh + 1) * 128],
                                                                                  scalar=sm[decn][:, b, h0 + hh:h0 + hh + 1], in1=psC[j][:, hh * 128:(hh + 1) * 128],
                                                                                  op0=ALU.mult, op1=ALU.add)
                                                    sc.ninst += 1
                                                return inst
                                            def sbf_fn(e, j=j, b=b, h0=h0, decn=decn):
                                                inst = None
                                                for hh in range(4):
                                                    inst = e.scalar_tensor_tensor(out=Sbf[j][:, hh * 128:(hh + 1) * 128], in0=St[j][:, hh * 128:(hh + 1) * 128],
                                                                                  scalar=sm[decn][:, b, h0 + hh:h0 + hh + 1], in1=psC[j][:, hh * 128:(hh + 1) * 128],
                                                                                  op0=ALU.mult, op1=ALU.add)
                                                    sc.ninst += 1
                                                return inst
                                            op("dve", sbf_fn, reads=[St[j], psC[j], sm[decn]], writes=[Sbf[j]])
                                            op("dve", s_fn, reads=[psC[j], sm[decn]], writes=[St[j]])
                                        ckpt("g2c%d" % c)
                                    for j in range(2):
                                        hl0 = j * 4
                                        h0 = hf * 8 + j * 4
                                        mm_group(OT[j], [(OT[j][:, hh * 128:(hh + 1) * 128], Vn[j][:, hh * 128:(hh + 1) * 128], AqT[:, b, hl0 + hh, :], False, True) for hh in range(4)],
                                                 reads=[Vn[j], AqT])
                                        if j == 0:
                                            ckpt("g2o")
                                        op("act", lambda e, j=j: e.activation(out=sqo[j][:], in_=OT[j][:], func=AF.Square), reads=[OT[j]], writes=[sqo[j]])
                                        op("pe", mm(psC[j][:], ones_b[:], sqo[j][:]), reads=[ones_b, sqo[j]], writes=[psC[j]])
                                        op("act", lambda e, j=j: e.activation(out=rno[j][:], in_=psC[j][:], func=AF.Ln, scale=1.0 / 128.0, bias=1e-6), reads=[psC[j]], writes=[rno[j]])
                                        op("act", lambda e, j=j: e.activation(out=rno[j][:], in_=rno[j][:], func=AF.Exp, scale=-0.5), reads=[rno[j]], writes=[rno[j]])
                                        op("dve", lambda e, j=j, h0=h0, bs=bs: e.tensor_tensor(out=oaT[:, h0:h0 + 4, bs], in0=v3(OT[j][:], 4), in1=v3(rno[j][:], 4), op=ALU.mult),
                                           reads=[OT[j], rno[j]], writes=[oaT])
                            tg.__exit__(None, None, None)
                        ckpt("g2")
                        with sc.scope() as p4:
                            szs = [p4.sb("sz%d" % i, [128, T], BF16) for i in range(2)]
                            for h in range(16):
                                if h % 2 == 0:
                                    slab = next_slab(l, 25 + h // 2)
                                bank = ps[h % 4]
                                sz = szs[h % 2]
                                mm_group(bank, [(bank[:], slab[:, kc, (h % 2) * 128:(h % 2 + 1) * 128], hT[:, kc, :], kc == 0, kc == KC - 1) for kc in range(KC)], reads=[slab, hT])
                                op("act", lambda e, bank=bank, sz=sz: e.activation(out=sz[:], in_=bank[:], func=AF.Silu), reads=[bank], writes=[sz])
                                op("dve", lambda e, h=h, sz=sz: e.scalar_tensor_tensor(out=oaT[:, h, :], in0=oaT[:, h, :], scalar=gnw[l][:, 0:1], in1=sz[:], op0=ALU.mult, op1=ALU.mult),
                                   reads=[oaT, gnw[l], sz], writes=[oaT])

                    ckpt("p4")
                    with sc.scope() as ws:
                        obT = ws.sb("obT", [128, KC, T], BF16)
                        with sc.scope() as sw:
                            qbT = sw.sb("qbT", [128, 4, T], BF16)
                            szb = sw.sb("szb", [128, 4, T], BF16)
                            kdT = [sw.sb("kdT%d" % i, [128, 128 + T], BF16) for i in range(2)]
                            Vpad = sw.sb("Vpad", [128, 5, 1024], BF16)
                            vp = lambda slot: Vpad[:, slot, :].rearrange("p (g v d) -> p g v d", g=4, v=2)
                            q16 = sw.sb("q16", [128, T], BF16)
                            t1 = sw.sb("t1", [128, T], F32)
                            t2 = sw.sb("t2", [128, T], F32)
                            sset = []
                            for si_ in range(4):
                                sset.append(dict(Pm=sw.sb("Pm", [128, 4, 256], BF16), PTs=sw.sb("PTs", [128, 8, 128], BF16), Dg=sw.sb("Dg", [128, 4, 128], BF16),
                                                 mx=sw.sb("mx", [128, 4], F32), negm=sw.sb("negm", [128, 4], F32), rsum=sw.sb("rsum", [128, 4], F32),
                                                 esk=sw.sb("esk", [128, 4], F32), bk=[ps[2 * si_], ps[2 * si_ + 1]]))
                            for sl_ in range(5):
                                op("pool", lambda e, sl_=sl_: e.memset(Vpad[:, sl_, :], 0.0), writes=[Vpad])

                            ckpt("s0a")
                            def rope_chunk(slab, sub, dst_ap, dst_tn):
                                bank, bank2 = ps[0], ps[1]
                                mm_group(bank, [(bank[:], slab[:, kc, sub * 128:(sub + 1) * 128], hT[:, kc, :], kc == 0, kc == KC - 1) for kc in range(KC)], reads=[slab, hT])
                                op("act", lambda e: e.activation(out=q16[:], in_=bank[:], func=AF.Copy), reads=[bank], writes=[q16])
                                ckpt("s0b")
                                op("pe", mm(bank2[:], C["rotT"][:], q16[:]), reads=[C["rotT"], q16], writes=[bank2])
                                ckpt("s0c")
                                op("dve", lambda e: e.tensor_tensor(out=t1[:], in0=bank[:], in1=cosT[:], op=ALU.mult), reads=[bank, cosT], writes=[t1])
                                op("dve", lambda e: e.tensor_tensor(out=t2[:], in0=bank2[:], in1=sinT[:], op=ALU.mult), reads=[bank2, sinT], writes=[t2])
                                ckpt("s0d")
                                op("pool", lambda e: e.tensor_tensor(out=dst_ap, in0=t1[:], in1=t2[:], op=ALU.add), reads=[t1, t2], writes=[dst_tn])

                            for g in range(4):
                                for a in range(4):
                                    pa_ = 4 * g + a
                                    if pa_ % 2 == 0:
                                        slab = next_slab(l, 33 + pa_ // 2)
                                    rope_chunk(slab, pa_ % 2, qbT[:, a, :], qbT)
                                ckpt("s1")
                                if g % 2 == 0:
                                    slab = next_slab(l, 41 + g // 2)
                                    for gp in range(2):
                                        gg = g + gp
                                        op("pool", lambda e, gp=gp, gg=gg: e.tensor_copy(out=kdT[gp][:, 0:128], in_=kh[l][:, gg, :]), reads=[kh[l]], writes=[kdT[gp]])
                                        rope_chunk(slab, gp, kdT[gp][:, 128:128 + T], kdT[gp])
                                        op("pool", lambda e, gp=gp, gg=gg: e.tensor_copy(out=kh[l][:, gg, :], in_=kdT[gp][:, T:T + 128]), reads=[kdT[gp]], writes=[kh[l]])
                                ckpt("s2")
                                if g == 0:
                                    slab = next_slab(l, 43)
                                    for var in range(2):
                                        op("pool", lambda e, var=var: e.tensor_copy(out=vp(0)[:, :, var, var * 64:(var + 1) * 64], in_=vh[l][:, :, :]), reads=[vh[l]], writes=[Vpad])
                                    for b2 in range(2):
                                        bank = ps[2 + b2]
                                        mm_group(bank, [(bank[:, bb * 256:(bb + 1) * 256], hT[:, kc, (2 * b2 + bb) * 128:(2 * b2 + bb + 1) * 128], slab[:, kc, :], kc == 0, kc == KC - 1)
                                                        for bb in range(2) for kc in range(KC)], reads=[hT, slab])
                                        for bb in range(2):
                                            blk = 2 * b2 + bb
                                            for var in range(2):
                                                op("act", lambda e, bank=bank, bb=bb, blk=blk, var=var: e.activation(
                                                    out=vp(1 + blk)[:, :, var, var * 64:(var + 1) * 64], in_=bank[:, bb * 256:(bb + 1) * 256].rearrange("p (g d) -> p g d", g=4), func=AF.Copy),
                                                   reads=[bank], writes=[Vpad])
                                            if blk == 3:
                                                op("act", lambda e, bank=bank, bb=bb: e.activation(out=vh[l][:, :, :], in_=bank[:, bb * 256:(bb + 1) * 256].rearrange("p (g d) -> p g d", g=4), func=AF.Copy),
                                                   reads=[bank], writes=[vh[l]])
                                ckpt("s3")
                                for a in range(4):
                                    pa_ = 4 * g + a
                                    if pa_ % 2 == 0:
                                        slab = next_slab(l, 44 + pa_ // 2)
                                    bank = ps[4 + a % 2]
                                    mm_group(bank, [(bank[:], slab[:, kc, (pa_ % 2) * 128:(pa_ % 2 + 1) * 128], hT[:, kc, :], kc == 0, kc == KC - 1) for kc in range(KC)], reads=[slab, hT])
                                    op("act", lambda e, bank=bank, a=a: e.activation(out=szb[:, a, :], in_=bank[:], func=AF.Silu), reads=[bank], writes=[szb])
                                ckpt("s4")
                                kd = kdT[g % 2]

                                def swa_stream(g, b, quad, B_):
                                    Pm, PTs, Dg, mx, negm, rsum, esk, bk = (B_[k_] for k_ in ("Pm", "PTs", "Dg", "mx", "negm", "rsum", "esk", "bk"))
                                    mask = C["mB0"] if (t == 0 and b == 0) else C["mB"]
                                    bsl = slice(b * 128, (b + 1) * 128)
                                    s0 = 8 * g + 4 * quad
                                    for i in range(4):
                                        hh = 4 * quad + i
                                        a, par = hh // 2, hh % 2
                                        bank = bk[i // 2]
                                        o_ = bank[:, (i % 2) * 256:(i % 2 + 1) * 256]
                                        mm_group(bank, [(o_, qbT[par * 64:(par + 1) * 64, a, bsl], kd[par * 64:(par + 1) * 64, b * 128:b * 128 + 256], True, False),
                                                        (o_, ident_b[:], mask[:], False, True)], reads=[qbT, kd, ident_b, mask])
                                    yield
                                    for q in range(2):
                                        op("dve", lambda e, q=q: e.tensor_reduce(out=mx[:, 2 * q:2 * q + 2], in_=v3(bk[q][:], 2), axis=AX.X, op=ALU.max), reads=[bk[q]], writes=[mx])
                                    op("dve", lambda e: e.scalar_tensor_tensor(out=negm[:], in0=mx[:], scalar=0.125, in1=sinkb[l][:, s0:s0 + 4], op0=ALU.mult, op1=ALU.max),
                                       reads=[mx, sinkb[l]], writes=[negm])
                                    op("dve", lambda e: e.tensor_scalar(out=negm[:], in0=negm[:], scalar1=-1.0, scalar2=None, op0=ALU.mult), reads=[negm], writes=[negm])
                                    yield
                                    for i in range(4):
                                        op("act", lambda e, i=i: e.activation(out=Pm[:, i, :], in_=bk[i // 2][:, (i % 2) * 256:(i % 2 + 1) * 256], func=AF.Exp,
                                                                            bias=negm[:, i:i + 1], scale=0.125, accum_out=rsum[:, i:i + 1]),
                                           reads=[bk[i // 2], negm], writes=[Pm, rsum])
                                    op("dve", lambda e: e.tensor_tensor(out=esk[:], in0=sinkb[l][:, s0:s0 + 4], in1=negm[:], op=ALU.add), reads=[sinkb[l], negm], writes=[esk])
                                    op("act", lambda e: e.activation(out=esk[:], in_=esk[:], func=AF.Exp), reads=[esk], writes=[esk])
                                    yield
                                    op("dve", lambda e: e.tensor_tensor(out=esk[:], in0=esk[:], in1=rsum[:], op=ALU.add), reads=[esk, rsum], writes=[esk])
                                    op("dve", lambda e: e.reciprocal(out=esk[:], in_=esk[:]), reads=[esk], writes=[esk])
                                    op("dve", lambda e: e.tensor_tensor(out=Dg[:], in0=bc_mid(ident_b[:, :], 4), in1=bc_last(esk[:, 0:4], 128), op=ALU.mult), reads=[ident_b, esk], writes=[Dg])
                                    yield
                                    for q in range(2):
                                        bank = bk[q]
                                        items = []
                                        for i4 in range(4):
                                            idx = q * 4 + i4
                                            i, kb = idx // 2, idx % 2
                                            items.append((bank[:, i4 * 128:(i4 + 1) * 128], Pm[:, i, kb * 128:(kb + 1) * 128], Dg[:, i, :], True, True))
                                        mm_group(bank, items, reads=[Pm, Dg])
                                    yield
                                    op("act", lambda e: e.activation(out=PTs[:, 0:4, :], in_=v3(bk[0][:], 4), func=AF.Copy), reads=[bk[0]], writes=[PTs])
                                    pt2 = Tn(None)
                                    op("dve", lambda e: e.tensor_copy(out=PTs[:, 4:8, :], in_=v3(bk[1][:], 4)), reads=[bk[1], PTs], writes=[pt2])
                                    _merge(PTs.w, pt2.w)
                                    yield
                                    items = []
                                    for al in range(2):
                                        n = 0
                                        for i in (2 * al, 2 * al + 1):
                                            hh = 4 * quad + i
                                            for kb in range(2):
                                                items.append((bk[0][:, al * 128:(al + 1) * 128], Vpad[:, b + kb, (g * 2 + hh % 2) * 128:(g * 2 + hh % 2 + 1) * 128], PTs[:, i * 2 + kb, :], n == 0, n == 3))
                                                n += 1
                                    mm_group(bk[0], items, reads=[Vpad, PTs])
                                    yield
                                    c0 = 4 * g + 2 * quad
                                    op("dve", lambda e: e.tensor_tensor(out=obT[:, c0:c0 + 2, bsl], in0=v3(bk[0][:, 0:256], 2), in1=szb[:, 2 * quad:2 * quad + 2, bsl], op=ALU.mult),
                                       reads=[bk[0], szb], writes=[obT])

                                run_window([(lambda k, g=g, b=b, q=q: swa_stream(g, b, q, sset[k])) for b in range(4) for q in range(2)], 4)
                        ckpt("swa")
                        with sc.scope() as mg:
                            yT = mg.sb("yT", [128, KC, T], BF16)
                            sg = [mg.sb("sg%d" % i, [128, T], F32) for i in range(2)]
                            ya = [mg.sb("ya%d" % i, [128, T], F32) for i in range(2)]
                            yb = [mg.sb("yb%d" % i, [128, T], F32) for i in range(2)]
                            for jj in range(8):
                                for (si_, srcT, which) in ((52, hT, "ga"), (68, oaT, "pa"), (60, hT, "gb"), (76, obT, "pb")):
                                    slab = next_slab(l, si_ + jj)
                                    for i in range(2):
                                        dc = 2 * jj + i
                                        bank = ps[{"ga": 0, "pa": 2, "gb": 4, "pb": 6}[which] + i]
                                        mm_group(bank, [(bank[:], slab[:, kc, i * 128:(i + 1) * 128], srcT[:, kc, :], kc == 0, kc == KC - 1) for kc in range(KC)], reads=[slab, srcT])
                                        if which in ("ga", "gb"):
                                            op("act", lambda e, bank=bank, i=i: e.activation(out=sg[i][:], in_=bank[:], func=AF.Sigmoid), reads=[bank], writes=[sg[i]])
                                        elif which == "pa":
                                            op("dve", lambda e, bank=bank, i=i: e.tensor_tensor(out=ya[i][:], in0=bank[:], in1=sg[i][:], op=ALU.mult), reads=[bank, sg[i]], writes=[ya[i]])
                                        else:
                                            op("dve", lambda e, bank=bank, i=i: e.tensor_tensor(out=yb[i][:], in0=bank[:], in1=sg[i][:], op=ALU.mult), reads=[bank, sg[i]], writes=[yb[i]])
                                            op("pool", lambda e, i=i, dc=dc: e.tensor_tensor(out=yT[:, dc, :], in0=ya[i][:], in1=yb[i][:], op=ALU.add), reads=[ya[i], yb[i]], writes=[yT])
                            ckpt("merge")
                            tw = [mg.sb("tw%d" % i, [128, 512], F32) for i in range(2)]
                            for w in range(8):
                                slab = next_slab(l, 84 + w)
                                for s2 in range(2):
                                    bank = ps[(2 * w + s2) % 8]
                                    mm_group(bank, [(bank[:, ss_ * 256:(ss_ + 1) * 256], yT[:, kc, (2 * s2 + ss_) * 128:(2 * s2 + ss_ + 1) * 128], slab[:, kc, :], kc == 0, kc == KC - 1)
                                                    for ss_ in range(2) for kc in range(KC)], reads=[yT, slab])
                                    tw_ = tw[s2]
                                    op("dve", lambda e, bank=bank, tw_=tw_, w=w: e.tensor_tensor(out=v3(tw_[:], 2), in0=v3(bank[:], 2), in1=bc_mid(bc8k[:, w * 256:(w + 1) * 256], 2), op=ALU.mult),
                                       reads=[bank, bc8k], writes=[tw_])
                                    for ss_ in range(2):
                                        s = 2 * s2 + ss_
                                        op("pool", lambda e, s=s, ss_=ss_, tw_=tw_, w=w: e.tensor_tensor(out=xs_[s][:, w * 256:(w + 1) * 256], in0=xs_[s][:, w * 256:(w + 1) * 256],
                                                                                                     in1=tw_[:, ss_ * 256:(ss_ + 1) * 256], op=ALU.add), reads=[xs_[s], tw_], writes=[xs_[s]])
                    ckpt("L%d" % l)
                ckpt("wout")
                with sc.scope() as fn_:
                    junk = fn_.sb("fjunk", [128, D], BF16)
                    ss = fn_.sb("fss", [128, 1], F32)
                    ob = [fn_.sb("ob%d" % i, [128, D], F32) for i in range(2)]
                    sc.dma("sp", [(bc8k[:, 0:1024], fnw_d[:, 0:1024].partition_broadcast(128)), (bc8k[:, 1024:2048], fnw_d[:, 1024:2048].partition_broadcast(128))], writes=[bc8k])
                    for s in range(4):
                        op("act", lambda e, s=s: e.activation(out=junk[:], in_=xs_[s][:], func=AF.Square, accum_out=ss[:, 0:1]), reads=[xs_[s]], writes=[junk, ss])
                        rstd = rsqrt_small(fn_, ss, 1.0 / D, 1e-6, "fn")
                        o_ = ob[s % 2]
                        op("dve", lambda e, s=s, o_=o_, rstd=rstd: e.scalar_tensor_tensor(out=o_[:], in0=xs_[s][:], scalar=rstd[:, 0:1], in1=bc8k[:], op0=ALU.mult, op1=ALU.mult),
                           reads=[xs_[s], rstd, bc8k], writes=[o_])
                        sc.dma("act", [(out_d[t0 + s * 128:t0 + (s + 1) * 128, 0:1024], o_[:, 0:1024]), (out_d[t0 + s * 128:t0 + (s + 1) * 128, 1024:2048], o_[:, 1024:2048])], reads=[o_])
                        if t + 1 < NT:
                            t1_ = t0 + T
                            sc.dma("act", [(xs_[s][:, 0:1024], x_d[t1_ + s * 128:t1_ + (s + 1) * 128, 0:1024]),
                                           (xs_[s][:, 1024:2048], x_d[t1_ + s * 128:t1_ + (s + 1) * 128, 1024:2048])], writes=[xs_[s]])
        try:
            if STOP != "c0":
                body()
        except StopBuild:
            pass
        sc.finish(xs_ + [bc8k])
        build_nc.ninst = sc.ninst
    return nc


_CACHE = {}


def kernel(**inputs):
    x = np.asarray(inputs["x"], dtype=np.float32)
    B, S, _ = x.shape
    DEPTH = inputs["ada_w"].shape[0]
    key = (S, DEPTH)
    if key not in _CACHE:
        _CACHE[key] = build_nc(S, DEPTH)
    nc = _CACHE[key]
    consts = _consts()
    f = lambda n: np.ascontiguousarray(np.asarray(inputs[n], dtype=np.float32))
    shared = {
        "ada_w": f("ada_w"), "ada_b": f("ada_b").reshape(DEPTH, 1, 3 * D), "norm_w": f("norm_w").reshape(DEPTH, 16, 128),
        "w_in": f("w_in"), "conv_w": f("conv_w").reshape(DEPTH, 192, 128), "gdn_a_log": f("gdn_a_log").reshape(DEPTH, 1, 16),
        "gdn_dt_bias": f("gdn_dt_bias").reshape(DEPTH, 1, 16), "gdn_norm_w": f("gdn_norm_w").reshape(DEPTH, 128, 1),
        "swa_sinks": f("swa_sinks").reshape(DEPTH, 1, 32), "proj_a": f("proj_a"), "proj_b": f("proj_b"), "w_out": f("w_out"),
        "final_norm_w": f("final_norm_w").reshape(1, D),
    }
    shared.update(consts)
    pos = np.asarray(inputs["positions"]).astype(np.int32)
    cc = f("c")
    in_maps = []
    for b in range(B):
        m = dict(shared)
        m["x"] = np.ascontiguousarray(x[b])
        m["c"] = np.ascontiguousarray(cc[b].reshape(16, 128))
        m["positions"] = np.ascontiguousarray(pos[b].reshape(1, S))
        in_maps.append(m)
    import os as _os
    _off = int(_os.environ.get("KCORE", "0"))
    res = run_bass_kernel_spmd(nc, in_maps, core_ids=[_off + i for i in range(B)])
    return np.stack([np.asarray(r["out"], dtype=np.float32) for r in res.results], axis=0)
```

```python
import math
from contextlib import ExitStack

import numpy as np
import ml_dtypes

import concourse.bass as bass
import concourse.mybir as mybir
from concourse.bass_utils import run_bass_kernel_spmd

F32 = mybir.dt.float32
BF16 = mybir.dt.bfloat16
I32 = mybir.dt.int32
AF = mybir.ActivationFunctionType
ALU = mybir.AluOpType
AX = mybir.AxisListType

D = 2048
KC = 16
T = 512
NEG = -30000.0
IN_COLS = 16928
NSLAB = 92
SW = 256
RING = 3
NDS = 16
TWO_PI = 2.0 * math.pi


def _merge(dst, src):
    for k, (s, v) in src.items():
        if k not in dst or dst[k][1] < v:
            dst[k] = (s, v)


class Tn:
    def __init__(self, t, inherit=None, excl=False):
        self.t = t
        self.excl = excl
        self.w = dict(inherit) if inherit else {}
        self.r = {}

    def __getitem__(self, k):
        return self.t[k]


class Scope:
    def __init__(self, sched):
        self.s = sched
        self.es = ExitStack()
        self.tns = []

    def __enter__(self):
        self.es.__enter__()
        return self

    def sb(self, name, shape, dt):
        t = self.es.enter_context(self.s.nc.sbuf_tensor(self.s.uname(name), list(shape), dt))
        tn = Tn(t, self.s.residue)
        self.tns.append(tn)
        return tn

    def __exit__(self, *a):
        for tn in self.tns:
            _merge(self.s.residue, tn.w)
            _merge(self.s.residue, tn.r)
        return self.es.__exit__(*a)


class Sched:
    def __init__(self, nc, es):
        self.nc = nc
        self.eng = {"pe": nc.tensor, "act": nc.scalar, "dve": nc.vector, "pool": nc.gpsimd, "sp": nc.sync}
        self.sem = {k: es.enter_context(nc.semaphore("sem_" + k)) for k in ("pe", "act", "dve", "pool")}
        self.cnt = {k: 0 for k in self.sem}
        self.known = {k: {} for k in self.eng}
        self.dsem = [es.enter_context(nc.semaphore("dsem%d" % i)) for i in range(NDS)]
        self.dcnt = [0] * NDS
        self.dnext = 0
        self.residue = {}
        self._n = 0
        self.ninst = 0
        self.stopped = False
        self.hist = {}
        CUR[0] = self

    def uname(self, n):
        self._n += 1
        if not self.stopped:
            NAMES[n] = "%s_%d" % (n, self._n)
        return "%s_%d" % (n, self._n)

    def scope(self):
        return Scope(self)

    def _need(self, reads, writes):
        need = {}
        for b in reads:
            _merge(need, b.w)
            if b.excl:
                _merge(need, b.r)
        for b in writes:
            _merge(need, b.w)
            _merge(need, b.r)
        return need

    def _wait(self, e, need):
        if self.stopped:
            return
        kn = self.known[e]
        for k, (s, v) in sorted(need.items(), key=lambda kv: -kv[1][1]):
            if kn.get(k, 0) < v:
                self.eng[e].wait_ge(s, v)
                kn[k] = v
                self.ninst += 1
                h = self.hist.get((k, v))
                if h:
                    for k2, v2 in h.items():
                        if kn.get(k2, 0) < v2:
                            kn[k2] = v2

    def _commit(self, tok, reads, writes):
        for b in writes:
            b.w = dict(tok)
            b.r = {}
        for b in reads:
            _merge(b.r, tok)

    def op(self, e, fn, reads=(), writes=()):
        if self.stopped:
            return
        need = self._need(reads, writes)
        if e == "pe":
            need.pop("pe", None)
        self._wait(e, need)
        inst = fn(self.eng[e])
        self.cnt[e] += 1
        inst.then_inc(self.sem[e], 1)
        self.ninst += 1
        self.hist[(e, self.cnt[e])] = dict(self.known[e])
        self._commit({e: (self.sem[e], self.cnt[e])}, reads, writes)

    def par(self, items, reads=(), writes=()):
        if self.stopped:
            return
        need = self._need(reads, writes)
        tok = {}
        for (e, fn) in items:
            nd = dict(need)
            if e == "pe":
                nd.pop("pe", None)
            self._wait(e, nd)
            inst = fn(self.eng[e])
            self.cnt[e] += 1
            inst.then_inc(self.sem[e], 1)
            self.ninst += 1
            self.hist[(e, self.cnt[e])] = dict(self.known[e])
            tok[e] = (self.sem[e], self.cnt[e])
        self._commit(tok, reads, writes)

    def dma(self, q, pairs, reads=(), writes=()):
        if self.stopped:
            return
        i = self.dnext
        self.dnext = (i + 1) % NDS
        need = self._need(reads, writes)
        if self.dcnt[i] > 0:
            need["d%d" % i] = (self.dsem[i], self.dcnt[i])
        self._wait(q, need)
        for (o, a) in pairs:
            self.eng[q].dma_start(out=o, in_=a).then_inc(self.dsem[i], 16)
            self.dcnt[i] += 16
            self.ninst += 1
        self.hist[("d%d" % i, self.dcnt[i])] = dict(self.known[q])
        self._commit({"d%d" % i: (self.dsem[i], self.dcnt[i])}, reads, writes)

    def finish(self, tns):
        need = {}
        for b in tns:
            _merge(need, b.w)
            _merge(need, b.r)
        _merge(need, self.residue)
        self.stopped = False
        for e in ("sp", "act", "pool", "dve", "pe"):
            self._wait(e, dict(need))


def _consts():
    idx = np.arange(128)
    ch = idx // 64
    same = ch[:, None] == ch[None, :]
    c = {}
    c["ident_f"] = np.eye(128, dtype=np.float32)
    c["ones_f"] = np.ones((128, 128), np.float32)
    mS = np.where(same & (idx[:, None] > idx[None, :]), 0.0, NEG)
    mST = np.where(same & (idx[None, :] > idx[:, None]), 0.0, NEG)
    mIT = np.where(same & (idx[None, :] >= idx[:, None]), 0.0, NEG)
    c["tri"] = (same & (idx[:, None] <= idx[None, :])).astype(np.float32)
    lsel = np.zeros((128, 128), np.float32)
    for m in range(128):
        lsel[64 * (m // 64) + 63, m] = 1.0
    c["lsel"] = lsel
    l0 = np.zeros((128, 128), np.float32); l0[63, :] = 1.0
    l1 = np.zeros((128, 128), np.float32); l1[127, :] = 1.0
    c["l0"] = l0
    c["l1"] = l1
    c["invf"] = (10000.0 ** (-(np.arange(128) % 32).astype(np.float64) / 32.0)).astype(np.float32)[:, None]
    bf = {}
    bf["ident_b"] = np.eye(128)
    bf["ident4_b"] = np.tile(np.eye(128), (1, 4))
    bf["ones_b"] = np.ones((128, 128))
    bf["mS4"] = np.tile(mS, (1, 4))
    bf["mST4"] = np.tile(mST, (1, 4))
    bf["mIT4"] = np.tile(mIT, (1, 4))
    left = np.where(idx[None, :] > idx[:, None], 0.0, NEG)
    right = np.where(idx[None, :] <= idx[:, None], 0.0, NEG)
    bf["mB"] = np.concatenate([left, right], 1)
    bf["mB0"] = np.concatenate([np.full((128, 128), NEG), right], 1)
    for lv in range(4):
        a = 8 * 2 ** lv
        if lv == 0:
            bf["hm0"] = (idx[:, None] // 8 == idx[None, :] // 8).astype(np.float64)
        else:
            bf["hm%d" % lv] = ((idx[:, None] // a == idx[None, :] // a) & (idx[:, None] // (a // 2) != idx[None, :] // (a // 2))).astype(np.float64)
    rot = np.zeros((128, 128))
    for m in range(128):
        if m % 64 < 32:
            rot[m + 32, m] = -1.0
        else:
            rot[m - 32, m] = 1.0
    bf["rotT"] = rot
    for k, v in bf.items():
        c[k] = v.astype(ml_dtypes.bfloat16)
    return c


CONST_F32 = ["ident_f", "ones_f", "tri", "lsel", "l0", "l1", "invf"]
CONST_BF = ["ident_b", "ident4_b", "ones_b", "mS4", "mST4", "mIT4", "mB", "mB0", "rotT", "hm0", "hm1", "hm2", "hm3"]


def _slab_pieces():
    sl = []
    for i in range(24):
        sl.append([("w_in", i * SW, SW, 0)])
    sl.append([("w_in", 6144, 32, 0)])
    for i in range(8):
        sl.append([("w_in", 6176 + i * SW, SW, 0)])
    for i in range(8):
        sl.append([("w_in", 8224 + i * SW, SW, 0)])
    kb = 10272
    sl.append([("w_in", kb, 64, 0), ("w_in", kb, 64, 64), ("w_in", kb + 64, 64, 128), ("w_in", kb + 64, 64, 192)])
    sl.append([("w_in", kb + 128, 64, 0), ("w_in", kb + 128, 64, 64), ("w_in", kb + 192, 64, 128), ("w_in", kb + 192, 64, 192)])
    sl.append([("w_in", 10528, SW, 0)])
    for i in range(8):
        sl.append([("w_in", 10784 + i * SW, SW, 0)])
    for i in range(8):
        sl.append([("w_in", 12832 + i * SW, SW, 0)])
    for i in range(8):
        sl.append([("w_in", 14880 + i * SW, SW, 0)])
    for nm in ("proj_a", "proj_b", "w_out"):
        for i in range(8):
            sl.append([(nm, i * SW, SW, 0)])
    assert len(sl) == NSLAB
    return sl


def _layer_order():
    o = [24]
    for hf in range(2):
        o += [hf * 4 + i for i in range(4)] + [8 + hf * 4 + i for i in range(4)] + [16 + hf * 4 + i for i in range(4)]
    o += list(range(25, 33))
    for g in range(4):
        o += [33 + 2 * g, 34 + 2 * g]
        if g % 2 == 0:
            o += [41 + g // 2]
        if g == 0:
            o += [43]
        o += [44 + 2 * g, 45 + 2 * g]
    for j in range(8):
        o += [52 + j, 68 + j, 60 + j, 76 + j]
    o += list(range(84, 92))
    return o


STOP = None


class StopBuild(Exception):
    pass


CUR = [None]
NAMES = {}


def ckpt(name):
    if STOP == name:
        CUR[0].stopped = True


def build_nc(S, DEPTH=2):
    NT = S // T
    nc = bass.Bass("TRN2", target_bir_lowering=False)
    dram = {}

    def din(name, shape, dt=F32):
        dram[name] = nc.dram_tensor(name, list(shape), dt, kind="ExternalInput").ap()
        return dram[name]

    x_d = din("x", [S, D])
    c_d = din("c", [16, 128])
    pos_d = din("positions", [1, S], I32)
    adaw_d = din("ada_w", [DEPTH, D, 3 * D])
    adab_d = din("ada_b", [DEPTH, 1, 3 * D])
    normw_d = din("norm_w", [DEPTH, 16, 128])
    win_d = din("w_in", [DEPTH, D, IN_COLS])
    convw_d = din("conv_w", [DEPTH, 192, 128])
    alog_d = din("gdn_a_log", [DEPTH, 1, 16])
    dtb_d = din("gdn_dt_bias", [DEPTH, 1, 16])
    gnw_d = din("gdn_norm_w", [DEPTH, 128, 1])
    sink_d = din("swa_sinks", [DEPTH, 1, 32])
    pa_d = din("proj_a", [DEPTH, D, D])
    pb_d = din("proj_b", [DEPTH, D, D])
    wo_d = din("w_out", [DEPTH, D, D])
    fnw_d = din("final_norm_w", [1, D])
    for n in CONST_F32:
        din(n, [128, 1] if n == "invf" else [128, 128])
    for n in CONST_BF:
        w = {"ident4_b": 512, "mS4": 512, "mST4": 512, "mIT4": 512, "mB": 256, "mB0": 256}.get(n, 128)
        din(n, [128, w], BF16)
    out_d = nc.dram_tensor("out", [S, D], F32, kind="ExternalOutput").ap()
    wb_d = nc.dram_tensor("wb", [DEPTH, NSLAB, 128, KC * SW], BF16, kind="Internal").ap()
    gate_d = nc.dram_tensor("gate_s", [DEPTH, 1, D], F32, kind="Internal").ap()
    wsrc = {"w_in": win_d, "proj_a": pa_d, "proj_b": pb_d, "w_out": wo_d}
    pieces = _slab_pieces()
    order = []
    for t in range(NT):
        for l in range(DEPTH):
            order += [(l, s) for s in _layer_order()]

    with ExitStack() as es:
        sc = Sched(nc, es)
        op = sc.op

        def P(name, shape, dt):
            return Tn(es.enter_context(nc.sbuf_tensor(name, list(shape), dt)))

        ps = [Tn(es.enter_context(nc.psum_tensor("psb%d" % i, [128, 512], F32)), excl=True) for i in range(8)]
        C = {}
        for n in CONST_F32 + CONST_BF:
            shp = list(dram[n].shape)
            C[n] = P("c_" + n, shp, BF16 if n in CONST_BF else F32)
            sc.dma("sp", [(C[n][:], dram[n][:, :])], writes=[C[n]])
        ident_f, ones_f, ident_b, ones_b = C["ident_f"], C["ones_f"], C["ident_b"], C["ones_b"]
        one11 = ones_f
        xs_ = [P("x%d" % s, [128, D], F32) for s in range(4)]
        hT = P("hT", [128, KC, T], BF16)
        oaT = P("oaT", [128, KC, T], BF16)
        ring = [P("ring%d" % i, [128, KC, SW], BF16) for i in range(RING)]
        Sst = [[P("S%d_%d" % (l, j), [128, 512], F32) for j in range(4)] for l in range(DEPTH)]
        bc8k = P("bc8k", [128, D], F32)
        cs = [P("cs%d" % l, [128, 48, 3], F32) for l in range(DEPTH)]
        kh = [P("kh%d" % l, [128, 4, 128], BF16) for l in range(DEPTH)]
        vh = [P("vh%d" % l, [128, 4, 64], BF16) for l in range(DEPTH)]
        AT = [P("AT%d" % l, [128, 16], F32) for l in range(DEPTH)]
        BT = [P("BT%d" % l, [128, 16], F32) for l in range(DEPTH)]
        cw = [P("cw%d" % l, [128, 4, 48], F32) for l in range(DEPTH)]
        negA = [P("negA%d" % l, [128, 16], F32) for l in range(DEPTH)]
        dtb = [P("dtb%d" % l, [128, 16], F32) for l in range(DEPTH)]
        gnw = [P("gnw%d" % l, [128, 1], F32) for l in range(DEPTH)]
        sinkb = [P("sink%d" % l, [128, 32], F32) for l in range(DEPTH)]
        cosT = P("cosT", [128, T], F32)
        sinT = P("sinT", [128, T], F32)
        for l in range(DEPTH):
            for j in range(4):
                op("pool", lambda e, a=Sst[l][j]: e.memset(a[:], 0.0), writes=[Sst[l][j]])
            op("pool", lambda e, a=cs[l]: e.memset(a[:], 0.0), writes=[cs[l]])
            op("pool", lambda e, a=kh[l]: e.memset(a[:], 0.0), writes=[kh[l]])
            op("pool", lambda e, a=vh[l]: e.memset(a[:], 0.0), writes=[vh[l]])
            sc.dma("sp", [(dtb[l][:], dtb_d[l].partition_broadcast(128))], writes=[dtb[l]])
            sc.dma("sp", [(negA[l][:], alog_d[l].partition_broadcast(128))], writes=[negA[l]])
            sc.dma("sp", [(sinkb[l][:], sink_d[l].partition_broadcast(128))], writes=[sinkb[l]])
            sc.dma("sp", [(gnw[l][:], gnw_d[l])], writes=[gnw[l]])
            op("act", lambda e, a=negA[l]: e.activation(out=a[:], in_=a[:], func=AF.Exp), reads=[negA[l]], writes=[negA[l]])
            op("dve", lambda e, a=negA[l]: e.tensor_scalar(out=a[:], in0=a[:], scalar1=-1.0, scalar2=None, op0=ALU.mult),
               reads=[negA[l]], writes=[negA[l]])

        def mm(out, lhsT, rhs, start=True, stop=True):
            return lambda e: e.matmul(out, lhsT=lhsT, rhs=rhs, start=start, stop=stop, skip_group_check=True)

        def mm_group(bank, items, reads):
            def fn(e):
                inst = None
                for (o, l_, r_, st, sp) in items:
                    inst = e.matmul(o, lhsT=l_, rhs=r_, start=st, stop=sp, skip_group_check=True)
                    sc.ninst += 1
                return inst
            op("pe", fn, reads=reads, writes=[bank])

        def run_window(factories, nslots):
            pending = list(factories)
            slots = [None] * nslots
            while pending or any(x is not None for x in slots):
                for k in range(nslots):
                    if slots[k] is None and pending:
                        slots[k] = pending.pop(0)(k)
                    if slots[k] is not None:
                        try:
                            next(slots[k])
                        except StopIteration:
                            slots[k] = None

        def bc_last(ap2, n):
            return ap2.unsqueeze(2).broadcast_to([128, ap2.shape[1], n])

        def bc_mid(ap2, k):
            return ap2.unsqueeze(1).broadcast_to([128, k, ap2.shape[1]])

        def v3(ap, k):
            return ap.rearrange("p (k n) -> p k n", k=k)

        def body():
            with sc.scope() as pre:
                crow = pre.sb("crow", [16, 128], F32)
                cact = pre.sb("cact", [128, 16], F32)
                tmp16 = pre.sb("tmp16", [128, 16], F32)
                sc.dma("sp", [(crow[:], c_d[:, :])], writes=[crow])
                op("pe", mm(ps[0][:, 0:16], crow[:], ident_f[0:16, 0:16]), reads=[crow, ident_f], writes=[ps[0]])
                op("act", lambda e: e.activation(out=tmp16[:], in_=ps[0][:, 0:16], func=AF.Exp, scale=-1.0), reads=[ps[0]], writes=[tmp16])
                op("dve", lambda e: e.tensor_scalar(out=tmp16[:], in0=tmp16[:], scalar1=1.0, scalar2=None, op0=ALU.add), reads=[tmp16], writes=[tmp16])
                op("dve", lambda e: e.reciprocal(out=tmp16[:], in_=tmp16[:]), reads=[tmp16], writes=[tmp16])
                op("dve", lambda e: e.tensor_tensor(out=cact[:], in0=ps[0][:, 0:16], in1=tmp16[:], op=ALU.mult), reads=[ps[0], tmp16], writes=[cact])
                ckpt("c1")
                row = pre.sb("row", [1, D], F32)
                brow = pre.sb("brow", [1, D], F32)
                wst = [pre.sb("wst%d" % i, [128, D], F32) for i in range(2)]
                nwrow = pre.sb("nwrow", [16, 128], F32)
                cwrow = [pre.sb("cwrow%d" % i, [96, 128], F32) for i in range(2)]
                modT = pre.sb("modT", [128, 32], F32)
                nwT = pre.sb("nwT", [128, 16], F32)
                wi = 0
                for l in range(DEPTH):
                    for third in range(3):
                        sc.dma("sp", [(brow[:], adab_d[l, :, third * D:(third + 1) * D])], writes=[brow])
                        for kc in range(KC):
                            w = wst[wi % 2]
                            wi += 1
                            sc.dma("sp", [(w[:, 0:1024], adaw_d[l, kc * 128:(kc + 1) * 128, third * D:third * D + 1024]),
                                          (w[:, 1024:2048], adaw_d[l, kc * 128:(kc + 1) * 128, third * D + 1024:third * D + 2048])], writes=[w])
                            for j in range(4):
                                op("pe", mm(ps[j][0:1, :], cact[:, kc:kc + 1], w[:, j * 512:(j + 1) * 512], start=(kc == 0), stop=(kc == KC - 1)),
                                   reads=[cact, w], writes=[ps[j]])
                        for j in range(4):
                            c0 = j * 512
                            op("dve", lambda e, j=j, c0=c0: e.tensor_tensor(out=row[0:1, c0:c0 + 512], in0=ps[j][0:1, :], in1=brow[0:1, c0:c0 + 512], op=ALU.add),
                               reads=[ps[j], brow], writes=[row])
                        if third == 2:
                            sc.dma("sp", [(gate_d[l], row[0:1, :])], reads=[row])
                        else:
                            for cidx in range(16):
                                op("pe", mm(ps[6][:, third * 16 + cidx:third * 16 + cidx + 1], row[0:1, cidx * 128:(cidx + 1) * 128], one11[0:1, 0:1]),
                                   reads=[row, ones_f], writes=[ps[6]])
                    ckpt("c2")
                    op("act", lambda e: e.activation(out=modT[:], in_=ps[6][:, 0:32], func=AF.Copy), reads=[ps[6]], writes=[modT])
                    sc.dma("sp", [(nwrow[:], normw_d[l])], writes=[nwrow])
                    op("pe", mm(ps[7][:, 0:16], nwrow[:], ident_f[0:16, 0:16]), reads=[nwrow, ident_f], writes=[ps[7]])
                    op("act", lambda e: e.activation(out=nwT[:], in_=ps[7][:, 0:16], func=AF.Copy), reads=[ps[7]], writes=[nwT])
                    op("dve", lambda e, l=l: e.scalar_tensor_tensor(out=AT[l][:], in0=modT[:, 16:32], scalar=1.0, in1=nwT[:], op0=ALU.add, op1=ALU.mult),
                       reads=[modT, nwT], writes=[AT[l]])
                    op("dve", lambda e, l=l: e.tensor_copy(out=BT[l][:], in_=modT[:, 0:16]), reads=[modT], writes=[BT[l]])
                    ckpt("c3")
                    for i in range(2):
                        sc.dma("sp", [(cwrow[i][:], convw_d[l, i * 96:(i + 1) * 96, :])], writes=[cwrow[i]])
                        op("pe", mm(ps[7][:, 32 + i * 96:32 + (i + 1) * 96], cwrow[i][:], ident_f[0:96, 0:96]), reads=[cwrow[i], ident_f], writes=[ps[7]])
                    op("act", lambda e, l=l: e.activation(out=cw[l][:].rearrange("p a b -> p (a b)"), in_=ps[7][:, 32:32 + 192], func=AF.Copy),
                       reads=[ps[7]], writes=[cw[l]])
            ckpt("pre1")
            with sc.scope() as pre2:
                stf = [pre2.sb("stf%d" % i, [128, KC, SW], F32) for i in range(3)]
                stb = [pre2.sb("stb%d" % i, [128, KC * SW], BF16) for i in range(2)]
                jobs = [(l, s) for l in range(DEPTH) for s in range(NSLAB)]

                def emit_load(i):
                    l, s = jobs[i]
                    f = stf[i % 3]
                    pairs = []
                    for (nm, c0, ncol, dc) in pieces[s]:
                        src = wsrc[nm][l, :, c0:c0 + ncol].rearrange("(k p) c -> p k c", p=128)
                        for q in range(4):
                            pairs.append((f[:, q * 4:(q + 1) * 4, dc:dc + ncol], src[:, q * 4:(q + 1) * 4, :]))
                    wcols = max(dc + ncol for (_, _, ncol, dc) in pieces[s])
                    if wcols < SW:
                        op("pool", lambda e, f=f: e.memset(f[:], 0.0), writes=[f])
                    sc.dma("sp", pairs, writes=[f])

                emit_load(0)
                emit_load(1)
                for i, (l, s) in enumerate(jobs):
                    if i + 2 < len(jobs):
                        emit_load(i + 2)
                    f = stf[i % 3]
                    b = stb[i % 2]
                    ff = f[:].rearrange("p k c -> p (k c)")
                    n = KC * SW
                    a1, a2 = 2000, 3100
                    sc.par([("act", lambda e, b=b, ff=ff: e.activation(out=b[:, 0:a1], in_=ff[:, 0:a1], func=AF.Copy)),
                            ("dve", lambda e, b=b, ff=ff: e.tensor_copy(out=b[:, a1:a2], in_=ff[:, a1:a2])),
                            ("pool", lambda e, b=b, ff=ff: e.tensor_copy(out=b[:, a2:n], in_=ff[:, a2:n]))], reads=[f], writes=[b])
                    sc.dma("sp", [(wb_d[l, s], b[:])], reads=[b])
                    _merge(sc.residue, b.r)
            ckpt("pre2")
            sc._wait("sp", dict(sc.residue))

            wstate = {"i": 0, "issued": 0}

            def issue_slab():
                k = wstate["issued"]
                if k >= len(order):
                    return
                l_, s_ = order[k]
                slot = ring[k % RING]
                half = KC * SW // 2
                dst = slot[:].rearrange("p k c -> p (k c)")
                sc.dma("sp", [(dst[:, 0:half], wb_d[l_, s_, :, 0:half]), (dst[:, half:2 * half], wb_d[l_, s_, :, half:2 * half])], writes=[slot])
                wstate["issued"] += 1

            for _ in range(RING - 1):
                issue_slab()

            def next_slab(l_, s_):
                i = wstate["i"]
                assert order[i] == (l_, s_), (order[i], l_, s_)
                issue_slab()
                wstate["i"] += 1
                return ring[i % RING]

            def rsqrt_small(sco, ss, scale, eps, name):
                lnv = sco.sb(name + "ln", [128, 1], F32)
                r = sco.sb(name + "r", [128, 1], F32)
                op("dve", lambda e: e.tensor_scalar(out=lnv[:], in0=ss[:], scalar1=scale, scalar2=eps, op0=ALU.mult, op1=ALU.add), reads=[ss], writes=[lnv])
                op("act", lambda e: e.activation(out=lnv[:], in_=lnv[:], func=AF.Ln), reads=[lnv], writes=[lnv])
                op("act", lambda e: e.activation(out=r[:], in_=lnv[:], func=AF.Exp, scale=-0.5), reads=[lnv], writes=[r])
                return r

            for t in range(NT):
                t0 = t * T
                for s in range(4):
                    sc.dma("act", [(xs_[s][:, 0:1024], x_d[t0 + s * 128:t0 + (s + 1) * 128, 0:1024]),
                                   (xs_[s][:, 1024:2048], x_d[t0 + s * 128:t0 + (s + 1) * 128, 1024:2048])], writes=[xs_[s]])
                def emit_rope():
                    with sc.scope() as rs_:
                        pib = rs_.sb("pib", [128, T], I32)
                        ang = rs_.sb("ang", [128, T], F32)
                        u = rs_.sb("u", [128, T], F32)
                        ni = rs_.sb("ni", [128, T], I32)
                        m_ = rs_.sb("m_", [128, T], F32)
                        sc.dma("sp", [(pib[:], pos_d[0:1, t0:t0 + T].partition_broadcast(128))], writes=[pib])
                        op("dve", lambda e: e.tensor_copy(out=ang[:], in_=pib[:]), reads=[pib], writes=[ang])
                        op("dve", lambda e: e.tensor_scalar(out=ang[:], in0=ang[:], scalar1=C["invf"][:, 0:1], scalar2=None, op0=ALU.mult),
                           reads=[ang, C["invf"]], writes=[ang])
                        for (dst, off) in ((sinT, 0.0), (cosT, math.pi / 2)):
                            op("dve", lambda e, off=off: e.tensor_scalar(out=u[:], in0=ang[:], scalar1=off, scalar2=1.0 / TWO_PI, op0=ALU.add, op1=ALU.mult),
                               reads=[ang], writes=[u])
                            op("dve", lambda e: e.tensor_copy(out=ni[:], in_=u[:]), reads=[u], writes=[ni])
                            op("dve", lambda e: e.tensor_copy(out=u[:], in_=ni[:]), reads=[ni], writes=[u])
                            op("dve", lambda e: e.scalar_tensor_tensor(out=u[:], in0=u[:], scalar=-TWO_PI, in1=ang[:], op0=ALU.mult, op1=ALU.add),
                               reads=[u, ang], writes=[u])
                            if off != 0.0:
                                op("dve", lambda e, off=off: e.tensor_scalar(out=u[:], in0=u[:], scalar1=off, scalar2=None, op0=ALU.add), reads=[u], writes=[u])
                            op("dve", lambda e: e.tensor_scalar(out=m_[:], in0=u[:], scalar1=math.pi, scalar2=-TWO_PI, op0=ALU.is_gt, op1=ALU.mult), reads=[u], writes=[m_])
                            op("dve", lambda e: e.tensor_tensor(out=u[:], in0=u[:], in1=m_[:], op=ALU.add), reads=[u, m_], writes=[u])
                            op("dve", lambda e: e.tensor_scalar(out=m_[:], in0=u[:], scalar1=-math.pi, scalar2=TWO_PI, op0=ALU.is_lt, op1=ALU.mult), reads=[u], writes=[m_])
                            op("dve", lambda e: e.tensor_tensor(out=u[:], in0=u[:], in1=m_[:], op=ALU.add), reads=[u, m_], writes=[u])
                            op("act", lambda e, dst=dst: e.activation(out=dst[:], in_=u[:], func=AF.Sin), reads=[u], writes=[dst])

                for l in range(DEPTH):
                    sc.dma("sp", [(bc8k[:, 0:1024], gate_d[l, :, 0:1024].partition_broadcast(128)),
                                  (bc8k[:, 1024:2048], gate_d[l, :, 1024:2048].partition_broadcast(128))], writes=[bc8k])
                    with sc.scope() as p0:
                        junk = p0.sb("junk", [128, D], BF16)
                        ss = p0.sb("ss", [128, 1], F32)
                        dg = p0.sb("dg", [128, 128], F32)
                        tmpm = [p0.sb("tmpm%d" % i, [128, 512], F32) for i in range(2)]
                        for s in range(4):
                            op("act", lambda e, s=s: e.activation(out=junk[:], in_=xs_[s][:], func=AF.Square, accum_out=ss[:, 0:1]), reads=[xs_[s]], writes=[junk, ss])
                            rstd = rsqrt_small(p0, ss, 1.0 / D, 1e-6, "p0")
                            op("dve", lambda e: e.tensor_scalar(out=dg[:], in0=ident_f[:], scalar1=rstd[:, 0:1], scalar2=None, op0=ALU.mult),
                               reads=[ident_f, rstd], writes=[dg])
                            for q4 in range(4):
                                mm_group(ps[q4], [(ps[q4][:, i * 128:(i + 1) * 128], xs_[s][:, (q4 * 4 + i) * 128:(q4 * 4 + i + 1) * 128], dg[:], True, True)
                                                  for i in range(4)], reads=[xs_[s], dg])
                                tm = tmpm[q4 % 2]
                                op("dve", lambda e, q4=q4, tm=tm: e.tensor_tensor(out=v3(tm[:], 4), in0=v3(ps[q4][:], 4), in1=bc_last(AT[l][:, q4 * 4:q4 * 4 + 4], 128), op=ALU.mult),
                                   reads=[ps[q4], AT[l]], writes=[tm])
                                op("pool", lambda e, q4=q4, tm=tm, s=s: e.tensor_tensor(out=hT[:, q4 * 4:q4 * 4 + 4, s * 128:(s + 1) * 128], in0=v3(tm[:], 4),
                                                                                      in1=bc_last(BT[l][:, q4 * 4:q4 * 4 + 4], 128), op=ALU.add),
                                   reads=[tm, BT[l]], writes=[hT])

                    ckpt("p0")
                    if l == 0:
                        emit_rope()
                        ckpt("rope")
                    with sc.scope() as gs:
                        sm = {n: gs.sb("sm_" + n, [128, 4, 16], F32) for n in ("bet", "q1", "gc", "ngc", "kdf", "ngam", "dec0", "dec1", "g", "tmp")}
                        slab = next_slab(l, 24)
                        mm_group(ps[0], [(ps[0][:, b * 32:(b + 1) * 32], hT[:, kc, b * 128:(b + 1) * 128], slab[:, kc, 0:32], kc == 0, kc == KC - 1)
                                         for b in range(4) for kc in range(KC)], reads=[hT, slab])
                        pv = v3(ps[0][:, 0:128], 4)
                        op("act", lambda e: e.activation(out=sm["tmp"][:], in_=pv[:, :, 0:16], func=AF.Exp, scale=-1.0), reads=[ps[0]], writes=[sm["tmp"]])
                        op("dve", lambda e: e.tensor_scalar(out=sm["tmp"][:], in0=sm["tmp"][:], scalar1=1.0, scalar2=None, op0=ALU.add), reads=[sm["tmp"]], writes=[sm["tmp"]])
                        op("dve", lambda e: e.reciprocal(out=sm["bet"][:], in_=sm["tmp"][:]), reads=[sm["tmp"]], writes=[sm["bet"]])
                        op("act", lambda e: e.activation(out=sm["q1"][:], in_=sm["bet"][:], func=AF.Ln), reads=[sm["bet"]], writes=[sm["q1"]])
                        op("dve", lambda e: e.tensor_tensor(out=sm["g"][:], in0=pv[:, :, 16:32], in1=bc_mid(dtb[l][:, :], 4), op=ALU.add), reads=[ps[0], dtb[l]], writes=[sm["g"]])
                        op("act", lambda e: e.activation(out=sm["g"][:], in_=sm["g"][:], func=AF.Exp), reads=[sm["g"]], writes=[sm["g"]])
                        op("act", lambda e: e.activation(out=sm["g"][:], in_=sm["g"][:], func=AF.Ln, bias=1.0), reads=[sm["g"]], writes=[sm["g"]])
                        op("dve", lambda e: e.tensor_tensor(out=sm["g"][:], in0=sm["g"][:], in1=bc_mid(negA[l][:, :], 4), op=ALU.mult), reads=[sm["g"], negA[l]], writes=[sm["g"]])
                        flat = lambda n: sm[n][:].rearrange("p a b -> p (a b)")
                        op("pe", mm(ps[1][:, 0:64], C["tri"][:], flat("g")), reads=[C["tri"], sm["g"]], writes=[ps[1]])
                        op("dve", lambda e: e.tensor_copy(out=flat("gc"), in_=ps[1][:, 0:64]), reads=[ps[1]], writes=[sm["gc"]])
                        op("dve", lambda e: e.tensor_scalar(out=flat("ngc"), in0=ps[1][:, 0:64], scalar1=-1.0, scalar2=None, op0=ALU.mult), reads=[ps[1]], writes=[sm["ngc"]])
                        op("dve", lambda e: e.tensor_tensor(out=flat("q1"), in0=flat("q1"), in1=flat("ngc"), op=ALU.add), reads=[sm["q1"], sm["ngc"]], writes=[sm["q1"]])
                        op("act", lambda e: e.activation(out=flat("ngam"), in_=flat("gc"), func=AF.Exp), reads=[sm["gc"]], writes=[sm["ngam"]])
                        op("dve", lambda e: e.tensor_scalar(out=flat("ngam"), in0=flat("ngam"), scalar1=-1.0, scalar2=None, op0=ALU.mult), reads=[sm["ngam"]], writes=[sm["ngam"]])
                        op("pe", mm(ps[2][:, 0:64], C["lsel"][:], flat("gc")), reads=[C["lsel"], sm["gc"]], writes=[ps[2]])
                        op("pe", mm(ps[2][:, 64:128], C["l0"][:], flat("gc")), reads=[C["l0"], sm["gc"]], writes=[ps[2]])
                        op("pe", mm(ps[2][:, 128:192], C["l1"][:], flat("gc")), reads=[C["l1"], sm["gc"]], writes=[ps[2]])
                        op("dve", lambda e: e.tensor_tensor(out=flat("kdf"), in0=ps[2][:, 0:64], in1=flat("ngc"), op=ALU.add), reads=[ps[2], sm["ngc"]], writes=[sm["kdf"]])
                        op("act", lambda e: e.activation(out=flat("kdf"), in_=flat("kdf"), func=AF.Exp), reads=[sm["kdf"]], writes=[sm["kdf"]])
                        op("act", lambda e: e.activation(out=flat("dec0"), in_=ps[2][:, 64:128], func=AF.Exp), reads=[ps[2]], writes=[sm["dec0"]])
                        op("act", lambda e: e.activation(out=flat("dec1"), in_=ps[2][:, 128:192], func=AF.Exp), reads=[ps[2]], writes=[sm["dec1"]])

                        ckpt("ba")
                        qT = gs.sb("qT", [128, 8, T], BF16)
                        kT = gs.sb("kT", [128, 8, T], BF16)
                        vT = gs.sb("vT", [128, 8, T], BF16)
                        for hf in range(2):
                            with sc.scope() as p1:
                                psets = []
                                for k_ in range(2):
                                    psets.append(dict(xsb=[p1.sb("xsb", [128, T + 3], F32) for i in range(2)], acca=[p1.sb("acca", [128, T], F32) for i in range(2)],
                                                      svb=[p1.sb("svb", [128, T], F32) for i in range(2)], sqb=[p1.sb("sqb", [128, T], BF16) for i in range(2)],
                                                      rnb=[p1.sb("rnb", [128, T], F32) for i in range(2)], pb=[ps[2 * k_], ps[2 * k_ + 1]], sb_=[ps[4 + 2 * k_], ps[5 + 2 * k_]]))
                                chunks = [(kind, hl) for kind in range(3) for hl in range(8)]

                                def p1_pair(pi_, B_):
                                    kind = chunks[2 * pi_][0]
                                    h_first = hf * 8 + chunks[2 * pi_][1]
                                    slab = next_slab(l, kind * 8 + h_first // 2)
                                    info = []
                                    for sub_i in range(2):
                                        hl = chunks[2 * pi_ + sub_i][1]
                                        h = hf * 8 + hl
                                        sub = h % 2
                                        ci = kind * 16 + h
                                        bank = B_["pb"][sub_i]
                                        mm_group(bank, [(bank[:], slab[:, kc, sub * 128:(sub + 1) * 128], hT[:, kc, :], kc == 0, kc == KC - 1) for kc in range(KC)],
                                                 reads=[slab, hT])
                                        info.append((hl, ci, bank))
                                    yield
                                    for sub_i, (hl, ci, bank) in enumerate(info):
                                        xb_ = B_["xsb"][sub_i]
                                        sc.par([("pool", lambda e, xb_=xb_, ci=ci: e.tensor_copy(out=xb_[:, 0:3], in_=cs[l][:, ci, :])),
                                                ("act", lambda e, xb_=xb_, bank=bank: e.activation(out=xb_[:, 3:T + 3], in_=bank[:], func=AF.Copy))],
                                               reads=[cs[l], bank], writes=[xb_])
                                        op("pool", lambda e, xb_=xb_, ci=ci: e.tensor_copy(out=cs[l][:, ci, :], in_=xb_[:, T:T + 3]), reads=[xb_], writes=[cs[l]])
                                    yield
                                    for sub_i, (hl, ci, bank) in enumerate(info):
                                        xb_, aa = B_["xsb"][sub_i], B_["acca"][sub_i]
                                        op("dve", lambda e, xb_=xb_, aa=aa, ci=ci: e.tensor_scalar(out=aa[:], in0=xb_[:, 0:T], scalar1=cw[l][:, 0, ci:ci + 1], scalar2=None, op0=ALU.mult),
                                           reads=[xb_, cw[l]], writes=[aa])
                                        for tap in (1, 2, 3):
                                            op("dve", lambda e, xb_=xb_, aa=aa, ci=ci, tap=tap: e.scalar_tensor_tensor(out=aa[:], in0=xb_[:, tap:T + tap], scalar=cw[l][:, tap, ci:ci + 1], in1=aa[:],
                                                                                                                 op0=ALU.mult, op1=ALU.add), reads=[xb_, cw[l], aa], writes=[aa])
                                    yield
                                    if kind == 2:
                                        for sub_i, (hl, ci, bank) in enumerate(info):
                                            aa = B_["acca"][sub_i]
                                            op("act", lambda e, aa=aa, hl=hl: e.activation(out=vT[:, hl, :], in_=aa[:], func=AF.Silu), reads=[aa], writes=[vT])
                                        return
                                    for sub_i, (hl, ci, bank) in enumerate(info):
                                        aa, sv, sq = B_["acca"][sub_i], B_["svb"][sub_i], B_["sqb"][sub_i]
                                        op("act", lambda e, aa=aa, sv=sv: e.activation(out=sv[:], in_=aa[:], func=AF.Silu), reads=[aa], writes=[sv])
                                        op("act", lambda e, sv=sv, sq=sq: e.activation(out=sq[:], in_=sv[:], func=AF.Square), reads=[sv], writes=[sq])
                                        bank2 = B_["sb_"][sub_i]
                                        op("pe", mm(bank2[:], ones_b[:], sq[:]), reads=[ones_b, sq], writes=[bank2])
                                    yield
                                    dst = qT if kind == 0 else kT
                                    scl = 128.0 ** -0.5 if kind == 0 else 1.0
                                    for sub_i, (hl, ci, bank) in enumerate(info):
                                        sv, rn, bank2 = B_["svb"][sub_i], B_["rnb"][sub_i], B_["sb_"][sub_i]
                                        op("act", lambda e, bank2=bank2, rn=rn: e.activation(out=rn[:], in_=bank2[:], func=AF.Ln, bias=1e-6), reads=[bank2], writes=[rn])
                                        op("act", lambda e, rn=rn: e.activation(out=rn[:], in_=rn[:], func=AF.Exp, scale=-0.5, bias=math.log(scl)), reads=[rn], writes=[rn])
                                        op("pool", lambda e, sv=sv, rn=rn, hl=hl: e.tensor_tensor(out=dst[:, hl, :], in0=sv[:], in1=rn[:], op=ALU.mult), reads=[sv, rn], writes=[dst])

                                for pp in range(6):
                                    alive = [p1_pair(2 * pp, psets[0]), p1_pair(2 * pp + 1, psets[1])]
                                    while alive:
                                        for gen in list(alive):
                                            try:
                                                next(gen)
                                            except StopIteration:
                                                alive.remove(gen)
                            ckpt("p1")
                            tg = sc.scope()
                            tg.__enter__()
                            TT = tg.sb("TT", [128, 4, 8, 128], BF16)
                            AqT = tg.sb("AqT", [128, 4, 8, 128], BF16)
                            with sc.scope() as g1:
                                def msk(dst, src, m):
                                    op("pool", lambda e: e.tensor_tensor(out=v3(dst[:], 4), in0=v3(src[:], 4), in1=bc_mid(C[m][:, :], 4), op=ALU.mult), reads=[src, C[m]], writes=[dst])

                                def ev_copy(dst, bank):
                                    op("act", lambda e: e.activation(out=dst[:], in_=bank[:], func=AF.Copy), reads=[bank], writes=[dst])

                                def ev_add(dst, bank, addend):
                                    op("dve", lambda e: e.tensor_tensor(out=dst[:], in0=bank[:], in1=addend[:], op=ALU.add), reads=[bank, addend], writes=[dst])

                                gsets = []
                                for gi in range(2):
                                    gsets.append(dict(
                                        RD1=g1.sb("RD1", [128, 512], F32), RD2=g1.sb("RD2", [128, 512], F32),
                                        gbc=g1.sb("gbc", [128, 512], BF16), E2T=g1.sb("E2T", [128, 512], BF16),
                                        E1T=g1.sb("E1T", [128, 512], BF16), E1=g1.sb("E1", [128, 512], BF16),
                                        Bf=[g1.sb("Bf%d" % i, [128, 512], BF16) for i in range(8)],
                                        banks=ps[4 * gi:4 * gi + 4], ctr=[0]))

                                def g1_group(b, j, G):
                                    RD1, RD2, gbc, E2T, E1T, E1, Bf = G["RD1"], G["RD2"], G["gbc"], G["E2T"], G["E1T"], G["E1"], G["Bf"]

                                    def nb():
                                        bank = G["banks"][G["ctr"][0] % 4]
                                        G["ctr"][0] += 1
                                        return bank

                                    def mm4(lhs, rhs):
                                        bank = nb()
                                        mm_group(bank, [(bank[:, hh * 128:(hh + 1) * 128], lhs[:, hh * 128:(hh + 1) * 128], rhs[:, hh * 128:(hh + 1) * 128], True, True) for hh in range(4)],
                                                 reads=[lhs, rhs])
                                        return bank
                                    bs = slice(b * 128, (b + 1) * 128)
                                    h0 = hf * 8 + j * 4
                                    hl0 = j * 4
                                    op("dve", lambda e: e.tensor_tensor(out=v3(RD1[:], 4), in0=bc_mid(ident_f[:, :], 4), in1=bc_last(sm["gc"][:, b, h0:h0 + 4], 128), op=ALU.mult),
                                       reads=[ident_f, sm["gc"]], writes=[RD1])
                                    op("pool", lambda e: e.tensor_tensor(out=v3(RD2[:], 4), in0=bc_mid(ident_f[:, :], 4), in1=bc_last(sm["q1"][:, b, h0:h0 + 4], 128), op=ALU.mult),
                                       reads=[ident_f, sm["q1"]], writes=[RD2])
                                    pG, pH, pI = nb(), nb(), nb()
                                    op("pe", mm(pG[:], ones_f[:], RD1[:], True, False), reads=[ones_f, RD1], writes=[pG])
                                    mm_group(pH, [(pH[:], ones_f[:], RD1[:], True, False), (pH[:], ident_b[:], C["mST4"][:], False, True)], reads=[ones_f, RD1, ident_b, C["mST4"]])
                                    mm_group(pI, [(pI[:], ones_f[:], RD2[:], True, False), (pI[:], ident_b[:], C["mS4"][:], False, True)], reads=[ones_f, RD2, ident_b, C["mS4"]])
                                    yield
                                    op("act", lambda e: e.activation(out=gbc[:], in_=pG[:], func=AF.Exp), reads=[pG], writes=[gbc])
                                    op("pe", mm(pG[:], ident_b[:], C["mIT4"][:], False, True), reads=[ident_b, C["mIT4"]], writes=[pG])
                                    for (bank, Et, bias) in ((pH, E1T, "q1"), (pI, E1, "gc")):
                                        for hh in range(4):
                                            op("act", lambda e, bank=bank, Et=Et, bias=bias, hh=hh: e.activation(
                                                out=Et[:, hh * 128:(hh + 1) * 128], in_=bank[:, hh * 128:(hh + 1) * 128], func=AF.Exp, bias=sm[bias][:, b, h0 + hh:h0 + hh + 1]),
                                               reads=[bank, sm[bias]], writes=[Et])
                                    pK = nb()
                                    mm_group(pK, [(pK[:, hh * 128:(hh + 1) * 128], kT[:, hl0 + hh, bs], kT[:, hl0 + hh, bs], True, True) for hh in range(4)], reads=[kT])
                                    yield
                                    for hh in range(4):
                                        op("act", lambda e, hh=hh: e.activation(out=E2T[:, hh * 128:(hh + 1) * 128], in_=pG[:, hh * 128:(hh + 1) * 128], func=AF.Exp,
                                                                              bias=sm["ngc"][:, b, h0 + hh:h0 + hh + 1]), reads=[pG, sm["ngc"]], writes=[E2T])
                                    P0, P0T = Bf[0], Bf[1]
                                    op("dve", lambda e: e.scalar_tensor_tensor(out=P0T[:], in0=pK[:], scalar=-1.0, in1=E1T[:], op0=ALU.mult, op1=ALU.mult), reads=[pK, E1T], writes=[P0T])
                                    op("dve", lambda e: e.scalar_tensor_tensor(out=P0[:], in0=pK[:], scalar=-1.0, in1=E1[:], op0=ALU.mult, op1=ALU.mult), reads=[pK, E1], writes=[P0])
                                    M, MT, M2, M2T, Y1, Y1T = Bf[2], Bf[3], Bf[4], Bf[5], Bf[6], Bf[7]
                                    msk(M, P0, "hm0")
                                    msk(MT, P0T, "hm0")
                                    yield
                                    pQ = nb()
                                    mm_group(pQ, [(pQ[:, hh * 128:(hh + 1) * 128], kT[:, hl0 + hh, bs], qT[:, hl0 + hh, bs], True, True) for hh in range(4)], reads=[kT, qT])
                                    bA = mm4(MT, M)
                                    bB = mm4(M, MT)
                                    yield
                                    op("dve", lambda e: e.tensor_tensor(out=AqT[:, b, hl0:hl0 + 4, :], in0=v3(pQ[:], 4), in1=v3(E2T[:], 4), op=ALU.mult), reads=[pQ, E2T], writes=[AqT])
                                    op("pool", lambda e: e.tensor_tensor(out=qT[:, hl0:hl0 + 4, bs], in0=qT[:, hl0:hl0 + 4, bs], in1=v3(gbc[:], 4), op=ALU.mult), reads=[qT, gbc], writes=[qT])
                                    ev_copy(M2, bA)
                                    op("dve", lambda e: e.tensor_copy(out=M2T[:], in_=bB[:]), reads=[bB], writes=[M2T])
                                    Y0, Y0T = M, MT
                                    op("pool", lambda e: e.tensor_tensor(out=Y0[:], in0=M[:], in1=C["ident4_b"][:], op=ALU.add), reads=[M, C["ident4_b"]], writes=[Y0])
                                    op("pool", lambda e: e.tensor_tensor(out=Y0T[:], in0=MT[:], in1=C["ident4_b"][:], op=ALU.add), reads=[MT, C["ident4_b"]], writes=[Y0T])
                                    yield
                                    bA = mm4(M2T, Y0)
                                    bB = mm4(M2, Y0T)
                                    bC = mm4(M2T, M2)
                                    bD = mm4(M2, M2T)
                                    yield
                                    ev_add(Y1, bA, Y0)
                                    ev_add(Y1T, bB, Y0T)
                                    M4, M4T = Bf[2], Bf[3]
                                    ev_copy(M4, bC)
                                    ev_copy(M4T, bD)
                                    yield
                                    bA = mm4(M4T, Y1)
                                    bB = mm4(M4, Y1T)
                                    yield
                                    Tm, Um = Bf[4], Bf[5]
                                    ev_add(Tm, bA, Y1)
                                    ev_add(Um, bB, Y1T)
                                    for lv in (1, 2):
                                        Pl, PlT, Wp, W = Bf[6], Bf[7], Bf[2], Bf[3]
                                        msk(Pl, P0, "hm%d" % lv)
                                        msk(PlT, P0T, "hm%d" % lv)
                                        yield
                                        bA = mm4(PlT, Tm)
                                        bB = mm4(Pl, Um)
                                        yield
                                        ev_copy(Wp, bA)
                                        op("dve", lambda e, W=W, bB=bB: e.tensor_copy(out=W[:], in_=bB[:]), reads=[bB], writes=[W])
                                        yield
                                        bA = mm4(Um, Wp)
                                        bB = mm4(Tm, W)
                                        yield
                                        ev_add(Tm, bA, Tm)
                                        ev_add(Um, bB, Um)
                                    Pl, W = Bf[6], Bf[2]
                                    msk(Pl, P0, "hm3")
                                    yield
                                    bA = mm4(Pl, Um)
                                    yield
                                    ev_copy(W, bA)
                                    yield
                                    bank = mm4(Tm, W)
                                    yield
                                    op("dve", lambda e: e.tensor_tensor(out=TT[:, b, hl0:hl0 + 4, :], in0=v3(bank[:], 4), in1=v3(Um[:], 4), op=ALU.add), reads=[bank, Um], writes=[TT])

                                run_window([(lambda k, b=b, j=j: g1_group(b, j, gsets[k])) for b in range(4) for j in range(2)], 2)
                            ckpt("g1")
                            with sc.scope() as g2:
                                Sbf = [g2.sb("Sbf%d" % j, [128, 512], BF16) for j in range(2)]
                                kdec = [g2.sb("kdec%d" % j, [128, 512], BF16) for j in range(2)]
                                vtok = [g2.sb("vtok%d" % j, [128, 512], BF16) for j in range(2)]
                                tmpr = [g2.sb("tmpr%d" % j, [128, 512], F32) for j in range(2)]
                                Rp = [g2.sb("Rp%d" % j, [128, 512], BF16) for j in range(2)]
                                Vn = [g2.sb("Vn%d" % j, [128, 512], BF16) for j in range(2)]
                                tmpS = [g2.sb("tmpS%d" % j, [128, 512], F32) for j in range(2)]
                                sqo = [g2.sb("sqo%d" % j, [128, 512], BF16) for j in range(2)]
                                rno = [g2.sb("rno%d" % j, [128, 512], F32) for j in range(2)]
                                for j in range(2):
                                    op("pool", lambda e, j=j: e.memset(Rp[j][:], 0.0), writes=[Rp[j]])
                                    op("pool", lambda e, j=j: e.memset(Vn[j][:], 0.0), writes=[Vn[j]])
                                St = [Sst[l][hf * 2 + j] for j in range(2)]
                                for j in range(2):
                                    op("act", lambda e, j=j: e.activation(out=Sbf[j][:], in_=St[j][:], func=AF.Copy), reads=[St[j]], writes=[Sbf[j]])
                                psA, psB, psC, OT = ps[0:2], ps[2:4], ps[4:6], ps[6:8]
                                for b in range(4):
                                    bs = slice(b * 128, (b + 1) * 128)
                                    for j in range(2):
                                        h0 = hf * 8 + j * 4
                                        hl0 = j * 4
                                        mm_group(psA[j], [(psA[j][:, hh * 128:(hh + 1) * 128], kT[:, hl0 + hh, bs], ident_b[:], True, True) for hh in range(4)], reads=[kT, ident_b])
                                        op("dve", lambda e, j=j, b=b, h0=h0: e.tensor_tensor(out=v3(kdec[j][:], 4), in0=v3(psA[j][:], 4), in1=bc_last(sm["kdf"][:, b, h0:h0 + 4], 128), op=ALU.mult),
                                           reads=[psA[j], sm["kdf"]], writes=[kdec[j]])
                                        mm_group(psB[j], [(psB[j][:, hh * 128:(hh + 1) * 128], vT[:, hl0 + hh, bs], ident_b[:], True, True) for hh in range(4)], reads=[vT, ident_b])
                                        op("act", lambda e, j=j: e.activation(out=vtok[j][:], in_=psB[j][:], func=AF.Copy), reads=[psB[j]], writes=[vtok[j]])
                                    for c in range(2):
                                        r0, r1 = 64 * c, 64 * c + 64
                                        decn = "dec%d" % c
                                        for j in range(2):
                                            hl0 = j * 4
                                            mm_group(psA[j], [(psA[j][:, hh * 128:(hh + 1) * 128], kT[:, hl0 + hh, bs], Sbf[j][:, hh * 128:(hh + 1) * 128], True, True) for hh in range(4)],
                                                     reads=[kT, Sbf[j]])
                                            mm_group(OT[j], [(OT[j][:, hh * 128 + r0:hh * 128 + r1], Sbf[j][:, hh * 128:(hh + 1) * 128], qT[:, hl0 + hh, b * 128 + r0:b * 128 + r1], (c == 0 and hh == 0), False)
                                                             for hh in range(4)], reads=[Sbf[j], qT])
                                        for j in range(2):
                                            h0 = hf * 8 + j * 4
                                            def rp_fn(e, j=j, b=b, h0=h0):
                                                inst = None
                                                for hh in range(4):
                                                    inst = e.scalar_tensor_tensor(out=Rp[j][r0:r1, hh * 128:(hh + 1) * 128], in0=psA[j][r0:r1, hh * 128:(hh + 1) * 128],
                                                                                  scalar=sm["ngam"][r0:r1, b, h0 + hh:h0 + hh + 1], in1=vtok[j][r0:r1, hh * 128:(hh + 1) * 128],
                                                                                  op0=ALU.mult, op1=ALU.add)
                                                    sc.ninst += 1
                                                return inst
                                            op("dve", rp_fn, reads=[psA[j], sm["ngam"], vtok[j]], writes=[Rp[j]])
                                        for j in range(2):
                                            hl0 = j * 4
                                            mm_group(psB[j], [(psB[j][:, hh * 128:(hh + 1) * 128], TT[r0:r1, b, hl0 + hh, :], Rp[j][r0:r1, hh * 128:(hh + 1) * 128], True, True) for hh in range(4)],
                                                     reads=[TT, Rp[j]])
                                        for j in range(2):
                                            h0 = hf * 8 + j * 4
                                            op("dve", lambda e, j=j, b=b, h0=h0: e.tensor_tensor(out=v3(Vn[j][r0:r1, :], 4), in0=v3(psB[j][r0:r1, :], 4),
                                                                                              in1=sm["bet"][r0:r1, b, h0:h0 + 4].unsqueeze(2).broadcast_to([64, 4, 128]), op=ALU.mult),
                                               reads=[psB[j], sm["bet"]], writes=[Vn[j]])
                                        for j in range(2):
                                            mm_group(psC[j], [(psC[j][:, hh * 128:(hh + 1) * 128], kdec[j][r0:r1, hh * 128:(hh + 1) * 128], Vn[j][r0:r1, hh * 128:(hh + 1) * 128], True, True)
                                                              for hh in range(4)], reads=[kdec[j], Vn[j]])
                                        for j in range(2):
                                            h0 = hf * 8 + j * 4
                                            def s_fn(e, j=j, b=b, h0=h0, decn=decn):
                                                inst = None
                                                for hh in range(4):
                                                    inst = e.scalar_tensor_tensor(out=St[j][:, hh * 128:(hh + 1) * 128], in0=St[j][:, hh * 128:(hh + 1) * 128],
                                                                                  scalar=sm[decn][:, b, h0 + hh:h0 + hh + 1], in1=psC[j][:, hh * 128:(hh + 1) * 128],
                                                                                  op0=ALU.mult, op1=ALU.add)
                                                    sc.ninst += 1
                                                return inst
                                            op("dve", s_fn, reads=[psC[j], sm[decn]], writes=[St[j]])
                                            op("act", lambda e, j=j: e.activation(out=Sbf[j][:], in_=St[j][:], func=AF.Copy), reads=[St[j]], writes=[Sbf[j]])
                                        ckpt("g2c%d" % c)
                                    for j in range(2):
                                        hl0 = j * 4
                                        h0 = hf * 8 + j * 4
                                        mm_group(OT[j], [(OT[j][:, hh * 128:(hh + 1) * 128], Vn[j][:, hh * 128:(hh + 1) * 128], AqT[:, b, hl0 + hh, :], False, True) for hh in range(4)],
                                                 reads=[Vn[j], AqT])
                                        if j == 0:
                                            ckpt("g2o")
                                        op("act", lambda e, j=j: e.activation(out=sqo[j][:], in_=OT[j][:], func=AF.Square), reads=[OT[j]], writes=[sqo[j]])
                                        op("pe", mm(psC[j][:], ones_b[:], sqo[j][:]), reads=[ones_b, sqo[j]], writes=[psC[j]])
                                        op("act", lambda e, j=j: e.activation(out=rno[j][:], in_=psC[j][:], func=AF.Ln, scale=1.0 / 128.0, bias=1e-6), reads=[psC[j]], writes=[rno[j]])
                                        op("act", lambda e, j=j: e.activation(out=rno[j][:], in_=rno[j][:], func=AF.Exp, scale=-0.5), reads=[rno[j]], writes=[rno[j]])
                                        op("dve", lambda e, j=j, h0=h0, bs=bs: e.tensor_tensor(out=oaT[:, h0:h0 + 4, bs], in0=v3(OT[j][:], 4), in1=v3(rno[j][:], 4), op=ALU.mult),
                                           reads=[OT[j], rno[j]], writes=[oaT])
                            tg.__exit__(None, None, None)
                        ckpt("g2")
                        with sc.scope() as p4:
                            szs = [p4.sb("sz%d" % i, [128, T], BF16) for i in range(2)]
                            for h in range(16):
                                if h % 2 == 0:
                                    slab = next_slab(l, 25 + h // 2)
                                bank = ps[h % 4]
                                sz = szs[h % 2]
                                mm_group(bank, [(bank[:], slab[:, kc, (h % 2) * 128:(h % 2 + 1) * 128], hT[:, kc, :], kc == 0, kc == KC - 1) for kc in range(KC)], reads=[slab, hT])
                                op("act", lambda e, bank=bank, sz=sz: e.activation(out=sz[:], in_=bank[:], func=AF.Silu), reads=[bank], writes=[sz])
                                op("dve", lambda e, h=h, sz=sz: e.scalar_tensor_tensor(out=oaT[:, h, :], in0=oaT[:, h, :], scalar=gnw[l][:, 0:1], in1=sz[:], op0=ALU.mult, op1=ALU.mult),
                                   reads=[oaT, gnw[l], sz], writes=[oaT])

                    ckpt("p4")
                    with sc.scope() as ws:
                        obT = ws.sb("obT", [128, KC, T], BF16)
                        with sc.scope() as sw:
                            qbT = sw.sb("qbT", [128, 4, T], BF16)
                            szb = sw.sb("szb", [128, 4, T], BF16)
                            kdT = [sw.sb("kdT%d" % i, [128, 128 + T], BF16) for i in range(2)]
                            Vpad = sw.sb("Vpad", [128, 5, 1024], BF16)
                            vp = lambda slot: Vpad[:, slot, :].rearrange("p (g v d) -> p g v d", g=4, v=2)
                            q16 = sw.sb("q16", [128, T], BF16)
                            t1 = sw.sb("t1", [128, T], F32)
                            t2 = sw.sb("t2", [128, T], F32)
                            sset = []
                            for si_ in range(4):
                                sset.append(dict(Pm=sw.sb("Pm", [128, 4, 256], BF16), PTs=sw.sb("PTs", [128, 8, 128], BF16), Dg=sw.sb("Dg", [128, 4, 128], BF16),
                                                 mx=sw.sb("mx", [128, 4], F32), negm=sw.sb("negm", [128, 4], F32), rsum=sw.sb("rsum", [128, 4], F32),
                                                 esk=sw.sb("esk", [128, 4], F32), bk=[ps[2 * si_], ps[2 * si_ + 1]]))
                            for sl_ in range(5):
                                op("pool", lambda e, sl_=sl_: e.memset(Vpad[:, sl_, :], 0.0), writes=[Vpad])

                            ckpt("s0a")
                            def rope_chunk(slab, sub, dst_ap, dst_tn):
                                bank, bank2 = ps[0], ps[1]
                                mm_group(bank, [(bank[:], slab[:, kc, sub * 128:(sub + 1) * 128], hT[:, kc, :], kc == 0, kc == KC - 1) for kc in range(KC)], reads=[slab, hT])
                                op("act", lambda e: e.activation(out=q16[:], in_=bank[:], func=AF.Copy), reads=[bank], writes=[q16])
                                ckpt("s0b")
                                op("pe", mm(bank2[:], C["rotT"][:], q16[:]), reads=[C["rotT"], q16], writes=[bank2])
                                ckpt("s0c")
                                op("dve", lambda e: e.tensor_tensor(out=t1[:], in0=bank[:], in1=cosT[:], op=ALU.mult), reads=[bank, cosT], writes=[t1])
                                op("dve", lambda e: e.tensor_tensor(out=t2[:], in0=bank2[:], in1=sinT[:], op=ALU.mult), reads=[bank2, sinT], writes=[t2])
                                ckpt("s0d")
                                op("pool", lambda e: e.tensor_tensor(out=dst_ap, in0=t1[:], in1=t2[:], op=ALU.add), reads=[t1, t2], writes=[dst_tn])

                            for g in range(4):
                                for a in range(4):
                                    pa_ = 4 * g + a
                                    if pa_ % 2 == 0:
                                        slab = next_slab(l, 33 + pa_ // 2)
                                    rope_chunk(slab, pa_ % 2, qbT[:, a, :], qbT)
                                ckpt("s1")
                                if g % 2 == 0:
                                    slab = next_slab(l, 41 + g // 2)
                                    for gp in range(2):
                                        gg = g + gp
                                        op("pool", lambda e, gp=gp, gg=gg: e.tensor_copy(out=kdT[gp][:, 0:128], in_=kh[l][:, gg, :]), reads=[kh[l]], writes=[kdT[gp]])
                                        rope_chunk(slab, gp, kdT[gp][:, 128:128 + T], kdT[gp])
                                        op("pool", lambda e, gp=gp, gg=gg: e.tensor_copy(out=kh[l][:, gg, :], in_=kdT[gp][:, T:T + 128]), reads=[kdT[gp]], writes=[kh[l]])
                                ckpt("s2")
                                if g == 0:
                                    slab = next_slab(l, 43)
                                    for var in range(2):
                                        op("pool", lambda e, var=var: e.tensor_copy(out=vp(0)[:, :, var, var * 64:(var + 1) * 64], in_=vh[l][:, :, :]), reads=[vh[l]], writes=[Vpad])
                                    for b2 in range(2):
                                        bank = ps[2 + b2]
                                        mm_group(bank, [(bank[:, bb * 256:(bb + 1) * 256], hT[:, kc, (2 * b2 + bb) * 128:(2 * b2 + bb + 1) * 128], slab[:, kc, :], kc == 0, kc == KC - 1)
                                                        for bb in range(2) for kc in range(KC)], reads=[hT, slab])
                                        for bb in range(2):
                                            blk = 2 * b2 + bb
                                            for var in range(2):
                                                op("act", lambda e, bank=bank, bb=bb, blk=blk, var=var: e.activation(
                                                    out=vp(1 + blk)[:, :, var, var * 64:(var + 1) * 64], in_=bank[:, bb * 256:(bb + 1) * 256].rearrange("p (g d) -> p g d", g=4), func=AF.Copy),
                                                   reads=[bank], writes=[Vpad])
                                            if blk == 3:
                                                op("act", lambda e, bank=bank, bb=bb: e.activation(out=vh[l][:, :, :], in_=bank[:, bb * 256:(bb + 1) * 256].rearrange("p (g d) -> p g d", g=4), func=AF.Copy),
                                                   reads=[bank], writes=[vh[l]])
                                ckpt("s3")
                                for a in range(4):
                                    pa_ = 4 * g + a
                                    if pa_ % 2 == 0:
                                        slab = next_slab(l, 44 + pa_ // 2)
                                    bank = ps[4 + a % 2]
                                    mm_group(bank, [(bank[:], slab[:, kc, (pa_ % 2) * 128:(pa_ % 2 + 1) * 128], hT[:, kc, :], kc == 0, kc == KC - 1) for kc in range(KC)], reads=[slab, hT])
                                    op("act", lambda e, bank=bank, a=a: e.activation(out=szb[:, a, :], in_=bank[:], func=AF.Silu), reads=[bank], writes=[szb])
                                ckpt("s4")
                                kd = kdT[g % 2]

                                def swa_stream(g, b, quad, B_):
                                    Pm, PTs, Dg, mx, negm, rsum, esk, bk = (B_[k_] for k_ in ("Pm", "PTs", "Dg", "mx", "negm", "rsum", "esk", "bk"))
                                    mask = C["mB0"] if (t == 0 and b == 0) else C["mB"]
                                    bsl = slice(b * 128, (b + 1) * 128)
                                    s0 = 8 * g + 4 * quad
                                    for i in range(4):
                                        hh = 4 * quad + i
                                        a, par = hh // 2, hh % 2
                                        bank = bk[i // 2]
                                        o_ = bank[:, (i % 2) * 256:(i % 2 + 1) * 256]
                                        mm_group(bank, [(o_, qbT[par * 64:(par + 1) * 64, a, bsl], kd[par * 64:(par + 1) * 64, b * 128:b * 128 + 256], True, False),
                                                        (o_, ident_b[:], mask[:], False, True)], reads=[qbT, kd, ident_b, mask])
                                    yield
                                    for q in range(2):
                                        op("dve", lambda e, q=q: e.tensor_reduce(out=mx[:, 2 * q:2 * q + 2], in_=v3(bk[q][:], 2), axis=AX.X, op=ALU.max), reads=[bk[q]], writes=[mx])
                                    op("dve", lambda e: e.scalar_tensor_tensor(out=negm[:], in0=mx[:], scalar=0.125, in1=sinkb[l][:, s0:s0 + 4], op0=ALU.mult, op1=ALU.max),
                                       reads=[mx, sinkb[l]], writes=[negm])
                                    op("dve", lambda e: e.tensor_scalar(out=negm[:], in0=negm[:], scalar1=-1.0, scalar2=None, op0=ALU.mult), reads=[negm], writes=[negm])
                                    yield
                                    for i in range(4):
                                        op("act", lambda e, i=i: e.activation(out=Pm[:, i, :], in_=bk[i // 2][:, (i % 2) * 256:(i % 2 + 1) * 256], func=AF.Exp,
                                                                            bias=negm[:, i:i + 1], scale=0.125, accum_out=rsum[:, i:i + 1]),
                                           reads=[bk[i // 2], negm], writes=[Pm, rsum])
                                    op("dve", lambda e: e.tensor_tensor(out=esk[:], in0=sinkb[l][:, s0:s0 + 4], in1=negm[:], op=ALU.add), reads=[sinkb[l], negm], writes=[esk])
                                    op("act", lambda e: e.activation(out=esk[:], in_=esk[:], func=AF.Exp), reads=[esk], writes=[esk])
                                    yield
                                    op("dve", lambda e: e.tensor_tensor(out=esk[:], in0=esk[:], in1=rsum[:], op=ALU.add), reads=[esk, rsum], writes=[esk])
                                    op("dve", lambda e: e.reciprocal(out=esk[:], in_=esk[:]), reads=[esk], writes=[esk])
                                    op("dve", lambda e: e.tensor_tensor(out=Dg[:], in0=bc_mid(ident_b[:, :], 4), in1=bc_last(esk[:, 0:4], 128), op=ALU.mult), reads=[ident_b, esk], writes=[Dg])
                                    yield
                                    for q in range(2):
                                        bank = bk[q]
                                        items = []
                                        for i4 in range(4):
                                            idx = q * 4 + i4
                                            i, kb = idx // 2, idx % 2
                                            items.append((bank[:, i4 * 128:(i4 + 1) * 128], Pm[:, i, kb * 128:(kb + 1) * 128], Dg[:, i, :], True, True))
                                        mm_group(bank, items, reads=[Pm, Dg])
                                    yield
                                    op("act", lambda e: e.activation(out=PTs[:, 0:4, :], in_=v3(bk[0][:], 4), func=AF.Copy), reads=[bk[0]], writes=[PTs])
                                    pt2 = Tn(None)
                                    op("dve", lambda e: e.tensor_copy(out=PTs[:, 4:8, :], in_=v3(bk[1][:], 4)), reads=[bk[1], PTs], writes=[pt2])
                                    _merge(PTs.w, pt2.w)
                                    yield
                                    items = []
                                    for al in range(2):
                                        n = 0
                                        for i in (2 * al, 2 * al + 1):
                                            hh = 4 * quad + i
                                            for kb in range(2):
                                                items.append((bk[0][:, al * 128:(al + 1) * 128], Vpad[:, b + kb, (g * 2 + hh % 2) * 128:(g * 2 + hh % 2 + 1) * 128], PTs[:, i * 2 + kb, :], n == 0, n == 3))
                                                n += 1
                                    mm_group(bk[0], items, reads=[Vpad, PTs])
                                    yield
                                    c0 = 4 * g + 2 * quad
                                    op("dve", lambda e: e.tensor_tensor(out=obT[:, c0:c0 + 2, bsl], in0=v3(bk[0][:, 0:256], 2), in1=szb[:, 2 * quad:2 * quad + 2, bsl], op=ALU.mult),
                                       reads=[bk[0], szb], writes=[obT])

                                run_window([(lambda k, g=g, b=b, q=q: swa_stream(g, b, q, sset[k])) for b in range(4) for q in range(2)], 4)
                        ckpt("swa")
                        with sc.scope() as mg:
                            yT = mg.sb("yT", [128, KC, T], BF16)
                            sg = [mg.sb("sg%d" % i, [128, T], F32) for i in range(2)]
                            ya = [mg.sb("ya%d" % i, [128, T], F32) for i in range(2)]
                            yb = [mg.sb("yb%d" % i, [128, T], F32) for i in range(2)]
                            for jj in range(8):
                                for (si_, srcT, which) in ((52, hT, "ga"), (68, oaT, "pa"), (60, hT, "gb"), (76, obT, "pb")):
                                    slab = next_slab(l, si_ + jj)
                                    for i in range(2):
                                        dc = 2 * jj + i
                                        bank = ps[{"ga": 0, "pa": 2, "gb": 4, "pb": 6}[which] + i]
                                        mm_group(bank, [(bank[:], slab[:, kc, i * 128:(i + 1) * 128], srcT[:, kc, :], kc == 0, kc == KC - 1) for kc in range(KC)], reads=[slab, srcT])
                                        if which in ("ga", "gb"):
                                            op("act", lambda e, bank=bank, i=i: e.activation(out=sg[i][:], in_=bank[:], func=AF.Sigmoid), reads=[bank], writes=[sg[i]])
                                        elif which == "pa":
                                            op("dve", lambda e, bank=bank, i=i: e.tensor_tensor(out=ya[i][:], in0=bank[:], in1=sg[i][:], op=ALU.mult), reads=[bank, sg[i]], writes=[ya[i]])
                                        else:
                                            op("dve", lambda e, bank=bank, i=i: e.tensor_tensor(out=yb[i][:], in0=bank[:], in1=sg[i][:], op=ALU.mult), reads=[bank, sg[i]], writes=[yb[i]])
                                            op("pool", lambda e, i=i, dc=dc: e.tensor_tensor(out=yT[:, dc, :], in0=ya[i][:], in1=yb[i][:], op=ALU.add), reads=[ya[i], yb[i]], writes=[yT])
                            ckpt("merge")
                            tw = [mg.sb("tw%d" % i, [128, 512], F32) for i in range(2)]
                            for w in range(8):
                                slab = next_slab(l, 84 + w)
                                for s2 in range(2):
                                    bank = ps[(2 * w + s2) % 8]
                                    mm_group(bank, [(bank[:, ss_ * 256:(ss_ + 1) * 256], yT[:, kc, (2 * s2 + ss_) * 128:(2 * s2 + ss_ + 1) * 128], slab[:, kc, :], kc == 0, kc == KC - 1)
                                                    for ss_ in range(2) for kc in range(KC)], reads=[yT, slab])
                                    tw_ = tw[s2]
                                    op("dve", lambda e, bank=bank, tw_=tw_, w=w: e.tensor_tensor(out=v3(tw_[:], 2), in0=v3(bank[:], 2), in1=bc_mid(bc8k[:, w * 256:(w + 1) * 256], 2), op=ALU.mult),
                                       reads=[bank, bc8k], writes=[tw_])
                                    for ss_ in range(2):
                                        s = 2 * s2 + ss_
                                        op("pool", lambda e, s=s, ss_=ss_, tw_=tw_, w=w: e.tensor_tensor(out=xs_[s][:, w * 256:(w + 1) * 256], in0=xs_[s][:, w * 256:(w + 1) * 256],
                                                                                                     in1=tw_[:, ss_ * 256:(ss_ + 1) * 256], op=ALU.add), reads=[xs_[s], tw_], writes=[xs_[s]])
                    ckpt("L%d" % l)
                ckpt("wout")
                with sc.scope() as fn_:
                    junk = fn_.sb("fjunk", [128, D], BF16)
                    ss = fn_.sb("fss", [128, 1], F32)
                    ob = [fn_.sb("ob%d" % i, [128, D], F32) for i in range(2)]
                    sc.dma("sp", [(bc8k[:, 0:1024], fnw_d[:, 0:1024].partition_broadcast(128)), (bc8k[:, 1024:2048], fnw_d[:, 1024:2048].partition_broadcast(128))], writes=[bc8k])
                    for s in range(4):
                        op("act", lambda e, s=s: e.activation(out=junk[:], in_=xs_[s][:], func=AF.Square, accum_out=ss[:, 0:1]), reads=[xs_[s]], writes=[junk, ss])
                        rstd = rsqrt_small(fn_, ss, 1.0 / D, 1e-6, "fn")
                        o_ = ob[s % 2]
                        op("dve", lambda e, s=s, o_=o_, rstd=rstd: e.scalar_tensor_tensor(out=o_[:], in0=xs_[s][:], scalar=rstd[:, 0:1], in1=bc8k[:], op0=ALU.mult, op1=ALU.mult),
                           reads=[xs_[s], rstd, bc8k], writes=[o_])
                        sc.dma("act", [(out_d[t0 + s * 128:t0 + (s + 1) * 128, 0:1024], o_[:, 0:1024]), (out_d[t0 + s * 128:t0 + (s + 1) * 128, 1024:2048], o_[:, 1024:2048])], reads=[o_])
        try:
            if STOP != "c0":
                body()
        except StopBuild:
            pass
        sc.finish(xs_ + [bc8k])
        build_nc.ninst = sc.ninst
    return nc


_CACHE = {}


def kernel(**inputs):
    x = np.asarray(inputs["x"], dtype=np.float32)
    B, S, _ = x.shape
    DEPTH = inputs["ada_w"].shape[0]
    key = (S, DEPTH)
    if key not in _CACHE:
        _CACHE[key] = build_nc(S, DEPTH)
    nc = _CACHE[key]
    consts = _consts()
    f = lambda n: np.ascontiguousarray(np.asarray(inputs[n], dtype=np.float32))
    shared = {
        "ada_w": f("ada_w"), "ada_b": f("ada_b").reshape(DEPTH, 1, 3 * D), "norm_w": f("norm_w").reshape(DEPTH, 16, 128),
        "w_in": f("w_in"), "conv_w": f("conv_w").reshape(DEPTH, 192, 128), "gdn_a_log": f("gdn_a_log").reshape(DEPTH, 1, 16),
        "gdn_dt_bias": f("gdn_dt_bias").reshape(DEPTH, 1, 16), "gdn_norm_w": f("gdn_norm_w").reshape(DEPTH, 128, 1),
        "swa_sinks": f("swa_sinks").reshape(DEPTH, 1, 32), "proj_a": f("proj_a"), "proj_b": f("proj_b"), "w_out": f("w_out"),
        "final_norm_w": f("final_norm_w").reshape(1, D),
    }
    shared.update(consts)
    pos = np.asarray(inputs["positions"]).astype(np.int32)
    cc = f("c")
    in_maps = []
    for b in range(B):
        m = dict(shared)
        m["x"] = np.ascontiguousarray(x[b])
        m["c"] = np.ascontiguousarray(cc[b].reshape(16, 128))
        m["positions"] = np.ascontiguousarray(pos[b].reshape(1, S))
        in_maps.append(m)
    import os as _os
    _off = int(_os.environ.get("KCORE", "0"))
    res = run_bass_kernel_spmd(nc, in_maps, core_ids=[_off + i for i in range(B)])
    return np.stack([np.asarray(r["out"], dtype=np.float32) for r in res.results], axis=0)
```

```python
import math
from contextlib import ExitStack

import numpy as np
import ml_dtypes

import concourse.bass as bass
import concourse.mybir as mybir
from concourse.bass_utils import run_bass_kernel_spmd

F32 = mybir.dt.float32
BF16 = mybir.dt.bfloat16
I32 = mybir.dt.int32
AF = mybir.ActivationFunctionType
ALU = mybir.AluOpType
AX = mybir.AxisListType

D = 2048
KC = 16
T = 512
NEG = -30000.0
IN_COLS = 16928
NSLAB = 92
SW = 256
RING = 3
NDS = 16
TWO_PI = 2.0 * math.pi


def _merge(dst, src):
    for k, (s, v) in src.items():
        if k not in dst or dst[k][1] < v:
            dst[k] = (s, v)


class Tn:
    def __init__(self, t, inherit=None, excl=False):
        self.t = t
        self.excl = excl
        self.w = dict(inherit) if inherit else {}
        self.r = {}

    def __getitem__(self, k):
        return self.t[k]


class Scope:
    def __init__(self, sched):
        self.s = sched
        self.es = ExitStack()
        self.tns = []

    def __enter__(self):
        self.es.__enter__()
        return self

    def sb(self, name, shape, dt):
        t = self.es.enter_context(self.s.nc.sbuf_tensor(self.s.uname(name), list(shape), dt))
        tn = Tn(t, self.s.residue)
        self.tns.append(tn)
        return tn

    def __exit__(self, *a):
        for tn in self.tns:
            _merge(self.s.residue, tn.w)
            _merge(self.s.residue, tn.r)
        return self.es.__exit__(*a)


class Sched:
    def __init__(self, nc, es):
        self.nc = nc
        self.eng = {"pe": nc.tensor, "act": nc.scalar, "dve": nc.vector, "pool": nc.gpsimd, "sp": nc.sync}
        self.sem = {k: es.enter_context(nc.semaphore("sem_" + k)) for k in ("pe", "act", "dve", "pool")}
        self.cnt = {k: 0 for k in self.sem}
        self.known = {k: {} for k in self.eng}
        self.dsem = [es.enter_context(nc.semaphore("dsem%d" % i)) for i in range(NDS)]
        self.dcnt = [0] * NDS
        self.dnext = 0
        self.residue = {}
        self._n = 0
        self.ninst = 0
        self.stopped = False
        self.hist = {}
        CUR[0] = self

    def uname(self, n):
        self._n += 1
        if not self.stopped:
            NAMES[n] = "%s_%d" % (n, self._n)
        return "%s_%d" % (n, self._n)

    def scope(self):
        return Scope(self)

    def _need(self, reads, writes):
        need = {}
        for b in reads:
            _merge(need, b.w)
            if b.excl:
                _merge(need, b.r)
        for b in writes:
            _merge(need, b.w)
            _merge(need, b.r)
        return need

    def _wait(self, e, need):
        if self.stopped:
            return
        kn = self.known[e]
        for k, (s, v) in sorted(need.items(), key=lambda kv: -kv[1][1]):
            if kn.get(k, 0) < v:
                self.eng[e].wait_ge(s, v)
                kn[k] = v
                self.ninst += 1
                h = self.hist.get((k, v))
                if h:
                    for k2, v2 in h.items():
                        if kn.get(k2, 0) < v2:
                            kn[k2] = v2

    def _commit(self, tok, reads, writes):
        for b in writes:
            b.w = dict(tok)
            b.r = {}
        for b in reads:
            _merge(b.r, tok)

    def op(self, e, fn, reads=(), writes=()):
        if self.stopped:
            return
        need = self._need(reads, writes)
        if e == "pe":
            need.pop("pe", None)
        self._wait(e, need)
        inst = fn(self.eng[e])
        self.cnt[e] += 1
        inst.then_inc(self.sem[e], 1)
        self.ninst += 1
        self.hist[(e, self.cnt[e])] = dict(self.known[e])
        self._commit({e: (self.sem[e], self.cnt[e])}, reads, writes)

    def par(self, items, reads=(), writes=()):
        if self.stopped:
            return
        need = self._need(reads, writes)
        tok = {}
        for (e, fn) in items:
            nd = dict(need)
            if e == "pe":
                nd.pop("pe", None)
            self._wait(e, nd)
            inst = fn(self.eng[e])
            self.cnt[e] += 1
            inst.then_inc(self.sem[e], 1)
            self.ninst += 1
            self.hist[(e, self.cnt[e])] = dict(self.known[e])
            tok[e] = (self.sem[e], self.cnt[e])
        self._commit(tok, reads, writes)

    def dma(self, q, pairs, reads=(), writes=()):
        if self.stopped:
            return
        i = self.dnext
        self.dnext = (i + 1) % NDS
        need = self._need(reads, writes)
        if self.dcnt[i] > 0:
            need["d%d" % i] = (self.dsem[i], self.dcnt[i])
        self._wait(q, need)
        for (o, a) in pairs:
            self.eng[q].dma_start(out=o, in_=a).then_inc(self.dsem[i], 16)
            self.dcnt[i] += 16
            self.ninst += 1
        self.hist[("d%d" % i, self.dcnt[i])] = dict(self.known[q])
        self._commit({"d%d" % i: (self.dsem[i], self.dcnt[i])}, reads, writes)

    def finish(self, tns):
        need = {}
        for b in tns:
            _merge(need, b.w)
            _merge(need, b.r)
        _merge(need, self.residue)
        self.stopped = False
        for e in ("sp", "act", "pool", "dve", "pe"):
            self._wait(e, dict(need))


def _consts():
    idx = np.arange(128)
    ch = idx // 64
    same = ch[:, None] == ch[None, :]
    c = {}
    c["ident_f"] = np.eye(128, dtype=np.float32)
    c["ones_f"] = np.ones((128, 128), np.float32)
    mS = np.where(same & (idx[:, None] > idx[None, :]), 0.0, NEG)
    mST = np.where(same & (idx[None, :] > idx[:, None]), 0.0, NEG)
    mIT = np.where(same & (idx[None, :] >= idx[:, None]), 0.0, NEG)
    c["tri"] = (same & (idx[:, None] <= idx[None, :])).astype(np.float32)
    lsel = np.zeros((128, 128), np.float32)
    for m in range(128):
        lsel[64 * (m // 64) + 63, m] = 1.0
    c["lsel"] = lsel
    l0 = np.zeros((128, 128), np.float32); l0[63, :] = 1.0
    l1 = np.zeros((128, 128), np.float32); l1[127, :] = 1.0
    c["l0"] = l0
    c["l1"] = l1
    c["invf"] = (10000.0 ** (-(np.arange(128) % 32).astype(np.float64) / 32.0)).astype(np.float32)[:, None]
    bf = {}
    bf["ident_b"] = np.eye(128)
    bf["ident4_b"] = np.tile(np.eye(128), (1, 4))
    bf["ones_b"] = np.ones((128, 128))
    bf["mS4"] = np.tile(mS, (1, 4))
    bf["mST4"] = np.tile(mST, (1, 4))
    bf["mIT4"] = np.tile(mIT, (1, 4))
    left = np.where(idx[None, :] > idx[:, None], 0.0, NEG)
    right = np.where(idx[None, :] <= idx[:, None], 0.0, NEG)
    bf["mB"] = np.concatenate([left, right], 1)
    bf["mB0"] = np.concatenate([np.full((128, 128), NEG), right], 1)
    for lv in range(4):
        a = 8 * 2 ** lv
        if lv == 0:
            bf["hm0"] = (idx[:, None] // 8 == idx[None, :] // 8).astype(np.float64)
        else:
            bf["hm%d" % lv] = ((idx[:, None] // a == idx[None, :] // a) & (idx[:, None] // (a // 2) != idx[None, :] // (a // 2))).astype(np.float64)
    rot = np.zeros((128, 128))
    for m in range(128):
        if m % 64 < 32:
            rot[m + 32, m] = -1.0
        else:
            rot[m - 32, m] = 1.0
    bf["rotT"] = rot
    for k, v in bf.items():
        c[k] = v.astype(ml_dtypes.bfloat16)
    return c


CONST_F32 = ["ident_f", "ones_f", "tri", "lsel", "l0", "l1", "invf"]
CONST_BF = ["ident_b", "ident4_b", "ones_b", "mS4", "mST4", "mIT4", "mB", "mB0", "rotT", "hm0", "hm1", "hm2", "hm3"]


def _slab_pieces():
    sl = []
    for i in range(24):
        sl.append([("w_in", i * SW, SW, 0)])
    sl.append([("w_in", 6144, 32, 0)])
    for i in range(8):
        sl.append([("w_in", 6176 + i * SW, SW, 0)])
    for i in range(8):
        sl.append([("w_in", 8224 + i * SW, SW, 0)])
    kb = 10272
    sl.append([("w_in", kb, 64, 0), ("w_in", kb, 64, 64), ("w_in", kb + 64, 64, 128), ("w_in", kb + 64, 64, 192)])
    sl.append([("w_in", kb + 128, 64, 0), ("w_in", kb + 128, 64, 64), ("w_in", kb + 192, 64, 128), ("w_in", kb + 192, 64, 192)])
    sl.append([("w_in", 10528, SW, 0)])
    for i in range(8):
        sl.append([("w_in", 10784 + i * SW, SW, 0)])
    for i in range(8):
        sl.append([("w_in", 12832 + i * SW, SW, 0)])
    for i in range(8):
        sl.append([("w_in", 14880 + i * SW, SW, 0)])
    for nm in ("proj_a", "proj_b", "w_out"):
        for i in range(8):
            sl.append([(nm, i * SW, SW, 0)])
    assert len(sl) == NSLAB
    return sl


def _layer_order():
    o = [24]
    for hf in range(2):
        o += [hf * 4 + i for i in range(4)] + [8 + hf * 4 + i for i in range(4)] + [16 + hf * 4 + i for i in range(4)]
    o += list(range(25, 33))
    for g in range(4):
        o += [33 + 2 * g, 34 + 2 * g]
        if g % 2 == 0:
            o += [41 + g // 2]
        if g == 0:
            o += [43]
        o += [44 + 2 * g, 45 + 2 * g]
    for j in range(8):
        o += [52 + j, 68 + j, 60 + j, 76 + j]
    o += list(range(84, 92))
    return o


STOP = None


class StopBuild(Exception):
    pass


CUR = [None]
NAMES = {}


def ckpt(name):
    if STOP == name:
        CUR[0].stopped = True


def build_nc(S, DEPTH=2):
    NT = S // T
    nc = bass.Bass("TRN2", target_bir_lowering=False)
    dram = {}

    def din(name, shape, dt=F32):
        dram[name] = nc.dram_tensor(name, list(shape), dt, kind="ExternalInput").ap()
        return dram[name]

    x_d = din("x", [S, D])
    c_d = din("c", [16, 128])
    pos_d = din("positions", [1, S], I32)
    adaw_d = din("ada_w", [DEPTH, D, 3 * D])
    adab_d = din("ada_b", [DEPTH, 1, 3 * D])
    normw_d = din("norm_w", [DEPTH, 16, 128])
    win_d = din("w_in", [DEPTH, D, IN_COLS])
    convw_d = din("conv_w", [DEPTH, 192, 128])
    alog_d = din("gdn_a_log", [DEPTH, 1, 16])
    dtb_d = din("gdn_dt_bias", [DEPTH, 1, 16])
    gnw_d = din("gdn_norm_w", [DEPTH, 128, 1])
    sink_d = din("swa_sinks", [DEPTH, 1, 32])
    pa_d = din("proj_a", [DEPTH, D, D])
    pb_d = din("proj_b", [DEPTH, D, D])
    wo_d = din("w_out", [DEPTH, D, D])
    fnw_d = din("final_norm_w", [1, D])
    for n in CONST_F32:
        din(n, [128, 1] if n == "invf" else [128, 128])
    for n in CONST_BF:
        w = {"ident4_b": 512, "mS4": 512, "mST4": 512, "mIT4": 512, "mB": 256, "mB0": 256}.get(n, 128)
        din(n, [128, w], BF16)
    out_d = nc.dram_tensor("out", [S, D], F32, kind="ExternalOutput").ap()
    wb_d = nc.dram_tensor("wb", [DEPTH, NSLAB, 128, KC * SW], BF16, kind="Internal").ap()
    gate_d = nc.dram_tensor("gate_s", [DEPTH, 1, D], F32, kind="Internal").ap()
    wsrc = {"w_in": win_d, "proj_a": pa_d, "proj_b": pb_d, "w_out": wo_d}
    pieces = _slab_pieces()
    order = []
    for t in range(NT):
        for l in range(DEPTH):
            order += [(l, s) for s in _layer_order()]

    with ExitStack() as es:
        sc = Sched(nc, es)
        op = sc.op

        def P(name, shape, dt):
            return Tn(es.enter_context(nc.sbuf_tensor(name, list(shape), dt)))

        ps = [Tn(es.enter_context(nc.psum_tensor("psb%d" % i, [128, 512], F32)), excl=True) for i in range(8)]
        C = {}
        for n in CONST_F32 + CONST_BF:
            shp = list(dram[n].shape)
            C[n] = P("c_" + n, shp, BF16 if n in CONST_BF else F32)
            sc.dma("sp", [(C[n][:], dram[n][:, :])], writes=[C[n]])
        ident_f, ones_f, ident_b, ones_b = C["ident_f"], C["ones_f"], C["ident_b"], C["ones_b"]
        one11 = ones_f
        xs_ = [P("x%d" % s, [128, D], F32) for s in range(4)]
        hT = P("hT", [128, KC, T], BF16)
        oaT = P("oaT", [128, KC, T], BF16)
        ring = [P("ring%d" % i, [128, KC, SW], BF16) for i in range(RING)]
        Sst = [[P("S%d_%d" % (l, j), [128, 512], F32) for j in range(4)] for l in range(DEPTH)]
        bc8k = P("bc8k", [128, D], F32)
        cs = [P("cs%d" % l, [128, 48, 3], F32) for l in range(DEPTH)]
        kh = [P("kh%d" % l, [128, 4, 128], BF16) for l in range(DEPTH)]
        vh = [P("vh%d" % l, [128, 4, 64], BF16) for l in range(DEPTH)]
        AT = [P("AT%d" % l, [128, 16], F32) for l in range(DEPTH)]
        BT = [P("BT%d" % l, [128, 16], F32) for l in range(DEPTH)]
        cw = [P("cw%d" % l, [128, 4, 48], F32) for l in range(DEPTH)]
        negA = [P("negA%d" % l, [128, 16], F32) for l in range(DEPTH)]
        dtb = [P("dtb%d" % l, [128, 16], F32) for l in range(DEPTH)]
        gnw = [P("gnw%d" % l, [128, 1], F32) for l in range(DEPTH)]
        sinkb = [P("sink%d" % l, [128, 32], F32) for l in range(DEPTH)]
        cosT = P("cosT", [128, T], F32)
        sinT = P("sinT", [128, T], F32)
        for l in range(DEPTH):
            for j in range(4):
                op("pool", lambda e, a=Sst[l][j]: e.memset(a[:], 0.0), writes=[Sst[l][j]])
            op("pool", lambda e, a=cs[l]: e.memset(a[:], 0.0), writes=[cs[l]])
            op("pool", lambda e, a=kh[l]: e.memset(a[:], 0.0), writes=[kh[l]])
            op("pool", lambda e, a=vh[l]: e.memset(a[:], 0.0), writes=[vh[l]])
            sc.dma("sp", [(dtb[l][:], dtb_d[l].partition_broadcast(128))], writes=[dtb[l]])
            sc.dma("sp", [(negA[l][:], alog_d[l].partition_broadcast(128))], writes=[negA[l]])
            sc.dma("sp", [(sinkb[l][:], sink_d[l].partition_broadcast(128))], writes=[sinkb[l]])
            sc.dma("sp", [(gnw[l][:], gnw_d[l])], writes=[gnw[l]])
            op("act", lambda e, a=negA[l]: e.activation(out=a[:], in_=a[:], func=AF.Exp), reads=[negA[l]], writes=[negA[l]])
            op("dve", lambda e, a=negA[l]: e.tensor_scalar(out=a[:], in0=a[:], scalar1=-1.0, scalar2=None, op0=ALU.mult),
               reads=[negA[l]], writes=[negA[l]])

        def mm(out, lhsT, rhs, start=True, stop=True):
            return lambda e: e.matmul(out, lhsT=lhsT, rhs=rhs, start=start, stop=stop, skip_group_check=True)

        def mm_group(bank, items, reads):
            def fn(e):
                inst = None
                for (o, l_, r_, st, sp) in items:
                    inst = e.matmul(o, lhsT=l_, rhs=r_, start=st, stop=sp, skip_group_check=True)
                    sc.ninst += 1
                return inst
            op("pe", fn, reads=reads, writes=[bank])

        def run_window(factories, nslots):
            pending = list(factories)
            slots = [None] * nslots
            while pending or any(x is not None for x in slots):
                for k in range(nslots):
                    if slots[k] is None and pending:
                        slots[k] = pending.pop(0)(k)
                    if slots[k] is not None:
                        try:
                            next(slots[k])
                        except StopIteration:
                            slots[k] = None

        def bc_last(ap2, n):
            return ap2.unsqueeze(2).broadcast_to([128, ap2.shape[1], n])

        def bc_mid(ap2, k):
            return ap2.unsqueeze(1).broadcast_to([128, k, ap2.shape[1]])

        def v3(ap, k):
            return ap.rearrange("p (k n) -> p k n", k=k)

        def body():
            with sc.scope() as pre:
                crow = pre.sb("crow", [16, 128], F32)
                cact = pre.sb("cact", [128, 16], F32)
                tmp16 = pre.sb("tmp16", [128, 16], F32)
                sc.dma("sp", [(crow[:], c_d[:, :])], writes=[crow])
                op("pe", mm(ps[0][:, 0:16], crow[:], ident_f[0:16, 0:16]), reads=[crow, ident_f], writes=[ps[0]])
                op("act", lambda e: e.activation(out=tmp16[:], in_=ps[0][:, 0:16], func=AF.Exp, scale=-1.0), reads=[ps[0]], writes=[tmp16])
                op("dve", lambda e: e.tensor_scalar(out=tmp16[:], in0=tmp16[:], scalar1=1.0, scalar2=None, op0=ALU.add), reads=[tmp16], writes=[tmp16])
                op("dve", lambda e: e.reciprocal(out=tmp16[:], in_=tmp16[:]), reads=[tmp16], writes=[tmp16])
                op("dve", lambda e: e.tensor_tensor(out=cact[:], in0=ps[0][:, 0:16], in1=tmp16[:], op=ALU.mult), reads=[ps[0], tmp16], writes=[cact])
                ckpt("c1")
                row = pre.sb("row", [1, D], F32)
                brow = pre.sb("brow", [1, D], F32)
                wst = [pre.sb("wst%d" % i, [128, D], F32) for i in range(2)]
                nwrow = pre.sb("nwrow", [16, 128], F32)
                cwrow = [pre.sb("cwrow%d" % i, [96, 128], F32) for i in range(2)]
                modT = pre.sb("modT", [128, 32], F32)
                nwT = pre.sb("nwT", [128, 16], F32)
                wi = 0
                for l in range(DEPTH):
                    for third in range(3):
                        sc.dma("sp", [(brow[:], adab_d[l, :, third * D:(third + 1) * D])], writes=[brow])
                        for kc in range(KC):
                            w = wst[wi % 2]
                            wi += 1
                            sc.dma("sp", [(w[:, 0:1024], adaw_d[l, kc * 128:(kc + 1) * 128, third * D:third * D + 1024]),
                                          (w[:, 1024:2048], adaw_d[l, kc * 128:(kc + 1) * 128, third * D + 1024:third * D + 2048])], writes=[w])
                            for j in range(4):
                                op("pe", mm(ps[j][0:1, :], cact[:, kc:kc + 1], w[:, j * 512:(j + 1) * 512], start=(kc == 0), stop=(kc == KC - 1)),
                                   reads=[cact, w], writes=[ps[j]])
                        for j in range(4):
                            c0 = j * 512
                            op("dve", lambda e, j=j, c0=c0: e.tensor_tensor(out=row[0:1, c0:c0 + 512], in0=ps[j][0:1, :], in1=brow[0:1, c0:c0 + 512], op=ALU.add),
                               reads=[ps[j], brow], writes=[row])
                        if third == 2:
                            sc.dma("sp", [(gate_d[l], row[0:1, :])], reads=[row])
                        else:
                            for cidx in range(16):
                                op("pe", mm(ps[6][:, third * 16 + cidx:third * 16 + cidx + 1], row[0:1, cidx * 128:(cidx + 1) * 128], one11[0:1, 0:1]),
                                   reads=[row, ones_f], writes=[ps[6]])
                    ckpt("c2")
                    op("act", lambda e: e.activation(out=modT[:], in_=ps[6][:, 0:32], func=AF.Copy), reads=[ps[6]], writes=[modT])
                    sc.dma("sp", [(nwrow[:], normw_d[l])], writes=[nwrow])
                    op("pe", mm(ps[7][:, 0:16], nwrow[:], ident_f[0:16, 0:16]), reads=[nwrow, ident_f], writes=[ps[7]])
                    op("act", lambda e: e.activation(out=nwT[:], in_=ps[7][:, 0:16], func=AF.Copy), reads=[ps[7]], writes=[nwT])
                    op("dve", lambda e, l=l: e.scalar_tensor_tensor(out=AT[l][:], in0=modT[:, 16:32], scalar=1.0, in1=nwT[:], op0=ALU.add, op1=ALU.mult),
                       reads=[modT, nwT], writes=[AT[l]])
                    op("dve", lambda e, l=l: e.tensor_copy(out=BT[l][:], in_=modT[:, 0:16]), reads=[modT], writes=[BT[l]])
                    ckpt("c3")
                    for i in range(2):
                        sc.dma("sp", [(cwrow[i][:], convw_d[l, i * 96:(i + 1) * 96, :])], writes=[cwrow[i]])
                        op("pe", mm(ps[7][:, 32 + i * 96:32 + (i + 1) * 96], cwrow[i][:], ident_f[0:96, 0:96]), reads=[cwrow[i], ident_f], writes=[ps[7]])
                    op("act", lambda e, l=l: e.activation(out=cw[l][:].rearrange("p a b -> p (a b)"), in_=ps[7][:, 32:32 + 192], func=AF.Copy),
                       reads=[ps[7]], writes=[cw[l]])
            ckpt("pre1")
            with sc.scope() as pre2:
                stf = [pre2.sb("stf%d" % i, [128, KC, SW], F32) for i in range(3)]
                stb = [pre2.sb("stb%d" % i, [128, KC * SW], BF16) for i in range(2)]
                jobs = [(l, s) for l in range(DEPTH) for s in range(NSLAB)]

                def emit_load(i):
                    l, s = jobs[i]
                    f = stf[i % 3]
                    pairs = []
                    for (nm, c0, ncol, dc) in pieces[s]:
                        src = wsrc[nm][l, :, c0:c0 + ncol].rearrange("(k p) c -> p k c", p=128)
                        for q in range(4):
                            pairs.append((f[:, q * 4:(q + 1) * 4, dc:dc + ncol], src[:, q * 4:(q + 1) * 4, :]))
                    wcols = max(dc + ncol for (_, _, ncol, dc) in pieces[s])
                    if wcols < SW:
                        op("pool", lambda e, f=f: e.memset(f[:], 0.0), writes=[f])
                    sc.dma("sp", pairs, writes=[f])

                emit_load(0)
                emit_load(1)
                for i, (l, s) in enumerate(jobs):
                    if i + 2 < len(jobs):
                        emit_load(i + 2)
                    f = stf[i % 3]
                    b = stb[i % 2]
                    ff = f[:].rearrange("p k c -> p (k c)")
                    n = KC * SW
                    a1, a2 = 2000, 3100
                    sc.par([("act", lambda e, b=b, ff=ff: e.activation(out=b[:, 0:a1], in_=ff[:, 0:a1], func=AF.Copy)),
                            ("dve", lambda e, b=b, ff=ff: e.tensor_copy(out=b[:, a1:a2], in_=ff[:, a1:a2])),
                            ("pool", lambda e, b=b, ff=ff: e.tensor_copy(out=b[:, a2:n], in_=ff[:, a2:n]))], reads=[f], writes=[b])
                    sc.dma("sp", [(wb_d[l, s], b[:])], reads=[b])
                    _merge(sc.residue, b.r)
            ckpt("pre2")
            sc._wait("sp", dict(sc.residue))

            wstate = {"i": 0, "issued": 0}

            def issue_slab():
                k = wstate["issued"]
                if k >= len(order):
                    return
                l_, s_ = order[k]
                slot = ring[k % RING]
                half = KC * SW // 2
                dst = slot[:].rearrange("p k c -> p (k c)")
                sc.dma("sp", [(dst[:, 0:half], wb_d[l_, s_, :, 0:half]), (dst[:, half:2 * half], wb_d[l_, s_, :, half:2 * half])], writes=[slot])
                wstate["issued"] += 1

            for _ in range(RING - 1):
                issue_slab()

            def next_slab(l_, s_):
                i = wstate["i"]
                assert order[i] == (l_, s_), (order[i], l_, s_)
                issue_slab()
                wstate["i"] += 1
                return ring[i % RING]

            def rsqrt_small(sco, ss, scale, eps, name):
                lnv = sco.sb(name + "ln", [128, 1], F32)
                r = sco.sb(name + "r", [128, 1], F32)
                op("dve", lambda e: e.tensor_scalar(out=lnv[:], in0=ss[:], scalar1=scale, scalar2=eps, op0=ALU.mult, op1=ALU.add), reads=[ss], writes=[lnv])
                op("act", lambda e: e.activation(out=lnv[:], in_=lnv[:], func=AF.Ln), reads=[lnv], writes=[lnv])
                op("act", lambda e: e.activation(out=r[:], in_=lnv[:], func=AF.Exp, scale=-0.5), reads=[lnv], writes=[r])
                return r

            for t in range(NT):
                t0 = t * T
                for s in range(4):
                    if t > 0:
                        break
                    sc.dma("act", [(xs_[s][:, 0:1024], x_d[t0 + s * 128:t0 + (s + 1) * 128, 0:1024]),
                                   (xs_[s][:, 1024:2048], x_d[t0 + s * 128:t0 + (s + 1) * 128, 1024:2048])], writes=[xs_[s]])
                def emit_rope():
                    with sc.scope() as rs_:
                        pib = rs_.sb("pib", [128, T], I32)
                        ang = rs_.sb("ang", [128, T], F32)
                        u = rs_.sb("u", [128, T], F32)
                        ni = rs_.sb("ni", [128, T], I32)
                        m_ = rs_.sb("m_", [128, T], F32)
                        sc.dma("sp", [(pib[:], pos_d[0:1, t0:t0 + T].partition_broadcast(128))], writes=[pib])
                        op("dve", lambda e: e.tensor_copy(out=ang[:], in_=pib[:]), reads=[pib], writes=[ang])
                        op("dve", lambda e: e.tensor_scalar(out=ang[:], in0=ang[:], scalar1=C["invf"][:, 0:1], scalar2=None, op0=ALU.mult),
                           reads=[ang, C["invf"]], writes=[ang])
                        for (dst, off) in ((sinT, 0.0), (cosT, math.pi / 2)):
                            op("dve", lambda e, off=off: e.tensor_scalar(out=u[:], in0=ang[:], scalar1=off, scalar2=1.0 / TWO_PI, op0=ALU.add, op1=ALU.mult),
                               reads=[ang], writes=[u])
                            op("dve", lambda e: e.tensor_copy(out=ni[:], in_=u[:]), reads=[u], writes=[ni])
                            op("dve", lambda e: e.tensor_copy(out=u[:], in_=ni[:]), reads=[ni], writes=[u])
                            op("dve", lambda e: e.scalar_tensor_tensor(out=u[:], in0=u[:], scalar=-TWO_PI, in1=ang[:], op0=ALU.mult, op1=ALU.add),
                               reads=[u, ang], writes=[u])
                            if off != 0.0:
                                op("dve", lambda e, off=off: e.tensor_scalar(out=u[:], in0=u[:], scalar1=off, scalar2=None, op0=ALU.add), reads=[u], writes=[u])
                            op("dve", lambda e: e.tensor_scalar(out=m_[:], in0=u[:], scalar1=math.pi, scalar2=-TWO_PI, op0=ALU.is_gt, op1=ALU.mult), reads=[u], writes=[m_])
                            op("dve", lambda e: e.tensor_tensor(out=u[:], in0=u[:], in1=m_[:], op=ALU.add), reads=[u, m_], writes=[u])
                            op("dve", lambda e: e.tensor_scalar(out=m_[:], in0=u[:], scalar1=-math.pi, scalar2=TWO_PI, op0=ALU.is_lt, op1=ALU.mult), reads=[u], writes=[m_])
                            op("dve", lambda e: e.tensor_tensor(out=u[:], in0=u[:], in1=m_[:], op=ALU.add), reads=[u, m_], writes=[u])
                            op("act", lambda e, dst=dst: e.activation(out=dst[:], in_=u[:], func=AF.Sin), reads=[u], writes=[dst])

                for l in range(DEPTH):
                    sc.dma("sp", [(bc8k[:, 0:1024], gate_d[l, :, 0:1024].partition_broadcast(128)),
                                  (bc8k[:, 1024:2048], gate_d[l, :, 1024:2048].partition_broadcast(128))], writes=[bc8k])
                    with sc.scope() as p0:
                        junk = p0.sb("junk", [128, D], BF16)
                        ss = p0.sb("ss", [128, 1], F32)
                        dg = p0.sb("dg", [128, 128], F32)
                        tmpm = [p0.sb("tmpm%d" % i, [128, 512], F32) for i in range(2)]
                        for s in range(4):
                            op("act", lambda e, s=s: e.activation(out=junk[:], in_=xs_[s][:], func=AF.Square, accum_out=ss[:, 0:1]), reads=[xs_[s]], writes=[junk, ss])
                            rstd = rsqrt_small(p0, ss, 1.0 / D, 1e-6, "p0")
                            op("dve", lambda e: e.tensor_scalar(out=dg[:], in0=ident_f[:], scalar1=rstd[:, 0:1], scalar2=None, op0=ALU.mult),
                               reads=[ident_f, rstd], writes=[dg])
                            for q4 in range(4):
                                mm_group(ps[q4], [(ps[q4][:, i * 128:(i + 1) * 128], xs_[s][:, (q4 * 4 + i) * 128:(q4 * 4 + i + 1) * 128], dg[:], True, True)
                                                  for i in range(4)], reads=[xs_[s], dg])
                                tm = tmpm[q4 % 2]
                                op("dve", lambda e, q4=q4, tm=tm: e.tensor_tensor(out=v3(tm[:], 4), in0=v3(ps[q4][:], 4), in1=bc_last(AT[l][:, q4 * 4:q4 * 4 + 4], 128), op=ALU.mult),
                                   reads=[ps[q4], AT[l]], writes=[tm])
                                op("pool", lambda e, q4=q4, tm=tm, s=s: e.tensor_tensor(out=hT[:, q4 * 4:q4 * 4 + 4, s * 128:(s + 1) * 128], in0=v3(tm[:], 4),
                                                                                      in1=bc_last(BT[l][:, q4 * 4:q4 * 4 + 4], 128), op=ALU.add),
                                   reads=[tm, BT[l]], writes=[hT])

                    ckpt("p0")
                    if l == 0:
                        emit_rope()
                        ckpt("rope")
                    with sc.scope() as gs:
                        sm = {n: gs.sb("sm_" + n, [128, 4, 16], F32) for n in ("bet", "q1", "gc", "ngc", "kdf", "ngam", "dec0", "dec1", "g", "tmp")}
                        slab = next_slab(l, 24)
                        mm_group(ps[0], [(ps[0][:, b * 32:(b + 1) * 32], hT[:, kc, b * 128:(b + 1) * 128], slab[:, kc, 0:32], kc == 0, kc == KC - 1)
                                         for b in range(4) for kc in range(KC)], reads=[hT, slab])
                        pv = v3(ps[0][:, 0:128], 4)
                        op("act", lambda e: e.activation(out=sm["tmp"][:], in_=pv[:, :, 0:16], func=AF.Exp, scale=-1.0), reads=[ps[0]], writes=[sm["tmp"]])
                        op("dve", lambda e: e.tensor_scalar(out=sm["tmp"][:], in0=sm["tmp"][:], scalar1=1.0, scalar2=None, op0=ALU.add), reads=[sm["tmp"]], writes=[sm["tmp"]])
                        op("dve", lambda e: e.reciprocal(out=sm["bet"][:], in_=sm["tmp"][:]), reads=[sm["tmp"]], writes=[sm["bet"]])
                        op("act", lambda e: e.activation(out=sm["q1"][:], in_=sm["bet"][:], func=AF.Ln), reads=[sm["bet"]], writes=[sm["q1"]])
                        op("dve", lambda e: e.tensor_tensor(out=sm["g"][:], in0=pv[:, :, 16:32], in1=bc_mid(dtb[l][:, :], 4), op=ALU.add), reads=[ps[0], dtb[l]], writes=[sm["g"]])
                        op("act", lambda e: e.activation(out=sm["g"][:], in_=sm["g"][:], func=AF.Exp), reads=[sm["g"]], writes=[sm["g"]])
                        op("act", lambda e: e.activation(out=sm["g"][:], in_=sm["g"][:], func=AF.Ln, bias=1.0), reads=[sm["g"]], writes=[sm["g"]])
                        op("dve", lambda e: e.tensor_tensor(out=sm["g"][:], in0=sm["g"][:], in1=bc_mid(negA[l][:, :], 4), op=ALU.mult), reads=[sm["g"], negA[l]], writes=[sm["g"]])
                        flat = lambda n: sm[n][:].rearrange("p a b -> p (a b)")
                        op("pe", mm(ps[1][:, 0:64], C["tri"][:], flat("g")), reads=[C["tri"], sm["g"]], writes=[ps[1]])
                        op("dve", lambda e: e.tensor_copy(out=flat("gc"), in_=ps[1][:, 0:64]), reads=[ps[1]], writes=[sm["gc"]])
                        op("dve", lambda e: e.tensor_scalar(out=flat("ngc"), in0=ps[1][:, 0:64], scalar1=-1.0, scalar2=None, op0=ALU.mult), reads=[ps[1]], writes=[sm["ngc"]])
                        op("dve", lambda e: e.tensor_tensor(out=flat("q1"), in0=flat("q1"), in1=flat("ngc"), op=ALU.add), reads=[sm["q1"], sm["ngc"]], writes=[sm["q1"]])
                        op("act", lambda e: e.activation(out=flat("ngam"), in_=flat("gc"), func=AF.Exp), reads=[sm["gc"]], writes=[sm["ngam"]])
                        op("dve", lambda e: e.tensor_scalar(out=flat("ngam"), in0=flat("ngam"), scalar1=-1.0, scalar2=None, op0=ALU.mult), reads=[sm["ngam"]], writes=[sm["ngam"]])
                        op("pe", mm(ps[2][:, 0:64], C["lsel"][:], flat("gc")), reads=[C["lsel"], sm["gc"]], writes=[ps[2]])
                        op("pe", mm(ps[2][:, 64:128], C["l0"][:], flat("gc")), reads=[C["l0"], sm["gc"]], writes=[ps[2]])
                        op("pe", mm(ps[2][:, 128:192], C["l1"][:], flat("gc")), reads=[C["l1"], sm["gc"]], writes=[ps[2]])
                        op("dve", lambda e: e.tensor_tensor(out=flat("kdf"), in0=ps[2][:, 0:64], in1=flat("ngc"), op=ALU.add), reads=[ps[2], sm["ngc"]], writes=[sm["kdf"]])
                        op("act", lambda e: e.activation(out=flat("kdf"), in_=flat("kdf"), func=AF.Exp), reads=[sm["kdf"]], writes=[sm["kdf"]])
                        op("act", lambda e: e.activation(out=flat("dec0"), in_=ps[2][:, 64:128], func=AF.Exp), reads=[ps[2]], writes=[sm["dec0"]])
                        op("act", lambda e: e.activation(out=flat("dec1"), in_=ps[2][:, 128:192], func=AF.Exp), reads=[ps[2]], writes=[sm["dec1"]])

                        ckpt("ba")
                        qT = gs.sb("qT", [128, 8, T], BF16)
                        kT = gs.sb("kT", [128, 8, T], BF16)
                        vT = gs.sb("vT", [128, 8, T], BF16)
                        for hf in range(2):
                            with sc.scope() as p1:
                                psets = []
                                for k_ in range(2):
                                    psets.append(dict(xsb=[p1.sb("xsb", [128, T + 3], F32) for i in range(2)], acca=[p1.sb("acca", [128, T], F32) for i in range(2)],
                                                      svb=[p1.sb("svb", [128, T], F32) for i in range(2)], sqb=[p1.sb("sqb", [128, T], BF16) for i in range(2)],
                                                      rnb=[p1.sb("rnb", [128, T], F32) for i in range(2)], pb=[ps[2 * k_], ps[2 * k_ + 1]], sb_=[ps[4 + 2 * k_], ps[5 + 2 * k_]]))
                                chunks = [(kind, hl) for kind in range(3) for hl in range(8)]

                                def p1_pair(pi_, B_):
                                    kind = chunks[2 * pi_][0]
                                    h_first = hf * 8 + chunks[2 * pi_][1]
                                    slab = next_slab(l, kind * 8 + h_first // 2)
                                    info = []
                                    for sub_i in range(2):
                                        hl = chunks[2 * pi_ + sub_i][1]
                                        h = hf * 8 + hl
                                        sub = h % 2
                                        ci = kind * 16 + h
                                        bank = B_["pb"][sub_i]
                                        mm_group(bank, [(bank[:], slab[:, kc, sub * 128:(sub + 1) * 128], hT[:, kc, :], kc == 0, kc == KC - 1) for kc in range(KC)],
                                                 reads=[slab, hT])
                                        info.append((hl, ci, bank))
                                    yield
                                    for sub_i, (hl, ci, bank) in enumerate(info):
                                        xb_ = B_["xsb"][sub_i]
                                        sc.par([("pool", lambda e, xb_=xb_, ci=ci: e.tensor_copy(out=xb_[:, 0:3], in_=cs[l][:, ci, :])),
                                                ("act", lambda e, xb_=xb_, bank=bank: e.activation(out=xb_[:, 3:T + 3], in_=bank[:], func=AF.Copy))],
                                               reads=[cs[l], bank], writes=[xb_])
                                        op("pool", lambda e, xb_=xb_, ci=ci: e.tensor_copy(out=cs[l][:, ci, :], in_=xb_[:, T:T + 3]), reads=[xb_], writes=[cs[l]])
                                    yield
                                    for sub_i, (hl, ci, bank) in enumerate(info):
                                        xb_, aa = B_["xsb"][sub_i], B_["acca"][sub_i]
                                        op("dve", lambda e, xb_=xb_, aa=aa, ci=ci: e.tensor_scalar(out=aa[:], in0=xb_[:, 0:T], scalar1=cw[l][:, 0, ci:ci + 1], scalar2=None, op0=ALU.mult),
                                           reads=[xb_, cw[l]], writes=[aa])
                                        for tap in (1, 2, 3):
                                            op("dve", lambda e, xb_=xb_, aa=aa, ci=ci, tap=tap: e.scalar_tensor_tensor(out=aa[:], in0=xb_[:, tap:T + tap], scalar=cw[l][:, tap, ci:ci + 1], in1=aa[:],
                                                                                                                 op0=ALU.mult, op1=ALU.add), reads=[xb_, cw[l], aa], writes=[aa])
                                    yield
                                    if kind == 2:
                                        for sub_i, (hl, ci, bank) in enumerate(info):
                                            aa = B_["acca"][sub_i]
                                            op("act", lambda e, aa=aa, hl=hl: e.activation(out=vT[:, hl, :], in_=aa[:], func=AF.Silu), reads=[aa], writes=[vT])
                                        return
                                    for sub_i, (hl, ci, bank) in enumerate(info):
                                        aa, sv, sq = B_["acca"][sub_i], B_["svb"][sub_i], B_["sqb"][sub_i]
                                        op("act", lambda e, aa=aa, sv=sv: e.activation(out=sv[:], in_=aa[:], func=AF.Silu), reads=[aa], writes=[sv])
                                        op("act", lambda e, sv=sv, sq=sq: e.activation(out=sq[:], in_=sv[:], func=AF.Square), reads=[sv], writes=[sq])
                                        bank2 = B_["sb_"][sub_i]
                                        op("pe", mm(bank2[:], ones_b[:], sq[:]), reads=[ones_b, sq], writes=[bank2])
                                    yield
                                    dst = qT if kind == 0 else kT
                                    scl = 128.0 ** -0.5 if kind == 0 else 1.0
                                    for sub_i, (hl, ci, bank) in enumerate(info):
                                        sv, rn, bank2 = B_["svb"][sub_i], B_["rnb"][sub_i], B_["sb_"][sub_i]
                                        op("act", lambda e, bank2=bank2, rn=rn: e.activation(out=rn[:], in_=bank2[:], func=AF.Ln, bias=1e-6), reads=[bank2], writes=[rn])
                                        op("act", lambda e, rn=rn: e.activation(out=rn[:], in_=rn[:], func=AF.Exp, scale=-0.5, bias=math.log(scl)), reads=[rn], writes=[rn])
                                        op("pool", lambda e, sv=sv, rn=rn, hl=hl: e.tensor_tensor(out=dst[:, hl, :], in0=sv[:], in1=rn[:], op=ALU.mult), reads=[sv, rn], writes=[dst])

                                for pp in range(6):
                                    alive = [p1_pair(2 * pp, psets[0]), p1_pair(2 * pp + 1, psets[1])]
                                    while alive:
                                        for gen in list(alive):
                                            try:
                                                next(gen)
                                            except StopIteration:
                                                alive.remove(gen)
                            ckpt("p1")
                            tg = sc.scope()
                            tg.__enter__()
                            TT = tg.sb("TT", [128, 4, 8, 128], BF16)
                            AqT = tg.sb("AqT", [128, 4, 8, 128], BF16)
                            with sc.scope() as g1:
                                def msk(dst, src, m):
                                    op("pool", lambda e: e.tensor_tensor(out=v3(dst[:], 4), in0=v3(src[:], 4), in1=bc_mid(C[m][:, :], 4), op=ALU.mult), reads=[src, C[m]], writes=[dst])

                                def ev_copy(dst, bank):
                                    op("act", lambda e: e.activation(out=dst[:], in_=bank[:], func=AF.Copy), reads=[bank], writes=[dst])

                                def ev_add(dst, bank, addend):
                                    op("dve", lambda e: e.tensor_tensor(out=dst[:], in0=bank[:], in1=addend[:], op=ALU.add), reads=[bank, addend], writes=[dst])

                                gsets = []
                                for gi in range(2):
                                    gsets.append(dict(
                                        RD1=g1.sb("RD1", [128, 512], F32), RD2=g1.sb("RD2", [128, 512], F32),
                                        gbc=g1.sb("gbc", [128, 512], BF16), E2T=g1.sb("E2T", [128, 512], BF16),
                                        E1T=g1.sb("E1T", [128, 512], BF16), E1=g1.sb("E1", [128, 512], BF16),
                                        Bf=[g1.sb("Bf%d" % i, [128, 512], BF16) for i in range(8)],
                                        banks=ps[4 * gi:4 * gi + 4], ctr=[0]))

                                def g1_group(b, j, G):
                                    RD1, RD2, gbc, E2T, E1T, E1, Bf = G["RD1"], G["RD2"], G["gbc"], G["E2T"], G["E1T"], G["E1"], G["Bf"]

                                    def nb():
                                        bank = G["banks"][G["ctr"][0] % 4]
                                        G["ctr"][0] += 1
                                        return bank

                                    def mm4(lhs, rhs):
                                        bank = nb()
                                        mm_group(bank, [(bank[:, hh * 128:(hh + 1) * 128], lhs[:, hh * 128:(hh + 1) * 128], rhs[:, hh * 128:(hh + 1) * 128], True, True) for hh in range(4)],
                                                 reads=[lhs, rhs])
                                        return bank
                                    bs = slice(b * 128, (b + 1) * 128)
                                    h0 = hf * 8 + j * 4
                                    hl0 = j * 4
                                    op("dve", lambda e: e.tensor_tensor(out=v3(RD1[:], 4), in0=bc_mid(ident_f[:, :], 4), in1=bc_last(sm["gc"][:, b, h0:h0 + 4], 128), op=ALU.mult),
                                       reads=[ident_f, sm["gc"]], writes=[RD1])
                                    op("pool", lambda e: e.tensor_tensor(out=v3(RD2[:], 4), in0=bc_mid(ident_f[:, :], 4), in1=bc_last(sm["q1"][:, b, h0:h0 + 4], 128), op=ALU.mult),
                                       reads=[ident_f, sm["q1"]], writes=[RD2])
                                    pG, pH, pI = nb(), nb(), nb()
                                    op("pe", mm(pG[:], ones_f[:], RD1[:], True, False), reads=[ones_f, RD1], writes=[pG])
                                    mm_group(pH, [(pH[:], ones_f[:], RD1[:], True, False), (pH[:], ident_b[:], C["mST4"][:], False, True)], reads=[ones_f, RD1, ident_b, C["mST4"]])
                                    mm_group(pI, [(pI[:], ones_f[:], RD2[:], True, False), (pI[:], ident_b[:], C["mS4"][:], False, True)], reads=[ones_f, RD2, ident_b, C["mS4"]])
                                    yield
                                    op("act", lambda e: e.activation(out=gbc[:], in_=pG[:], func=AF.Exp), reads=[pG], writes=[gbc])
                                    op("pe", mm(pG[:], ident_b[:], C["mIT4"][:], False, True), reads=[ident_b, C["mIT4"]], writes=[pG])
                                    for (bank, Et, bias) in ((pH, E1T, "q1"), (pI, E1, "gc")):
                                        for hh in range(4):
                                            op("act", lambda e, bank=bank, Et=Et, bias=bias, hh=hh: e.activation(
                                                out=Et[:, hh * 128:(hh + 1) * 128], in_=bank[:, hh * 128:(hh + 1) * 128], func=AF.Exp, bias=sm[bias][:, b, h0 + hh:h0 + hh + 1]),
                                               reads=[bank, sm[bias]], writes=[Et])
                                    pK = nb()
                                    mm_group(pK, [(pK[:, hh * 128:(hh + 1) * 128], kT[:, hl0 + hh, bs], kT[:, hl0 + hh, bs], True, True) for hh in range(4)], reads=[kT])
                                    yield
                                    for hh in range(4):
                                        op("act", lambda e, hh=hh: e.activation(out=E2T[:, hh * 128:(hh + 1) * 128], in_=pG[:, hh * 128:(hh + 1) * 128], func=AF.Exp,
                                                                              bias=sm["ngc"][:, b, h0 + hh:h0 + hh + 1]), reads=[pG, sm["ngc"]], writes=[E2T])
                                    P0, P0T = Bf[0], Bf[1]
                                    op("dve", lambda e: e.scalar_tensor_tensor(out=P0T[:], in0=pK[:], scalar=-1.0, in1=E1T[:], op0=ALU.mult, op1=ALU.mult), reads=[pK, E1T], writes=[P0T])
                                    op("dve", lambda e: e.scalar_tensor_tensor(out=P0[:], in0=pK[:], scalar=-1.0, in1=E1[:], op0=ALU.mult, op1=ALU.mult), reads=[pK, E1], writes=[P0])
                                    M, MT, M2, M2T, Y1, Y1T = Bf[2], Bf[3], Bf[4], Bf[5], Bf[6], Bf[7]
                                    msk(M, P0, "hm0")
                                    msk(MT, P0T, "hm0")
                                    yield
                                    pQ = nb()
                                    mm_group(pQ, [(pQ[:, hh * 128:(hh + 1) * 128], kT[:, hl0 + hh, bs], qT[:, hl0 + hh, bs], True, True) for hh in range(4)], reads=[kT, qT])
                                    bA = mm4(MT, M)
                                    bB = mm4(M, MT)
                                    yield
                                    op("dve", lambda e: e.tensor_tensor(out=AqT[:, b, hl0:hl0 + 4, :], in0=v3(pQ[:], 4), in1=v3(E2T[:], 4), op=ALU.mult), reads=[pQ, E2T], writes=[AqT])
                                    op("pool", lambda e: e.tensor_tensor(out=qT[:, hl0:hl0 + 4, bs], in0=qT[:, hl0:hl0 + 4, bs], in1=v3(gbc[:], 4), op=ALU.mult), reads=[qT, gbc], writes=[qT])
                                    ev_copy(M2, bA)
                                    op("dve", lambda e: e.tensor_copy(out=M2T[:], in_=bB[:]), reads=[bB], writes=[M2T])
                                    Y0, Y0T = M, MT
                                    op("pool", lambda e: e.tensor_tensor(out=Y0[:], in0=M[:], in1=C["ident4_b"][:], op=ALU.add), reads=[M, C["ident4_b"]], writes=[Y0])
                                    op("pool", lambda e: e.tensor_tensor(out=Y0T[:], in0=MT[:], in1=C["ident4_b"][:], op=ALU.add), reads=[MT, C["ident4_b"]], writes=[Y0T])
                                    yield
                                    bA = mm4(M2T, Y0)
                                    bB = mm4(M2, Y0T)
                                    bC = mm4(M2T, M2)
                                    bD = mm4(M2, M2T)
                                    yield
                                    ev_add(Y1, bA, Y0)
                                    ev_add(Y1T, bB, Y0T)
                                    M4, M4T = Bf[2], Bf[3]
                                    ev_copy(M4, bC)
                                    ev_copy(M4T, bD)
                                    yield
                                    bA = mm4(M4T, Y1)
                                    bB = mm4(M4, Y1T)
                                    yield
                                    Tm, Um = Bf[4], Bf[5]
                                    ev_add(Tm, bA, Y1)
                                    ev_add(Um, bB, Y1T)
                                    for lv in (1, 2):
                                        Pl, PlT, Wp, W = Bf[6], Bf[7], Bf[2], Bf[3]
                                        msk(Pl, P0, "hm%d" % lv)
                                        msk(PlT, P0T, "hm%d" % lv)
                                        yield
                                        bA = mm4(PlT, Tm)
                                        bB = mm4(Pl, Um)
                                        yield
                                        ev_copy(Wp, bA)
                                        op("dve", lambda e, W=W, bB=bB: e.tensor_copy(out=W[:], in_=bB[:]), reads=[bB], writes=[W])
                                        yield
                                        bA = mm4(Um, Wp)
                                        bB = mm4(Tm, W)
                                        yield
                                        ev_add(Tm, bA, Tm)
                                        ev_add(Um, bB, Um)
                                    Pl, W = Bf[6], Bf[2]
                                    msk(Pl, P0, "hm3")
                                    yield
                                    bA = mm4(Pl, Um)
                                    yield
                                    ev_copy(W, bA)
                                    yield
                                    bank = mm4(Tm, W)
                                    yield
                                    op("dve", lambda e: e.tensor_tensor(out=TT[:, b, hl0:hl0 + 4, :], in0=v3(bank[:], 4), in1=v3(Um[:], 4), op=ALU.add), reads=[bank, Um], writes=[TT])

                                run_window([(lambda k, b=b, j=j: g1_group(b, j, gsets[k])) for b in range(4) for j in range(2)], 2)
                            ckpt("g1")
                            with sc.scope() as g2:
                                Sbf = [g2.sb("Sbf%d" % j, [128, 512], BF16) for j in range(2)]
                                kdec = [g2.sb("kdec%d" % j, [128, 512], BF16) for j in range(2)]
                                vtok = [g2.sb("vtok%d" % j, [128, 512], BF16) for j in range(2)]
                                tmpr = [g2.sb("tmpr%d" % j, [128, 512], F32) for j in range(2)]
                                Rp = [g2.sb("Rp%d" % j, [128, 512], BF16) for j in range(2)]
                                Vn = [g2.sb("Vn%d" % j, [128, 512], BF16) for j in range(2)]
                                tmpS = [g2.sb("tmpS%d" % j, [128, 512], F32) for j in range(2)]
                                sqo = [g2.sb("sqo%d" % j, [128, 512], BF16) for j in range(2)]
                                rno = [g2.sb("rno%d" % j, [128, 512], F32) for j in range(2)]
                                for j in range(2):
                                    op("pool", lambda e, j=j: e.memset(Rp[j][:], 0.0), writes=[Rp[j]])
                                    op("pool", lambda e, j=j: e.memset(Vn[j][:], 0.0), writes=[Vn[j]])
                                St = [Sst[l][hf * 2 + j] for j in range(2)]
                                for j in range(2):
                                    op("act", lambda e, j=j: e.activation(out=Sbf[j][:], in_=St[j][:], func=AF.Copy), reads=[St[j]], writes=[Sbf[j]])
                                psA, psB, psC, OT = ps[0:2], ps[2:4], ps[4:6], ps[6:8]
                                for b in range(4):
                                    bs = slice(b * 128, (b + 1) * 128)
                                    for j in range(2):
                                        h0 = hf * 8 + j * 4
                                        hl0 = j * 4
                                        mm_group(psA[j], [(psA[j][:, hh * 128:(hh + 1) * 128], kT[:, hl0 + hh, bs], ident_b[:], True, True) for hh in range(4)], reads=[kT, ident_b])
                                        op("dve", lambda e, j=j, b=b, h0=h0: e.tensor_tensor(out=v3(kdec[j][:], 4), in0=v3(psA[j][:], 4), in1=bc_last(sm["kdf"][:, b, h0:h0 + 4], 128), op=ALU.mult),
                                           reads=[psA[j], sm["kdf"]], writes=[kdec[j]])
                                        mm_group(psB[j], [(psB[j][:, hh * 128:(hh + 1) * 128], vT[:, hl0 + hh, bs], ident_b[:], True, True) for hh in range(4)], reads=[vT, ident_b])
                                        op("act", lambda e, j=j: e.activation(out=vtok[j][:], in_=psB[j][:], func=AF.Copy), reads=[psB[j]], writes=[vtok[j]])
                                    for c in range(2):
                                        r0, r1 = 64 * c, 64 * c + 64
                                        decn = "dec%d" % c
                                        for j in range(2):
                                            hl0 = j * 4
                                            mm_group(psA[j], [(psA[j][:, hh * 128:(hh + 1) * 128], kT[:, hl0 + hh, bs], Sbf[j][:, hh * 128:(hh + 1) * 128], True, True) for hh in range(4)],
                                                     reads=[kT, Sbf[j]])
                                            mm_group(OT[j], [(OT[j][:, hh * 128 + r0:hh * 128 + r1], Sbf[j][:, hh * 128:(hh + 1) * 128], qT[:, hl0 + hh, b * 128 + r0:b * 128 + r1], (c == 0 and hh == 0), False)
                                                             for hh in range(4)], reads=[Sbf[j], qT])
                                        for j in range(2):
                                            h0 = hf * 8 + j * 4
                                            def rp_fn(e, j=j, b=b, h0=h0):
                                                inst = None
                                                for hh in range(4):
                                                    inst = e.scalar_tensor_tensor(out=Rp[j][r0:r1, hh * 128:(hh + 1) * 128], in0=psA[j][r0:r1, hh * 128:(hh + 1) * 128],
                                                                                  scalar=sm["ngam"][r0:r1, b, h0 + hh:h0 + hh + 1], in1=vtok[j][r0:r1, hh * 128:(hh + 1) * 128],
                                                                                  op0=ALU.mult, op1=ALU.add)
                                                    sc.ninst += 1
                                                return inst
                                            op("dve", rp_fn, reads=[psA[j], sm["ngam"], vtok[j]], writes=[Rp[j]])
                                        for j in range(2):
                                            hl0 = j * 4
                                            mm_group(psB[j], [(psB[j][:, hh * 128:(hh + 1) * 128], TT[r0:r1, b, hl0 + hh, :], Rp[j][r0:r1, hh * 128:(hh + 1) * 128], True, True) for hh in range(4)],
                                                     reads=[TT, Rp[j]])
                                        for j in range(2):
                                            h0 = hf * 8 + j * 4
                                            op("dve", lambda e, j=j, b=b, h0=h0: e.tensor_tensor(out=v3(Vn[j][r0:r1, :], 4), in0=v3(psB[j][r0:r1, :], 4),
                                                                                              in1=sm["bet"][r0:r1, b, h0:h0 + 4].unsqueeze(2).broadcast_to([64, 4, 128]), op=ALU.mult),
                                               reads=[psB[j], sm["bet"]], writes=[Vn[j]])
                                        for j in range(2):
                                            mm_group(psC[j], [(psC[j][:, hh * 128:(hh + 1) * 128], kdec[j][r0:r1, hh * 128:(hh + 1) * 128], Vn[j][r0:r1, hh * 128:(hh + 1) * 128], True, True)
                                                              for hh in range(4)], reads=[kdec[j], Vn[j]])
                                        for j in range(2):
                                            h0 = hf * 8 + j * 4
                                            def s_fn(e, j=j, b=b, h0=h0, decn=decn):
                                                inst = None
                                                for hh in range(4):
                                                    inst = e.scalar_tensor_tensor(out=St[j][:, hh * 128:(hh + 1) * 128], in0=St[j][:, hh * 128:(hh + 1) * 128],
                                                                                  scalar=sm[decn][:, b, h0 + hh:h0 + hh + 1], in1=psC[j][:, hh * 128:(hh + 1) * 128],
                                                                                  op0=ALU.mult, op1=ALU.add)
                                                    sc.ninst += 1
                                                return inst
                                            def sbf_fn(e, j=j, b=b, h0=h0, decn=decn):
                                                inst = None
                                                for hh in range(4):
                                                    inst = e.scalar_tensor_tensor(out=Sbf[j][:, hh * 128:(hh + 1) * 128], in0=St[j][:, hh * 128:(hh + 1) * 128],
                                                                                  scalar=sm[decn][:, b, h0 + hh:h0 + hh + 1], in1=psC[j][:, hh * 128:(hh + 1) * 128],
                                                                                  op0=ALU.mult, op1=ALU.add)
                                                    sc.ninst += 1
                                                return inst
                                            op("dve", sbf_fn, reads=[St[j], psC[j], sm[decn]], writes=[Sbf[j]])
                                            op("dve", s_fn, reads=[psC[j], sm[decn]], writes=[St[j]])
                                        ckpt("g2c%d" % c)
                                    for j in range(2):
                                        hl0 = j * 4
                                        h0 = hf * 8 + j * 4
                                        mm_group(OT[j], [(OT[j][:, hh * 128:(hh + 1) * 128], Vn[j][:, hh * 128:(hh + 1) * 128], AqT[:, b, hl0 + hh, :], False, True) for hh in range(4)],
                                                 reads=[Vn[j], AqT])
                                        if j == 0:
                                            ckpt("g2o")
                                        op("act", lambda e, j=j: e.activation(out=sqo[j][:], in_=OT[j][:], func=AF.Square), reads=[OT[j]], writes=[sqo[j]])
                                        op("pe", mm(psC[j][:], ones_b[:], sqo[j][:]), reads=[ones_b, sqo[j]], writes=[psC[j]])
                                        op("act", lambda e, j=j: e.activation(out=rno[j][:], in_=psC[j][:], func=AF.Ln, scale=1.0 / 128.0, bias=1e-6), reads=[psC[j]], writes=[rno[j]])
                                        op("act", lambda e, j=j: e.activation(out=rno[j][:], in_=rno[j][:], func=AF.Exp, scale=-0.5), reads=[rno[j]], writes=[rno[j]])
                                        op("dve", lambda e, j=j, h0=h0, bs=bs: e.tensor_tensor(out=oaT[:, h0:h0 + 4, bs], in0=v3(OT[j][:], 4), in1=v3(rno[j][:], 4), op=ALU.mult),
                                           reads=[OT[j], rno[j]], writes=[oaT])
                            tg.__exit__(None, None, None)
                        ckpt("g2")
                        with sc.scope() as p4:
                            szs = [p4.sb("sz%d" % i, [128, T], BF16) for i in range(2)]
                            for h in range(16):
                                if h % 2 == 0:
                                    slab = next_slab(l, 25 + h // 2)
                                bank = ps[h % 4]
                                sz = szs[h % 2]
                                mm_group(bank, [(bank[:], slab[:, kc, (h % 2) * 128:(h % 2 + 1) * 128], hT[:, kc, :], kc == 0, kc == KC - 1) for kc in range(KC)], reads=[slab, hT])
                                op("act", lambda e, bank=bank, sz=sz: e.activation(out=sz[:], in_=bank[:], func=AF.Silu), reads=[bank], writes=[sz])
                                op("dve", lambda e, h=h, sz=sz: e.scalar_tensor_tensor(out=oaT[:, h, :], in0=oaT[:, h, :], scalar=gnw[l][:, 0:1], in1=sz[:], op0=ALU.mult, op1=ALU.mult),
                                   reads=[oaT, gnw[l], sz], writes=[oaT])

                    ckpt("p4")
                    with sc.scope() as ws:
                        obT = ws.sb("obT", [128, KC, T], BF16)
                        with sc.scope() as sw:
                            qbT = sw.sb("qbT", [128, 4, T], BF16)
                            szb = sw.sb("szb", [128, 4, T], BF16)
                            kdT = [sw.sb("kdT%d" % i, [128, 128 + T], BF16) for i in range(2)]
                            Vpad = sw.sb("Vpad", [128, 5, 1024], BF16)
                            vp = lambda slot: Vpad[:, slot, :].rearrange("p (g v d) -> p g v d", g=4, v=2)
                            q16 = sw.sb("q16", [128, T], BF16)
                            t1 = sw.sb("t1", [128, T], F32)
                            t2 = sw.sb("t2", [128, T], F32)
                            sset = []
                            for si_ in range(4):
                                sset.append(dict(Pm=sw.sb("Pm", [128, 4, 256], BF16), PTs=sw.sb("PTs", [128, 8, 128], BF16), Dg=sw.sb("Dg", [128, 4, 128], BF16),
                                                 mx=sw.sb("mx", [128, 4], F32), negm=sw.sb("negm", [128, 4], F32), rsum=sw.sb("rsum", [128, 4], F32),
                                                 esk=sw.sb("esk", [128, 4], F32), bk=[ps[2 * si_], ps[2 * si_ + 1]]))
                            for sl_ in range(5):
                                op("pool", lambda e, sl_=sl_: e.memset(Vpad[:, sl_, :], 0.0), writes=[Vpad])

                            ckpt("s0a")
                            def rope_chunk(slab, sub, dst_ap, dst_tn):
                                bank, bank2 = ps[0], ps[1]
                                mm_group(bank, [(bank[:], slab[:, kc, sub * 128:(sub + 1) * 128], hT[:, kc, :], kc == 0, kc == KC - 1) for kc in range(KC)], reads=[slab, hT])
                                op("act", lambda e: e.activation(out=q16[:], in_=bank[:], func=AF.Copy), reads=[bank], writes=[q16])
                                ckpt("s0b")
                                op("pe", mm(bank2[:], C["rotT"][:], q16[:]), reads=[C["rotT"], q16], writes=[bank2])
                                ckpt("s0c")
                                op("dve", lambda e: e.tensor_tensor(out=t1[:], in0=bank[:], in1=cosT[:], op=ALU.mult), reads=[bank, cosT], writes=[t1])
                                op("dve", lambda e: e.tensor_tensor(out=t2[:], in0=bank2[:], in1=sinT[:], op=ALU.mult), reads=[bank2, sinT], writes=[t2])
                                ckpt("s0d")
                                op("pool", lambda e: e.tensor_tensor(out=dst_ap, in0=t1[:], in1=t2[:], op=ALU.add), reads=[t1, t2], writes=[dst_tn])

                            for g in range(4):
                                for a in range(4):
                                    pa_ = 4 * g + a
                                    if pa_ % 2 == 0:
                                        slab = next_slab(l, 33 + pa_ // 2)
                                    rope_chunk(slab, pa_ % 2, qbT[:, a, :], qbT)
                                ckpt("s1")
                                if g % 2 == 0:
                                    slab = next_slab(l, 41 + g // 2)
                                    for gp in range(2):
                                        gg = g + gp
                                        op("pool", lambda e, gp=gp, gg=gg: e.tensor_copy(out=kdT[gp][:, 0:128], in_=kh[l][:, gg, :]), reads=[kh[l]], writes=[kdT[gp]])
                                        rope_chunk(slab, gp, kdT[gp][:, 128:128 + T], kdT[gp])
                                        op("pool", lambda e, gp=gp, gg=gg: e.tensor_copy(out=kh[l][:, gg, :], in_=kdT[gp][:, T:T + 128]), reads=[kdT[gp]], writes=[kh[l]])
                                ckpt("s2")
                                if g == 0:
                                    slab = next_slab(l, 43)
                                    for var in range(2):
                                        op("pool", lambda e, var=var: e.tensor_copy(out=vp(0)[:, :, var, var * 64:(var + 1) * 64], in_=vh[l][:, :, :]), reads=[vh[l]], writes=[Vpad])
                                    for b2 in range(2):
                                        bank = ps[2 + b2]
                                        mm_group(bank, [(bank[:, bb * 256:(bb + 1) * 256], hT[:, kc, (2 * b2 + bb) * 128:(2 * b2 + bb + 1) * 128], slab[:, kc, :], kc == 0, kc == KC - 1)
                                                        for bb in range(2) for kc in range(KC)], reads=[hT, slab])
                                        for bb in range(2):
                                            blk = 2 * b2 + bb
                                            for var in range(2):
                                                op("act", lambda e, bank=bank, bb=bb, blk=blk, var=var: e.activation(
                                                    out=vp(1 + blk)[:, :, var, var * 64:(var + 1) * 64], in_=bank[:, bb * 256:(bb + 1) * 256].rearrange("p (g d) -> p g d", g=4), func=AF.Copy),
                                                   reads=[bank], writes=[Vpad])
                                            if blk == 3:
                                                op("act", lambda e, bank=bank, bb=bb: e.activation(out=vh[l][:, :, :], in_=bank[:, bb * 256:(bb + 1) * 256].rearrange("p (g d) -> p g d", g=4), func=AF.Copy),
                                                   reads=[bank], writes=[vh[l]])
                                ckpt("s3")
                                for a in range(4):
                                    pa_ = 4 * g + a
                                    if pa_ % 2 == 0:
                                        slab = next_slab(l, 44 + pa_ // 2)
                                    bank = ps[4 + a % 2]
                                    mm_group(bank, [(bank[:], slab[:, kc, (pa_ % 2) * 128:(pa_ % 2 + 1) * 128], hT[:, kc, :], kc == 0, kc == KC - 1) for kc in range(KC)], reads=[slab, hT])
                                    op("act", lambda e, bank=bank, a=a: e.activation(out=szb[:, a, :], in_=bank[:], func=AF.Silu), reads=[bank], writes=[szb])
                                ckpt("s4")
                                kd = kdT[g % 2]

                                def swa_stream(g, b, quad, B_):
                                    Pm, PTs, Dg, mx, negm, rsum, esk, bk = (B_[k_] for k_ in ("Pm", "PTs", "Dg", "mx", "negm", "rsum", "esk", "bk"))
                                    mask = C["mB0"] if (t == 0 and b == 0) else C["mB"]
                                    bsl = slice(b * 128, (b + 1) * 128)
                                    s0 = 8 * g + 4 * quad
                                    for i in range(4):
                                        hh = 4 * quad + i
                                        a, par = hh // 2, hh % 2
                                        bank = bk[i // 2]
                                        o_ = bank[:, (i % 2) * 256:(i % 2 + 1) * 256]
                                        mm_group(bank, [(o_, qbT[par * 64:(par + 1) * 64, a, bsl], kd[par * 64:(par + 1) * 64, b * 128:b * 128 + 256], True, False),
                                                        (o_, ident_b[:], mask[:], False, True)], reads=[qbT, kd, ident_b, mask])
                                    yield
                                    for q in range(2):
                                        op("dve", lambda e, q=q: e.tensor_reduce(out=mx[:, 2 * q:2 * q + 2], in_=v3(bk[q][:], 2), axis=AX.X, op=ALU.max), reads=[bk[q]], writes=[mx])
                                    op("dve", lambda e: e.scalar_tensor_tensor(out=negm[:], in0=mx[:], scalar=0.125, in1=sinkb[l][:, s0:s0 + 4], op0=ALU.mult, op1=ALU.max),
                                       reads=[mx, sinkb[l]], writes=[negm])
                                    op("dve", lambda e: e.tensor_scalar(out=negm[:], in0=negm[:], scalar1=-1.0, scalar2=None, op0=ALU.mult), reads=[negm], writes=[negm])
                                    yield
                                    for i in range(4):
                                        op("act", lambda e, i=i: e.activation(out=Pm[:, i, :], in_=bk[i // 2][:, (i % 2) * 256:(i % 2 + 1) * 256], func=AF.Exp,
                                                                            bias=negm[:, i:i + 1], scale=0.125, accum_out=rsum[:, i:i + 1]),
                                           reads=[bk[i // 2], negm], writes=[Pm, rsum])
                                    op("dve", lambda e: e.tensor_tensor(out=esk[:], in0=sinkb[l][:, s0:s0 + 4], in1=negm[:], op=ALU.add), reads=[sinkb[l], negm], writes=[esk])
                                    op("act", lambda e: e.activation(out=esk[:], in_=esk[:], func=AF.Exp), reads=[esk], writes=[esk])
                                    yield
                                    op("dve", lambda e: e.tensor_tensor(out=esk[:], in0=esk[:], in1=rsum[:], op=ALU.add), reads=[esk, rsum], writes=[esk])
                                    op("dve", lambda e: e.reciprocal(out=esk[:], in_=esk[:]), reads=[esk], writes=[esk])
                                    op("dve", lambda e: e.tensor_tensor(out=Dg[:], in0=bc_mid(ident_b[:, :], 4), in1=bc_last(esk[:, 0:4], 128), op=ALU.mult), reads=[ident_b, esk], writes=[Dg])
                                    yield
                                    for q in range(2):
                                        bank = bk[q]
                                        items = []
                                        for i4 in range(4):
                                            idx = q * 4 + i4
                                            i, kb = idx // 2, idx % 2
                                            items.append((bank[:, i4 * 128:(i4 + 1) * 128], Pm[:, i, kb * 128:(kb + 1) * 128], Dg[:, i, :], True, True))
                                        mm_group(bank, items, reads=[Pm, Dg])
                                    yield
                                    op("act", lambda e: e.activation(out=PTs[:, 0:4, :], in_=v3(bk[0][:], 4), func=AF.Copy), reads=[bk[0]], writes=[PTs])
                                    pt2 = Tn(None)
                                    op("dve", lambda e: e.tensor_copy(out=PTs[:, 4:8, :], in_=v3(bk[1][:], 4)), reads=[bk[1], PTs], writes=[pt2])
                                    _merge(PTs.w, pt2.w)
                                    yield
                                    items = []
                                    for al in range(2):
                                        n = 0
                                        for i in (2 * al, 2 * al + 1):
                                            hh = 4 * quad + i
                                            for kb in range(2):
                                                items.append((bk[0][:, al * 128:(al + 1) * 128], Vpad[:, b + kb, (g * 2 + hh % 2) * 128:(g * 2 + hh % 2 + 1) * 128], PTs[:, i * 2 + kb, :], n == 0, n == 3))
                                                n += 1
                                    mm_group(bk[0], items, reads=[Vpad, PTs])
                                    yield
                                    c0 = 4 * g + 2 * quad
                                    op("dve", lambda e: e.tensor_tensor(out=obT[:, c0:c0 + 2, bsl], in0=v3(bk[0][:, 0:256], 2), in1=szb[:, 2 * quad:2 * quad + 2, bsl], op=ALU.mult),
                                       reads=[bk[0], szb], writes=[obT])

                                run_window([(lambda k, g=g, b=b, q=q: swa_stream(g, b, q, sset[k])) for b in range(4) for q in range(2)], 4)
                        ckpt("swa")
                        with sc.scope() as mg:
                            yT = mg.sb("yT", [128, KC, T], BF16)
                            sg = [mg.sb("sg%d" % i, [128, T], F32) for i in range(2)]
                            ya = [mg.sb("ya%d" % i, [128, T], F32) for i in range(2)]
                            yb = [mg.sb("yb%d" % i, [128, T], F32) for i in range(2)]
                            for jj in range(8):
                                for (si_, srcT, which) in ((52, hT, "ga"), (68, oaT, "pa"), (60, hT, "gb"), (76, obT, "pb")):
                                    slab = next_slab(l, si_ + jj)
                                    for i in range(2):
                                        dc = 2 * jj + i
                                        bank = ps[{"ga": 0, "pa": 2, "gb": 4, "pb": 6}[which] + i]
                                        mm_group(bank, [(bank[:], slab[:, kc, i * 128:(i + 1) * 128], srcT[:, kc, :], kc == 0, kc == KC - 1) for kc in range(KC)], reads=[slab, srcT])
                                        if which in ("ga", "gb"):
                                            op("act", lambda e, bank=bank, i=i: e.activation(out=sg[i][:], in_=bank[:], func=AF.Sigmoid), reads=[bank], writes=[sg[i]])
                                        elif which == "pa":
                                            op("dve", lambda e, bank=bank, i=i: e.tensor_tensor(out=ya[i][:], in0=bank[:], in1=sg[i][:], op=ALU.mult), reads=[bank, sg[i]], writes=[ya[i]])
                                        else:
                                            op("dve", lambda e, bank=bank, i=i: e.tensor_tensor(out=yb[i][:], in0=bank[:], in1=sg[i][:], op=ALU.mult), reads=[bank, sg[i]], writes=[yb[i]])
                                            op("pool", lambda e, i=i, dc=dc: e.tensor_tensor(out=yT[:, dc, :], in0=ya[i][:], in1=yb[i][:], op=ALU.add), reads=[ya[i], yb[i]], writes=[yT])
                            ckpt("merge")
                            tw = [mg.sb("tw%d" % i, [128, 512], F32) for i in range(2)]
                            for w in range(8):
                                slab = next_slab(l, 84 + w)
                                for s2 in range(2):
                                    bank = ps[(2 * w + s2) % 8]
                                    mm_group(bank, [(bank[:, ss_ * 256:(ss_ + 1) * 256], yT[:, kc, (2 * s2 + ss_) * 128:(2 * s2 + ss_ + 1) * 128], slab[:, kc, :], kc == 0, kc == KC - 1)
                                                    for ss_ in range(2) for kc in range(KC)], reads=[yT, slab])
                                    tw_ = tw[s2]
                                    op("dve", lambda e, bank=bank, tw_=tw_, w=w: e.tensor_tensor(out=v3(tw_[:], 2), in0=v3(bank[:], 2), in1=bc_mid(bc8k[:, w * 256:(w + 1) * 256], 2), op=ALU.mult),
                                       reads=[bank, bc8k], writes=[tw_])
                                    for ss_ in range(2):
                                        s = 2 * s2 + ss_
                                        op("pool", lambda e, s=s, ss_=ss_, tw_=tw_, w=w: e.tensor_tensor(out=xs_[s][:, w * 256:(w + 1) * 256], in0=xs_[s][:, w * 256:(w + 1) * 256],
                                                                                                     in1=tw_[:, ss_ * 256:(ss_ + 1) * 256], op=ALU.add), reads=[xs_[s], tw_], writes=[xs_[s]])
                    ckpt("L%d" % l)
                ckpt("wout")
                with sc.scope() as fn_:
                    junk = fn_.sb("fjunk", [128, D], BF16)
                    ss = fn_.sb("fss", [128, 1], F32)
                    ob = [fn_.sb("ob%d" % i, [128, D], F32) for i in range(2)]
                    sc.dma("sp", [(bc8k[:, 0:1024], fnw_d[:, 0:1024].partition_broadcast(128)), (bc8k[:, 1024:2048], fnw_d[:, 1024:2048].partition_broadcast(128))], writes=[bc8k])
                    for s in range(4):
                        op("act", lambda e, s=s: e.activation(out=junk[:], in_=xs_[s][:], func=AF.Square, accum_out=ss[:, 0:1]), reads=[xs_[s]], writes=[junk, ss])
                        rstd = rsqrt_small(fn_, ss, 1.0 / D, 1e-6, "fn")
                        o_ = ob[s % 2]
                        op("dve", lambda e, s=s, o_=o_, rstd=rstd: e.scalar_tensor_tensor(out=o_[:], in0=xs_[s][:], scalar=rstd[:, 0:1], in1=bc8k[:], op0=ALU.mult, op1=ALU.mult),
                           reads=[xs_[s], rstd, bc8k], writes=[o_])
                        sc.dma("act", [(out_d[t0 + s * 128:t0 + (s + 1) * 128, 0:1024], o_[:, 0:1024]), (out_d[t0 + s * 128:t0 + (s + 1) * 128, 1024:2048], o_[:, 1024:2048])], reads=[o_])
                        if t + 1 < NT:
                            t1_ = t0 + T
                            sc.dma("act", [(xs_[s][:, 0:1024], x_d[t1_ + s * 128:t1_ + (s + 1) * 128, 0:1024]),
                                           (xs_[s][:, 1024:2048], x_d[t1_ + s * 128:t1_ + (s + 1) * 128, 1024:2048])], writes=[xs_[s]])
        try:
            if STOP != "c0":
                body()
        except StopBuild:
            pass
        sc.finish(xs_ + [bc8k])
        build_nc.ninst = sc.ninst
    return nc


_CACHE = {}


def kernel(**inputs):
    x = np.asarray(inputs["x"], dtype=np.float32)
    B, S, _ = x.shape
    DEPTH = inputs["ada_w"].shape[0]
    key = (S, DEPTH)
    if key not in _CACHE:
        _CACHE[key] = build_nc(S, DEPTH)
    nc = _CACHE[key]
    consts = _consts()
    f = lambda n: np.ascontiguousarray(np.asarray(inputs[n], dtype=np.float32))
    shared = {
        "ada_w": f("ada_w"), "ada_b": f("ada_b").reshape(DEPTH, 1, 3 * D), "norm_w": f("norm_w").reshape(DEPTH, 16, 128),
        "w_in": f("w_in"), "conv_w": f("conv_w").reshape(DEPTH, 192, 128), "gdn_a_log": f("gdn_a_log").reshape(DEPTH, 1, 16),
        "gdn_dt_bias": f("gdn_dt_bias").reshape(DEPTH, 1, 16), "gdn_norm_w": f("gdn_norm_w").reshape(DEPTH, 128, 1),
        "swa_sinks": f("swa_sinks").reshape(DEPTH, 1, 32), "proj_a": f("proj_a"), "proj_b": f("proj_b"), "w_out": f("w_out"),
        "final_norm_w": f("final_norm_w").reshape(1, D),
    }
    shared.update(consts)
    pos = np.asarray(inputs["positions"]).astype(np.int32)
    cc = f("c")
    in_maps = []
    for b in range(B):
        m = dict(shared)
        m["x"] = np.ascontiguousarray(x[b])
        m["c"] = np.ascontiguousarray(cc[b].reshape(16, 128))
        m["positions"] = np.ascontiguousarray(pos[b].reshape(1, S))
        in_maps.append(m)
    import os as _os
    _off = int(_os.environ.get("KCORE", "0"))
    res = run_bass_kernel_spmd(nc, in_maps, core_ids=[_off + i for i in range(B)])
    return np.stack([np.asarray(r["out"], dtype=np.float32) for r in res.results], axis=0)
```
